# Optimizing a Trainium2 kernel written in Bass

```python
import math
import jax, jax.numpy as jnp
from jax import lax
import numpy as np

D_MODEL = 1024
BATCH = 4
SEQ = 8192
DEPTH = 1

CHUNK = 64
D_SSM = D_MODEL // 2
SSM_GROUP = 16
N_GROUPS = D_SSM // SSM_GROUP
STATE = 64
N_HEADS = 8
HEAD_DIM = 64
D_ATTN = N_HEADS * HEAD_DIM
D_IN = D_SSM + 3 * D_ATTN
D_FF = 4 * D_MODEL
Q_BLOCK = 128
EPS = 1e-6
DT_MIN = 1e-3
DT_MAX = 1e-1

kernel_name = "hybrid_s5_stickbreaking_gated_block"


def rmsnorm(x, g):
    xf = x.astype(jnp.float32)
    y = xf * lax.rsqrt(jnp.mean(xf * xf, axis=-1, keepdims=True) + EPS)
    return (y * g.astype(jnp.float32)).astype(x.dtype)


def s5_branch(u, A_re, A_im, log_dt, B_re, B_im, C_re, C_im, D_skip, w_glu, b_glu):
    f32 = jnp.float32
    bsz, seq, _ = u.shape
    uf = u.astype(f32)
    ug = uf.reshape(bsz, seq, N_GROUPS, SSM_GROUP)
    lam = lax.complex(A_re.astype(f32), A_im.astype(f32))
    dt = jnp.exp(log_dt.astype(f32))[:, None]
    a_bar = jnp.exp(lam * dt)
    b_bar = ((a_bar - 1.0) / lam)[..., None] * lax.complex(B_re.astype(f32), B_im.astype(f32))
    bu = lax.complex(jnp.einsum('bsgc,gpc->bsgp', ug, b_bar.real),
                     jnp.einsum('bsgc,gpc->bsgp', ug, b_bar.imag))
    a = jnp.broadcast_to(a_bar, (1, seq, N_GROUPS, STATE))

    def combine(left, right):
        a_l, b_l = left
        a_r, b_r = right
        return a_r * a_l, a_r * b_l + b_r

    _, states = lax.associative_scan(combine, (a, bu), axis=1)
    y = (jnp.einsum('bsgp,gcp->bsgc', states.real, C_re.astype(f32))
         - jnp.einsum('bsgp,gcp->bsgc', states.imag, C_im.astype(f32)))
    y = y.reshape(bsz, seq, D_SSM) + D_skip.astype(f32) * uf
    y = jax.nn.gelu(y)
    y = y * jax.nn.sigmoid(y @ w_glu.astype(f32) + b_glu.astype(f32))
    return y.astype(u.dtype)


def stick_breaking_attention(q, k, v):
    f32 = jnp.float32
    bsz, seq, nh, dh = q.shape
    nb = seq // Q_BLOCK
    qb = q.astype(f32).reshape(bsz, nb, Q_BLOCK, nh, dh).transpose(1, 0, 2, 3, 4)
    kf = k.astype(f32)
    vf = v.astype(f32)
    key_pos = jnp.arange(seq)
    scale = HEAD_DIM ** -0.5

    def one_block(args):
        blk, q_blk = args
        q_pos = blk * Q_BLOCK + jnp.arange(Q_BLOCK)
        z = jnp.einsum('bqhd,bkhd->bhqk', q_blk, kf) * scale
        mask = key_pos[None, :] < q_pos[:, None]
        log_fail = jnp.where(mask, jax.nn.log_sigmoid(-z), 0.0)
        after = lax.cumsum(log_fail, axis=3, reverse=True) - log_fail
        log_w = jax.nn.log_sigmoid(z) + after
        w = jnp.where(mask, jnp.exp(log_w), 0.0)
        return jnp.einsum('bhqk,bkhd->bqhd', w, vf)

    out = lax.map(one_block, (jnp.arange(nb), qb))
    return out.transpose(1, 0, 2, 3, 4).reshape(bsz, seq, nh * dh).astype(q.dtype)


def setup_inputs(seed: int = 0) -> dict:
    key = jax.random.key(seed)
    ks = jax.random.split(key, 22)
    f32 = jnp.float32

    def nrm(k, shape, scale):
        return jax.random.normal(k, shape, f32) * scale

    x = nrm(ks[0], (BATCH, SEQ, D_MODEL), 1.0)
    norm_mix = 1.0 + nrm(ks[1], (DEPTH, D_MODEL), 0.02)
    w_in = nrm(ks[2], (DEPTH, D_MODEL, D_IN), D_MODEL ** -0.5)
    A_re = -0.5 + nrm(ks[3], (DEPTH, N_GROUPS, STATE), 0.01)
    A_im = jnp.pi * jnp.arange(STATE, dtype=f32) + nrm(ks[4], (DEPTH, N_GROUPS, STATE), 0.01)
    log_dt = jax.random.uniform(ks[5], (DEPTH, N_GROUPS), f32, math.log(DT_MIN), math.log(DT_MAX))
    B_re = nrm(ks[6], (DEPTH, N_GROUPS, STATE, SSM_GROUP), (2 * SSM_GROUP) ** -0.5)
    B_im = nrm(ks[7], (DEPTH, N_GROUPS, STATE, SSM_GROUP), (2 * SSM_GROUP) ** -0.5)
    C_re = nrm(ks[8], (DEPTH, N_GROUPS, SSM_GROUP, STATE), STATE ** -0.5)
    C_im = nrm(ks[9], (DEPTH, N_GROUPS, SSM_GROUP, STATE), STATE ** -0.5)
    D_skip = nrm(ks[10], (DEPTH, D_SSM), 1.0)
    w_glu = nrm(ks[11], (DEPTH, D_SSM, D_SSM), D_SSM ** -0.5)
    b_glu = nrm(ks[12], (DEPTH, D_SSM), 0.02)
    w_up_ssm = nrm(ks[13], (DEPTH, D_SSM, D_MODEL), D_SSM ** -0.5)
    w_up_attn = nrm(ks[14], (DEPTH, D_ATTN, D_MODEL), D_ATTN ** -0.5)
    w_gate = nrm(ks[15], (DEPTH, D_MODEL, 2 * D_MODEL), D_MODEL ** -0.5)
    b_gate = nrm(ks[16], (DEPTH, 2 * D_MODEL), 0.02)
    w_out = nrm(ks[17], (DEPTH, D_MODEL, D_MODEL), D_MODEL ** -0.5)
    norm_mlp = 1.0 + nrm(ks[18], (DEPTH, D_MODEL), 0.02)
    w_ff1 = nrm(ks[19], (DEPTH, D_MODEL, D_FF), D_MODEL ** -0.5)
    w_ff2 = nrm(ks[20], (DEPTH, D_FF, D_MODEL), D_FF ** -0.5)
    norm_final = 1.0 + nrm(ks[21], (D_MODEL,), 0.02)
    return {"x": x, "norm_mix": norm_mix, "w_in": w_in, "A_re": A_re, "A_im": A_im,
            "log_dt": log_dt, "B_re": B_re, "B_im": B_im, "C_re": C_re, "C_im": C_im,
            "D_skip": D_skip, "w_glu": w_glu, "b_glu": b_glu, "w_up_ssm": w_up_ssm,
            "w_up_attn": w_up_attn, "w_gate": w_gate, "b_gate": b_gate, "w_out": w_out,
            "norm_mlp": norm_mlp, "w_ff1": w_ff1, "w_ff2": w_ff2, "norm_final": norm_final}


def reference(x, norm_mix, w_in, A_re, A_im, log_dt, B_re, B_im, C_re, C_im, D_skip, w_glu, b_glu,
              w_up_ssm, w_up_attn, w_gate, b_gate, w_out, norm_mlp, w_ff1, w_ff2, norm_final):
    h = x
    bsz, seq, _ = x.shape
    for l in range(DEPTH):
        u = rmsnorm(h, norm_mix[l])
        proj = u @ w_in[l]
        u_ssm, q, k, v = jnp.split(proj, [D_SSM, D_SSM + D_ATTN, D_SSM + 2 * D_ATTN], axis=-1)
        y_ssm = s5_branch(u_ssm, A_re[l], A_im[l], log_dt[l], B_re[l], B_im[l], C_re[l], C_im[l],
                          D_skip[l], w_glu[l], b_glu[l])
        y_attn = stick_breaking_attention(q.reshape(bsz, seq, N_HEADS, HEAD_DIM),
                                          k.reshape(bsz, seq, N_HEADS, HEAD_DIM),
                                          v.reshape(bsz, seq, N_HEADS, HEAD_DIM))
        gates = jax.nn.sigmoid((u @ w_gate[l] + b_gate[l]).astype(jnp.float32)).astype(u.dtype)
        g_ssm, g_attn = jnp.split(gates, 2, axis=-1)
        merged = g_ssm * (y_ssm @ w_up_ssm[l]) + g_attn * (y_attn @ w_up_attn[l])
        h = h + merged @ w_out[l]
        hid = jax.nn.relu(rmsnorm(h, norm_mlp[l]) @ w_ff1[l])
        h = h + (hid * hid) @ w_ff2[l]
    return rmsnorm(h, norm_final)
```

```python
import contextlib
import math
import numpy as np
import concourse.bass as bass
import concourse.mybir as mybir
from concourse.bass_utils import run_bass_kernel_spmd

F32 = mybir.dt.float32
BF16 = mybir.dt.bfloat16
I32 = mybir.dt.int32
AF = mybir.ActivationFunctionType
ALU = mybir.AluOpType

D = 1024
KT = 8
S = 8192
TA = 8448
TO = 4096
NCH = TA // 16
SL = 256
EPS = 1e-6
TWO_PI = 2.0 * math.pi
GELU_C = math.sqrt(2.0 / math.pi)


class Prog:
    ENGS = ("pe", "act", "dve", "pool", "sp")
    EPOCH = 30000
    NEP = 4
    NDMA = 72

    def __init__(self, nc, st):
        self.nc = nc
        self.ops = {e: [] for e in self.ENGS}
        self.cnt = {e: 0 for e in self.ENGS}
        self.last_w = {}
        self.readers = {}
        self.waited = {e: {} for e in self.ENGS}
        self.streams = {}
        self.strict_same = False
        self.sems = {}
        for e in self.ENGS[:4]:
            for ep in range(self.NEP):
                self.sems[("eng", e, ep)] = st.enter_context(nc.semaphore(f"s_{e}_{ep}"))
        self.dma_pool = [st.enter_context(nc.semaphore(f"s_dma_{i}")) for i in range(self.NDMA)]
        self.block = st.enter_context(nc.Block())
        self.engmap = {"pe": self.block.tensor, "act": self.block.scalar, "dve": self.block.vector,
                       "pool": self.block.gpsimd, "sp": self.block.sync}

    def _sem(self, key):
        if key not in self.sems:
            self.sems[key] = self.dma_pool.pop()
        return self.sems[key]

    def _need(self, eng, tok, waits):
        if tok is None:
            return
        key, val, peng = tok
        if peng == eng and key[0] != "dma" and not self.strict_same:
            return
        if self.waited[eng].get(key, 0) >= val:
            return
        self.waited[eng][key] = val
        waits[key] = max(waits.get(key, 0), val)

    def op(self, eng, fn, reads=(), writes=(), stream=None):
        waits = {}
        for r in reads:
            self._need(eng, self.last_w.get(r), waits)
        for w in writes:
            self._need(eng, self.last_w.get(w), waits)
            for t in self.readers.get(w, ()):
                self._need(eng, t, waits)
        if eng == "sp":
            assert stream is not None
            self.streams[stream] = self.streams.get(stream, 0) + 1
            tok = (("dma", stream), 16 * self.streams[stream], "sp")
        else:
            self.cnt[eng] += 1
            g = self.cnt[eng]
            ep = (g - 1) // self.EPOCH
            assert ep < self.NEP
            tok = (("eng", eng, ep), g - ep * self.EPOCH, eng)
        for r in reads:
            self.readers.setdefault(r, []).append(tok)
        for w in writes:
            self.last_w[w] = tok
            self.readers[w] = []
        self.ops[eng].append((fn, list(waits.items()), tok))
        return tok

    def barrier(self):
        toks = []
        for e in self.ENGS[:4]:
            g = self.cnt[e]
            if g:
                ep = (g - 1) // self.EPOCH
                toks.append((("eng", e, ep), g - ep * self.EPOCH, e))
        for s, n in self.streams.items():
            toks.append((("dma", s), 16 * n, "sp"))
        for e in self.ENGS:
            waits = {}
            for t in toks:
                self._need(e, t, waits)
            self.ops[e].append((None, list(waits.items()), None))
        self.last_w.clear()
        self.readers.clear()
        self.flush()

    def flush(self):
        for e in self.ENGS:
            ops = self.ops[e]
            if not ops:
                continue

            def body(eng, ops=ops):
                for fn, waits, tok in ops:
                    for k, v in waits:
                        eng.wait_ge(self._sem(k), v)
                    if fn is None:
                        continue
                    ins = fn(eng)
                    ins.then_inc(self._sem(tok[0]), 16 if tok[0][0] == "dma" else 1)
            self.engmap[e](body)
            self.ops[e] = []


class Ctx:
    pass


def _v3(ap, j):
    return ap.rearrange("p (c j) -> p c j", j=j)


def dma(c, out, in_, reads, writes, stream):
    return c.p.op("sp", lambda e: e.dma_start(out=out, in_=in_), reads=reads, writes=writes, stream=stream)


def act(c, out, in_, func, reads, writes, scale=None, bias=None):
    kw = {}
    if scale is not None:
        kw["scale"] = scale
    if bias is not None:
        kw["bias"] = bias
    return c.p.op("act", lambda e: e.activation(out=out, in_=in_, func=func, **kw), reads=reads, writes=writes)


def tt(c, out, a, b, op, reads, writes, eng="dve"):
    return c.p.op(eng, lambda e: e.tensor_tensor(out=out, in0=a, in1=b, op=op), reads=reads, writes=writes)


def ts(c, out, a, s1, op0, reads, writes, s2=None, op1=None, eng="dve"):
    if op1 is None:
        return c.p.op(eng, lambda e: e.tensor_scalar(out=out, in0=a, scalar1=s1, scalar2=None, op0=op0),
                      reads=reads, writes=writes)
    return c.p.op(eng, lambda e: e.tensor_scalar(out=out, in0=a, scalar1=s1, scalar2=s2, op0=op0, op1=op1),
                  reads=reads, writes=writes)


def stt(c, out, a, scalar, b, op0, op1, reads, writes):
    return c.p.op("dve", lambda e: e.scalar_tensor_tensor(out=out, in0=a, scalar=scalar, in1=b, op0=op0, op1=op1),
                  reads=reads, writes=writes)


def cp(c, eng, out, in_, reads, writes):
    if eng == "act":
        return act(c, out, in_, AF.Copy, reads, writes)
    return c.p.op(eng, lambda e: e.tensor_copy(out=out, in_=in_), reads=reads, writes=writes)


def mm(c, out, lhsT, rhs, start, stop, reads, writes):
    return c.p.op("pe", lambda e: e.matmul(out, lhsT=lhsT, rhs=rhs, start=start, stop=stop), reads=reads, writes=writes)


def memset(c, ap, val, writes, eng="pool"):
    return c.p.op(eng, lambda e: e.memset(ap, val), writes=writes)


class Alloc:
    def __init__(self, nc):
        self.nc = nc
        self.st = contextlib.ExitStack()
        self.n = 0

    _uid = [0]

    def sb(self, name, shape, dt):
        Alloc._uid[0] += 1
        return self.st.enter_context(self.nc.sbuf_tensor(f"{name}_u{Alloc._uid[0]}", list(shape), dt))

    def close(self):
        self.st.close()


def load_weight(c, al, name, w_dram, kd, col_ranges, scale_cols=None, wsb=None):
    ncols = sum(b - a for a, b in col_ranges)
    nk = kd // 128
    if wsb is None:
        wsb = al.sb(name, [128, nk, ncols], BF16)
    i = 0
    for kt in range(nk):
        o = 0
        for (a, b) in col_ranges:
            for a2 in range(a, b, 1024):
                b2 = min(b, a2 + 1024)
                w = b2 - a2
                slot = c.wst_i % len(c.wst)
                c.wst_i += 1
                stg = c.wst[slot]
                dma(c, stg[:, 0:w], w_dram[kt * 128:(kt + 1) * 128, a2:b2], [], [f"wst{slot}"], f"wst{slot}")
                dst = wsb[:, kt, o:o + w]
                if scale_cols is not None:
                    sc = c.vecs[:, scale_cols + kt:scale_cols + kt + 1]
                    if i % 2 == 0:
                        act(c, dst, stg[:, 0:w], AF.Copy, [f"wst{slot}", "vecs"], [name], scale=sc)
                    else:
                        ts(c, dst, stg[:, 0:w], sc, ALU.mult, [f"wst{slot}", "vecs"], [name])
                else:
                    cp(c, "act" if i % 2 == 0 else "dve", dst, stg[:, 0:w], [f"wst{slot}"], [name])
                i += 1
                o += w
    return wsb


def norm_slab(c, xs, xkey, W, nb, slot):
    sq = nb["sq"][slot]
    rt = nb["rt"][slot]
    rs = nb["rs"][slot]
    uT = nb["uT"][slot]
    pfx = nb["pfx"]
    act(c, sq[:, :, 0:W], xs[:, :, 0:W], AF.Square, [xkey], [f"{pfx}sq{slot}"])
    bank = c.ps_next()
    ps = c.PS[bank]
    for k in range(KT):
        mm(c, ps[:, 0:W], c.ones[:], sq[:, k, 0:W], k == 0, k == KT - 1, [f"{pfx}sq{slot}", "ones"], [f"ps{bank}"])
    act(c, rt[:, 0:W], ps[:, 0:W], AF.Sqrt, [f"ps{bank}"], [f"{pfx}rt{slot}"], scale=1.0 / D, bias=c.epsb[:, 0:1])
    c.p.op("dve", lambda e: e.reciprocal(out=rs[:, 0:W], in_=rt[:, 0:W]), reads=[f"{pfx}rt{slot}"], writes=[f"{pfx}rs{slot}"])
    tt(c, uT[:, :, 0:W], xs[:, :, 0:W], rs[:, 0:W].unsqueeze(1).to_broadcast([128, KT, W]), ALU.mult,
       [xkey, f"{pfx}rs{slot}"], [f"{pfx}uT{slot}"])
    return f"{pfx}uT{slot}"


def slab_pipeline(n, load_fn, norm_fn, proj_a, proj_b):
    load_fn(0)
    if n > 1:
        load_fn(1)
    norm_fn(0)
    for s in range(n):
        if s + 2 < n:
            load_fn(s + 2)
        proj_a(s)
        if s + 1 < n:
            norm_fn(s + 1)
        proj_b(s)


def alloc_norm_bufs(al, pfx, W, nslot=2):
    return {"pfx": pfx,
            "sq": [al.sb(f"{pfx}sq{i}", [128, KT, W], BF16) for i in range(nslot)],
            "rt": [al.sb(f"{pfx}rt{i}", [128, W], F32) for i in range(nslot)],
            "rs": [al.sb(f"{pfx}rs{i}", [128, W], F32) for i in range(nslot)],
            "uT": [al.sb(f"{pfx}uT{i}", [128, KT, W], BF16) for i in range(nslot)]}


def gen_abeta(c, al, par, pk, pfx, F=512):
    T = lambda n: al.sb(f"{pfx}_{n}", [128, F], F32)
    dt, lr, li, er, t1, t2, sn, cs, ar, ai, br, bi = (T(n) for n in
                                                        ("dt", "lr", "li", "er", "t1", "t2", "sn", "cs", "ar", "ai", "br", "bi"))
    ti = al.sb(f"{pfx}_ti", [128, F], I32)
    K = lambda n: f"{pfx}_{n}"
    Are, Aim, Ldt = par[:, 0, :], par[:, 1, :], par[:, 2, :]
    act(c, dt[:], Ldt, AF.Exp, [pk], [K("dt")])
    tt(c, lr[:], Are, dt[:], ALU.mult, [pk, K("dt")], [K("lr")])
    tt(c, li[:], Aim, dt[:], ALU.mult, [pk, K("dt")], [K("li")])
    act(c, er[:], lr[:], AF.Exp, [K("lr")], [K("er")])

    def sinshift(out, okey, shift):
        ts(c, t1[:], li[:], 1.0 / TWO_PI, ALU.mult, [K("li")], [K("t1")], s2=shift / TWO_PI, op1=ALU.add)
        cp(c, "dve", ti[:], t1[:], [K("t1")], [K("ti")])
        cp(c, "dve", t2[:], ti[:], [K("ti")], [K("t2")])
        tt(c, t1[:], t1[:], t2[:], ALU.subtract, [K("t1"), K("t2")], [K("t1")])
        ts(c, t1[:], t1[:], TWO_PI, ALU.mult, [K("t1")], [K("t1")], s2=math.pi, op1=ALU.min)
        ts(c, t1[:], t1[:], -math.pi, ALU.max, [K("t1")], [K("t1")])
        act(c, out[:], t1[:], AF.Sin, [K("t1")], [okey])

    sinshift(sn, K("sn"), 0.0)
    sinshift(cs, K("cs"), math.pi / 2)
    tt(c, ar[:], er[:], cs[:], ALU.mult, [K("er"), K("cs")], [K("ar")])
    tt(c, ai[:], er[:], sn[:], ALU.mult, [K("er"), K("sn")], [K("ai")])
    den, am1 = dt, lr
    tt(c, t1[:], Are, Are, ALU.mult, [pk], [K("t1")])
    tt(c, t2[:], Aim, Aim, ALU.mult, [pk], [K("t2")])
    tt(c, den[:], t1[:], t2[:], ALU.add, [K("t1"), K("t2")], [K("dt")])
    c.p.op("dve", lambda e: e.reciprocal(out=den[:], in_=den[:]), reads=[K("dt")], writes=[K("dt")])
    ts(c, am1[:], ar[:], -1.0, ALU.add, [K("ar")], [K("lr")])
    tt(c, t1[:], am1[:], Are, ALU.mult, [K("lr"), pk], [K("t1")])
    tt(c, t2[:], ai[:], Aim, ALU.mult, [K("ai"), pk], [K("t2")])
    tt(c, t1[:], t1[:], t2[:], ALU.add, [K("t1"), K("t2")], [K("t1")])
    tt(c, br[:], t1[:], den[:], ALU.mult, [K("t1"), K("dt")], [K("br")])
    tt(c, t1[:], ai[:], Are, ALU.mult, [K("ai"), pk], [K("t1")])
    tt(c, t2[:], am1[:], Aim, ALU.mult, [K("lr"), pk], [K("t2")])
    tt(c, t1[:], t1[:], t2[:], ALU.subtract, [K("t1"), K("t2")], [K("t1")])
    tt(c, bi[:], t1[:], den[:], ALU.mult, [K("t1"), K("dt")], [K("bi")])
    return dict(ar=ar, ai=ai, br=br, bi=bi, kar=K("ar"), kai=K("ai"), kbr=K("br"), kbi=K("bi"),
                tmp=[(t1, K("t1")), (t2, K("t2")), (sn, K("sn")), (cs, K("cs")), (er, K("er")), (li, K("li"))])


def cmul(c, outr, outi, kor, koi, xr, xi, kxr, kxi, yr, yi, kyr, kyi, tmps):
    (t1, k1), (t2, k2), (t3, k3), (t4, k4) = tmps[:4]
    tt(c, t1[:], xr, yr, ALU.mult, [kxr, kyr], [k1])
    tt(c, t2[:], xi, yi, ALU.mult, [kxi, kyi], [k2])
    tt(c, t3[:], xr, yi, ALU.mult, [kxr, kyi], [k3])
    tt(c, t4[:], xi, yr, ALU.mult, [kxi, kyr], [k4])
    tt(c, outr, t1[:], t2[:], ALU.subtract, [k1, k2], [kor])
    tt(c, outi, t3[:], t4[:], ALU.add, [k3, k4], [koi])


def build(debug=False):
    nc = bass.Bass("TRN2", target_bir_lowering=False)
    din = lambda n, s, d=F32: nc.dram_tensor(n, list(s), d, kind="ExternalInput").ap()
    scr_kind = "ExternalOutput"
    dscr = lambda n, s, d: nc.dram_tensor(n, list(s), d, kind=scr_kind).ap()

    xTr = din("xTr", [D, TA])
    xTn = din("xTn", [D, TA])
    xTo = din("xTo", [D, TO])
    w_in = din("w_in", [D, 2048])
    w_gate = din("w_gate", [D, 2048])
    w_glu = din("w_glu", [512, 512])
    w_up_ssm = din("w_up_ssm", [512, D])
    w_up_attn = din("w_up_attn", [512, D])
    w_out = din("w_out", [D, D])
    w_ff1 = din("w_ff1", [D, 4096])
    w_ff2 = din("w_ff2", [4096, D])
    vecs_d = din("vecs", [128, 48])
    sp_par_d = din("sp_par", [128, 3, 512])
    sp_C_d = din("sp_C", [128, 2, 512])
    sp_B_d = din("sp_B", [128, 2, 512])
    ch_par_d = din("ch_par", [128, 3, 1024])
    ch_B_d = din("ch_B", [128, 2, 1024])
    mask_d = din("mask", [128, 128])
    yT = nc.dram_tensor("yT", [D, TO], F32, kind="ExternalOutput").ap()

    kT_scr = dscr("kT_scr", [512, TA], BF16)
    v_scr = dscr("v_scr", [4, 128, 66, 128], BF16)
    qT_scr = dscr("qT_scr", [512, TO], BF16)
    ysT_scr = dscr("ysT_scr", [512, TO], BF16)
    yaT_scr = dscr("yaT_scr", [512, TO], BF16)
    hT_scr = dscr("hT_scr", [D, TO], F32)

    c = Ctx()
    c.nc = nc
    with contextlib.ExitStack() as st:
        c.p = Prog(nc, st)
        p = c.p
        c.PS = [st.enter_context(nc.psum_tensor(f"psf{i}", [128, 512], F32)) for i in range(6)]
        c.PB = [st.enter_context(nc.psum_tensor(f"psb{i}", [128, 1024], BF16)) for i in range(2)]
        c.ps_i = 0

        def ps_next():
            c.ps_i = (c.ps_i + 1) % 6
            return c.ps_i
        c.ps_next = ps_next

        G = Alloc(nc)
        c.vecs = G.sb("vecs", [128, 48], F32)
        c.ones = G.sb("ones", [128, 128], BF16)
        c.ident = G.sb("ident", [128, 128], BF16)
        c.epsb = G.sb("epsb", [128, 1], F32)
        c.mask = G.sb("maskt", [128, 128], F32)
        c.wst_i = 0

        def set_wst(al, n):
            c.wst = [al.sb(f"wst{i}", [128, 1024], F32) for i in range(n)]
        dma(c, c.vecs[:], vecs_d[:, :], [], ["vecs"], "c_vecs")
        dma(c, c.mask[:], mask_d[:, :], [], ["mask"], "c_mask")
        memset(c, c.ones[:], 1.0, ["ones"])
        memset(c, c.epsb[:], EPS, ["epsb"])
        memset(c, c.ident[:], 1.0, ["ident"])
        p.op("pool", lambda e: e.affine_select(out=c.ident[:], in_=c.ident[:], pattern=[[-1, 128]],
                                               compare_op=ALU.is_equal, fill=0.0, base=0, channel_multiplier=1),
             reads=["ident"], writes=["ident"])
        G_MIX, G_MLP, G_FIN, B_GATE, D_SKIP, B_GLU = 0, 8, 16, 24, 40, 44

        A5 = Alloc(nc)
        Xown = [A5.sb(f"Xown{ri}", [128, 16, 256], BF16) for ri in range(2)]
        ksr = A5.sb("ksr", [128, 10, 16], F32)
        ksi = A5.sb("ksi", [128, 10, 16], F32)
        ksni = A5.sb("ksni", [128, 10, 16], F32)
        pwr = A5.sb("pwr", [128, 8, 16], F32)
        pwi = A5.sb("pwi", [128, 8, 16], F32)
        pwn = A5.sb("pwn", [128, 8, 16], F32)
        memset(c, pwr[:].rearrange("p k n -> p (k n)"), 0.0, ["pwr"])
        memset(c, pwi[:].rearrange("p k n -> p (k n)"), 0.0, ["pwi"])

        def sp_gen(want_ks, Cm=None, Kt=None, CmZ=None):
            A0 = Alloc(nc)
            sp_par = A0.sb("sp_par", [128, 3, 512], F32)
            dma(c, sp_par[:], sp_par_d[:, :, :], [], ["sp_par"], "c_sp_par")
            g = gen_abeta(c, A0, sp_par, "sp_par", "sp")
            tmps = g["tmp"]
            if Cm is not None:
                sp_C = A0.sb("sp_C", [128, 2, 512], F32)
                sp_B = A0.sb("sp_B", [128, 2, 512], F32)
                dma(c, sp_C[:], sp_C_d[:, :, :], [], ["sp_C"], "c_sp_C")
                dma(c, sp_B[:], sp_B_d[:, :, :], [], ["sp_B"], "c_sp_B")
                BTb = [A0.sb(f"BTb{ri}", [128, 512], BF16) for ri in range(2)]
                cmul(c, BTb[0][:], BTb[1][:], "BTb0", "BTb1", g["br"][:], g["bi"][:], g["kbr"], g["kbi"],
                     sp_B[:, 0, :], sp_B[:, 1, :], "sp_B", "sp_B", tmps)
                curr = A0.sb("curr", [128, 512], F32)
                curi = A0.sb("curi", [128, 512], F32)
                cp(c, "act", curr[:], sp_C[:, 0, :], ["sp_C"], ["curr"])
                cp(c, "act", curi[:], sp_C[:, 1, :], ["sp_C"], ["curi"])
                for tau in range(17):
                    cp(c, "act", Cm[0][:, tau, :], curr[:], ["curr"], ["Cm0"])
                    act(c, Cm[1][:, tau, :], curi[:], AF.Copy, ["curi"], ["Cm1"], scale=-1.0)
                    if tau < 16:
                        cmul(c, curr[:], curi[:], "curr", "curi", curr[:], curi[:], "curr", "curi",
                             g["ar"][:], g["ai"][:], g["kar"], g["kai"], tmps)
                BTbZ = [A0.sb(f"BTbZ{ri}", [128, 4, 64], BF16) for ri in range(2)]
                for ri in range(2):
                    act(c, BTbZ[ri][:].rearrange("p a b -> p (a b)"), c.ones[:, 0:1].to_broadcast([128, 256]), AF.Copy,
                        ["ones"], [f"BTbZ{ri}"], scale=0.0)
                    act(c, CmZ[ri][:].rearrange("p a b c -> p (a b c)"), c.ones[:, 0:1].to_broadcast([128, 17 * 4 * 64]), AF.Copy,
                        ["ones"], [f"CmZ{ri}"], scale=0.0)
                    for t in range(4):
                        cols = slice((4 * t + 3) * 32, (4 * t + 4) * 32)
                        cp(c, "act", BTbZ[ri][:, t, 32:64], BTb[ri][:, cols], [f"BTb{ri}", f"BTbZ{ri}"], [f"BTbZ{ri}"])
                        cp(c, "act", CmZ[ri][:, :, t, 32:64], Cm[ri][:, :, cols], [f"Cm{ri}", f"CmZ{ri}"], [f"CmZ{ri}"])
                act(c, Kt[:].rearrange("p a b c -> p (a b c)"), c.ones[:, 0:1].to_broadcast([128, 4 * 16 * 128]), AF.Copy,
                    ["ones"], ["Kt"], scale=0.0)
                for t in range(4):
                    bank = ps_next()
                    ps = c.PS[bank]
                    for q in (3, 2, 1, 0):
                        pr = 4 * t + q
                        cols = slice(pr * 32, (pr + 1) * 32)
                        if q == 3:
                            o3 = _v3(ps[64:128, :], 32)
                            l0, l1 = BTbZ[0][:, t, :], BTbZ[1][:, t, :]
                        else:
                            o3 = _v3(ps[32 * q:32 * q + 32, :], 32)
                            l0, l1 = BTb[0][:, cols], BTb[1][:, cols]
                        mm(c, o3, l0, Cm[0][:, 0:16, cols], q != 2, False, ["BTb0", "BTbZ0", "Cm0"], [f"ps{bank}"])
                        mm(c, o3, l1, Cm[1][:, 0:16, cols], False, q == 0, ["BTb1", "BTbZ1", "Cm1"], [f"ps{bank}"])
                    for q in range(4):
                        act(c, Kt[32 * q:32 * q + 32, t, :, 32 * q:32 * q + 32], _v3(ps[32 * q:32 * q + 32, :], 32), AF.Copy,
                            [f"ps{bank}"], ["Kt"])
            if want_ks:
                acr = A0.sb("acr", [128, 16], F32)
                aci = A0.sb("aci", [128, 16], F32)
                q1 = A0.sb("q1", [128, 16], F32)
                q2 = A0.sb("q2", [128, 16], F32)
                q3 = A0.sb("q3", [128, 16], F32)
                cp(c, "dve", acr[:], _v3(g["ar"][:], 32)[:, :, 0], [g["kar"]], ["acr"])
                cp(c, "dve", aci[:], _v3(g["ai"][:], 32)[:, :, 0], [g["kai"]], ["aci"])

                pad_t = A0.sb("pad_t", [128, 8], F32)

                def pad(n=2):
                    for _ in range(n):
                        memset(c, pad_t[:, 0:1], 0.0, ["pad_t"], eng="dve")

                def csq(outr, outi, kor, koi, inr, ini, kir, kii):
                    tt(c, q1[:], inr, inr, ALU.mult, [kir], ["q1"])
                    tt(c, q2[:], ini, ini, ALU.mult, [kii], ["q2"])
                    tt(c, q3[:], inr, ini, ALU.mult, [kir, kii], ["q3"])
                    pad(2)
                    tt(c, outr, q1[:], q2[:], ALU.subtract, ["q1", "q2"], [kor])
                    ts(c, outi, q3[:], 2.0, ALU.mult, ["q3"], [koi])
                    pad(2)

                for _ in range(4):
                    csq(acr[:], aci[:], "acr", "aci", acr[:], aci[:], "acr", "aci")
                cp(c, "dve", ksr[:, 0, :], acr[:], ["acr"], ["ksr"])
                cp(c, "dve", ksi[:, 0, :], aci[:], ["aci"], ["ksi"])
                for k in range(1, 10):
                    csq(ksr[:, k, :], ksi[:, k, :], "ksr", "ksi", ksr[:, k - 1, :], ksi[:, k - 1, :], "ksr", "ksi")
                ts(c, ksni[:].rearrange("p k n -> p (k n)"), ksi[:].rearrange("p k n -> p (k n)"), -1.0, ALU.mult, ["ksi"], ["ksni"])
                T_ = A0.sb("pwT", [128, 16, 16], F32)
                pad(3)

                def prod_mults(ti, xr, xi, yr, yi, rd):
                    tt(c, T_[:, ti + 0, :], xr, yr, ALU.mult, rd, ["pwT"])
                    tt(c, T_[:, ti + 1, :], xi, yi, ALU.mult, rd, ["pwT"])
                    tt(c, T_[:, ti + 2, :], xr, yi, ALU.mult, rd, ["pwT"])
                    tt(c, T_[:, ti + 3, :], xi, yr, ALU.mult, rd, ["pwT"])

                def prod_fin(ti, m):
                    tt(c, pwr[:, m, :], T_[:, ti + 0, :], T_[:, ti + 1, :], ALU.subtract, ["pwT"], ["pwr"])
                    tt(c, pwi[:, m, :], T_[:, ti + 2, :], T_[:, ti + 3, :], ALU.add, ["pwT"], ["pwi"])

                K_ = lambda k: (ksr[:, k, :], ksi[:, k, :])
                prod_mults(0, *K_(0), *K_(1), ["ksr", "ksi"])
                prod_mults(4, *K_(0), *K_(2), ["ksr", "ksi"])
                prod_mults(8, *K_(1), *K_(2), ["ksr", "ksi"])
                prod_fin(0, 3)
                prod_fin(4, 5)
                prod_fin(8, 6)
                for m, k in ((1, 0), (2, 1), (4, 2)):
                    cp(c, "dve", pwr[:, m, :], ksr[:, k, :], ["ksr"], ["pwr"])
                    cp(c, "dve", pwi[:, m, :], ksi[:, k, :], ["ksi"], ["pwi"])
                prod_mults(12, pwr[:, 3, :], pwi[:, 3, :], *K_(2), ["pwr", "pwi", "ksr", "ksi"])
                pad(3)
                prod_fin(12, 7)
                pad(3)
                ts(c, pwn[:].rearrange("p k n -> p (k n)"), pwi[:].rearrange("p k n -> p (k n)"), -1.0, ALU.mult, ["pwi"], ["pwn"])
                pad(3)
            p.barrier()
            A0.close()

        sp_gen(True)
        if debug:
            dbg = nc.dram_tensor("dbg", [128, 4, 8, 16], F32, kind="ExternalOutput").ap()
            dma(c, dbg[:, 0, :, :], pwr[:], ["pwr"], ["dbg"], "dbg0")
            dma(c, dbg[:, 1, :, :], pwi[:], ["pwi"], ["dbg"], "dbg1")
            dma(c, dbg[:, 2, :, :], ksr[:, 0:8, :], ["ksr"], ["dbg"], "dbg2")
            dma(c, dbg[:, 3, :, :], ksi[:, 0:8, :], ["ksi"], ["dbg"], "dbg3")
        A1 = Alloc(nc)
        Uall = A1.sb("Uall", [128, 4, 16, NCH], BF16)
        A1b = Alloc(nc)
        set_wst(A1b, 2)
        Wssm = load_weight(c, A1b, "Wssm", w_in, D, [(0, 512)], scale_cols=G_MIX)
        SLs = 256
        nb = alloc_norm_bufs(A1b, "a", SLs)
        xs = [A1b.sb(f"xs{i}", [128, KT, SLs], F32) for i in range(2)]
        xTn_v = xTn.rearrange("(k p) t -> p k t", p=128)
        nsl = TA // SLs
        uks = {}

        def ld_s(s):
            sl_ = s % 2
            dma(c, xs[sl_][:], xTn_v[:, :, s * SLs:(s + 1) * SLs], [], [f"xs{sl_}"], f"xs{sl_}")

        def nm_s(s):
            uks[s] = norm_slab(c, xs[s % 2], f"xs{s % 2}", SLs, nb, s % 2)

        def pj_s(s, cts=range(4)):
            uT = nb["uT"][s % 2]
            for ct in cts:
                bank = ps_next()
                ps = c.PS[bank]
                for k in range(KT):
                    mm(c, ps[:, 0:SLs], Wssm[:, k, ct * 128:(ct + 1) * 128], uT[:, k, :], k == 0, k == KT - 1,
                       ["Wssm", uks[s]], [f"ps{bank}"])
                n0, nn = (s * SLs) // 16, SLs // 16
                cp(c, "act" if ct % 2 == 0 else "dve", Uall[:, ct, :, n0:n0 + nn],
                   ps[:, 0:SLs].rearrange("p (n r) -> p r n", r=16), [f"ps{bank}"], ["Uall"])
        slab_pipeline(nsl, ld_s, nm_s, lambda s: pj_s(s, range(0, 2)), lambda s: pj_s(s, range(2, 4)))
        p.barrier()
        A1b.close()

        ABm = Alloc(nc)
        Bm = [ABm.sb(f"Bm{ri}", [128, 16, 1024], BF16) for ri in range(2)]
        for hf in range(2):
            A0 = Alloc(nc)
            cs_ = slice(hf * 512, (hf + 1) * 512)
            ch_par = A0.sb("ch_par", [128, 3, 512], F32)
            ch_B = A0.sb("ch_B", [128, 2, 512], F32)
            dma(c, ch_par[:], ch_par_d[:, :, cs_], [], ["ch_par"], "c_ch_par")
            dma(c, ch_B[:], ch_B_d[:, :, cs_], [], ["ch_B"], "c_ch_B")
            g = gen_abeta(c, A0, ch_par, "ch_par", "ch")
            tmps = g["tmp"]
            curr = A0.sb("curr2", [128, 512], F32)
            curi = A0.sb("curi2", [128, 512], F32)
            cmul(c, curr[:], curi[:], "curr2", "curi2", g["br"][:], g["bi"][:], g["kbr"], g["kbi"],
                 ch_B[:, 0, :], ch_B[:, 1, :], "ch_B", "ch_B", tmps)
            for rho in range(15, -1, -1):
                cp(c, "act", Bm[0][:, rho, cs_], curr[:], ["curr2"], ["Bm0"])
                cp(c, "act", Bm[1][:, rho, cs_], curi[:], ["curi2"], ["Bm1"])
                if rho > 0:
                    cmul(c, curr[:], curi[:], "curr2", "curi2", curr[:], curi[:], "curr2", "curi2",
                         g["ar"][:], g["ai"][:], g["kar"], g["kai"], tmps)
            p.barrier()
            A0.close()

        A1c = Alloc(nc)
        Sst2 = [[A1c.sb(f"S{ri}_{b_}", [128, 4, NCH], F32) for ri in range(2)] for b_ in range(2)]
        kb = [[A1c.sb(f"kb{j}_{ri}", [128, NCH], F32) for ri in range(2)] for j in range(2)]
        HN = NCH // 2
        i = 0
        for t in range(4):
            Sst = Sst2[t % 2]
            sp_ = f"b{t % 2}"
            for q in range(4):
                pr = 4 * t + q
                if q < 3:
                    rows = slice(32 * q, 32 * q + 32)
                    bcol = (t * 2 + 0) * 128
                else:
                    rows = slice(64, 128)
                    bcol = (t * 2 + 1) * 128
                for ri in range(2):
                    for half in range(2):
                        bank = ps_next()
                        ps = c.PS[bank]
                        for rho in range(16):
                            mm(c, ps[:, 0:HN], Bm[ri][rows, rho, bcol:bcol + 128], Uall[rows, t, rho, half * HN:(half + 1) * HN],
                               rho == 0, rho == 15, [f"Bm{ri}", "Uall"], [f"ps{bank}"])
                        cp(c, "act", Sst[ri][:, q, half * HN:(half + 1) * HN], ps[:, 0:HN],
                           [f"ps{bank}"], [f"S{sp_}{ri}_{q}"])
                        i += 1
            for q in range(4):
                pr = 4 * t + q
                jb = pr % 2
                Sk = [f"S{sp_}0_{q}", f"S{sp_}1_{q}"]
                L3 = [Sst[ri][:, q, :].rearrange("p (b w) -> p b w", w=8) for ri in range(2)]
                a16r, a16i, a16n = ksr[:, 0, pr:pr + 1], ksi[:, 0, pr:pr + 1], ksni[:, 0, pr:pr + 1]
                for cc in range(1, 8):
                    stt(c, L3[0][:, :, cc], L3[0][:, :, cc - 1], a16r, L3[0][:, :, cc], ALU.mult, ALU.add, [Sk[0], "ksr"], [Sk[0]])
                    stt(c, L3[1][:, :, cc], L3[1][:, :, cc - 1], a16r, L3[1][:, :, cc], ALU.mult, ALU.add, [Sk[1], "ksr"], [Sk[1]])
                    stt(c, L3[0][:, :, cc], L3[1][:, :, cc - 1], a16n, L3[0][:, :, cc], ALU.mult, ALU.add, [Sk[0], Sk[1], "ksni"], [Sk[0]])
                    stt(c, L3[1][:, :, cc], L3[0][:, :, cc - 1], a16i, L3[1][:, :, cc], ALU.mult, ALU.add, [Sk[0], Sk[1], "ksi"], [Sk[1]])
                NBK = NCH // 8
                ea = [kb[jb][0][:, 0:NBK], kb[jb][1][:, 0:NBK]]
                eb = [kb[jb][0][:, 128:128 + NBK], kb[jb][1][:, 128:128 + NBK]]
                ka = [f"kb{jb}_0a", f"kb{jb}_1a"]
                kbk = [f"kb{jb}_0b", f"kb{jb}_1b"]
                for ri in range(2):
                    cp(c, "dve", ea[ri], L3[ri][:, :, 7], [Sk[ri]], [ka[ri]])
                cur, oth, ck, ok = ea, eb, ka, kbk
                for k in range(7):
                    sft = 1 << k
                    n = NBK - sft
                    Ar = ksr[:, k + 3, pr:pr + 1]
                    Ai = ksi[:, k + 3, pr:pr + 1]
                    nAi = ksni[:, k + 3, pr:pr + 1]
                    stt(c, oth[0][:, sft:], cur[0][:, 0:n], Ar, cur[0][:, sft:], ALU.mult, ALU.add, [ck[0], "ksr"], [ok[0]])
                    stt(c, oth[1][:, sft:], cur[1][:, 0:n], Ar, cur[1][:, sft:], ALU.mult, ALU.add, [ck[1], "ksr"], [ok[1]])
                    stt(c, oth[0][:, sft:], cur[1][:, 0:n], nAi, oth[0][:, sft:], ALU.mult, ALU.add, [ck[1], "ksni", ok[0]], [ok[0]])
                    stt(c, oth[1][:, sft:], cur[0][:, 0:n], Ai, oth[1][:, sft:], ALU.mult, ALU.add, [ck[0], "ksi", ok[1]], [ok[1]])
                    cp(c, "act", oth[0][:, 0:sft], cur[0][:, 0:sft], [ck[0]], [ok[0]])
                    cp(c, "act", oth[1][:, 0:sft], cur[1][:, 0:sft], [ck[1]], [ok[1]])
                    cur, oth = oth, cur
                    ck, ok = ok, ck
                Xp = [cur[ri].rearrange("p (i w) -> p i w", w=2)[:, 0:32, 0] for ri in range(2)]
                Lo = [Sst[ri][:, q, :].rearrange("p (i w c) -> p i w c", w=2, c=8)[:, 0:32, 1, :] for ri in range(2)]
                Xo = [Xown[ri][:, pr, :].rearrange("p (i c) -> p i c", c=8) for ri in range(2)]
                tmp = [kb[jb][0][:, 256:288], kb[jb][1][:, 256:288]]
                tk_ = [f"kb{jb}_0t", f"kb{jb}_1t"]
                for ri in range(2):
                    cp(c, "act", Xo[ri][:, :, 0], Xp[ri], [ck[ri]], [f"Xown{ri}"])
                for cc in range(1, 8):
                    pr_, pi_, pn_ = pwr[:, cc, pr:pr + 1], pwi[:, cc, pr:pr + 1], pwn[:, cc, pr:pr + 1]
                    tb_ = (cc % 2) * 32
                    t0_, t1_ = kb[jb][0][:, 256 + tb_:288 + tb_], kb[jb][1][:, 256 + tb_:288 + tb_]
                    stt(c, t0_, Xp[0], pr_, Lo[0][:, :, cc - 1], ALU.mult, ALU.add, [ck[0], Sk[0], "pwr"], [tk_[0]])
                    stt(c, t1_, Xp[1], pr_, Lo[1][:, :, cc - 1], ALU.mult, ALU.add, [ck[1], Sk[1], "pwr"], [tk_[1]])
                    stt(c, Xo[0][:, :, cc], Xp[1], pn_, t0_, ALU.mult, ALU.add, [ck[1], tk_[0], "pwn"], ["Xown0"])
                    stt(c, Xo[1][:, :, cc], Xp[0], pi_, t1_, ALU.mult, ALU.add, [ck[0], tk_[1], "pwi"], ["Xown1"])
        p.barrier()
        A1c.close()
        ABm.close()
        A1.close()

        A5b = Alloc(nc)
        Cm = [A5b.sb(f"Cm{ri}", [128, 17, 512], BF16) for ri in range(2)]
        Kt = A5b.sb("Kt", [128, 4, 16, 128], BF16)
        CmZ = [A5b.sb(f"CmZ{ri}", [128, 17, 4, 64], BF16) for ri in range(2)]
        sp_gen(False, Cm, Kt, CmZ)

        A2 = Alloc(nc)
        Uown = A2.sb("Uown", [128, 4, TO], BF16)
        A2b = Alloc(nc)
        set_wst(A2b, 2)
        Wsq = load_weight(c, A2b, "Wsq", w_in, D, [(0, 1024)], scale_cols=G_MIX)
        nb = alloc_norm_bufs(A2b, "b", SL)
        xs = [A2b.sb(f"xo{i}", [128, KT, SL], F32) for i in range(2)]
        qst = [A2b.sb(f"qst{i}", [128, 4, SL], BF16) for i in range(2)]
        xTo_v = xTo.rearrange("(k p) t -> p k t", p=128)
        qT_v = qT_scr.rearrange("(k p) t -> p k t", p=128)
        nsl = TO // SL
        uko = {}

        def ld_o(s):
            sl_ = s % 2
            dma(c, xs[sl_][:], xTo_v[:, :, s * SL:(s + 1) * SL], [], [f"xo{sl_}"], f"xs{sl_}")

        def nm_o(s):
            uko[s] = norm_slab(c, xs[s % 2], f"xo{s % 2}", SL, nb, s % 2)

        def pj_o(s, cts=range(8), store=True):
            slot = s % 2
            uT = nb["uT"][slot]
            for ct in cts:
                bank = ps_next()
                ps = c.PS[bank]
                for k in range(KT):
                    mm(c, ps[:, 0:SL], Wsq[:, k, ct * 128:(ct + 1) * 128], uT[:, k, :], k == 0, k == KT - 1,
                       ["Wsq", uko[s]], [f"ps{bank}"])
                eng = "act" if ct % 2 == 0 else "dve"
                if ct < 4:
                    cp(c, eng, Uown[:, ct, s * SL:(s + 1) * SL], ps[:, 0:SL], [f"ps{bank}"], ["Uown"])
                else:
                    cp(c, eng, qst[slot][:, ct - 4, :], ps[:, 0:SL], [f"ps{bank}"], [f"qst{slot}"])
            if store:
                dma(c, qT_v[:, :, s * SL:(s + 1) * SL], qst[slot][:], [f"qst{slot}"], ["qT_scr"], f"qo{slot}")
        slab_pipeline(nsl, ld_o, nm_o, lambda s: pj_o(s, range(0, 4), False), lambda s: pj_o(s, range(4, 8), True))
        p.barrier()
        A2b.close()

        A2c = Alloc(nc)
        set_wst(A2c, 2)
        Wglu = load_weight(c, A2c, "Wglu", w_glu, 512, [(0, 512)])
        SP_ = 512
        NCS = SP_ // 16
        yf = [A2c.sb(f"yf{i}", [128, SP_], F32) for i in range(2)]
        wk = [A2c.sb(f"wk{i}", [128, SP_], F32) for i in range(2)]
        sg = [A2c.sb(f"sg{i}", [128, SP_], F32) for i in range(2)]
        gf = [A2c.sb(f"gf{i}", [128, 4, SP_], F32) for i in range(2)]
        gb = [A2c.sb(f"gb{i}", [128, 4, SP_], BF16) for i in range(2)]
        sz = [A2c.sb(f"sz{i}", [128, SP_], F32) for i in range(2)]
        yst = [A2c.sb(f"yst{i}", [128, 4, SP_], BF16) for i in range(2)]
        ysT_v = ysT_scr.rearrange("(k p) t -> p k t", p=128)
        it = 0
        for s in range(TO // SP_):
            slot = s % 2
            cs_ = slice(s * SP_, (s + 1) * SP_)
            for t in range(4):
                b2 = it % 2
                it += 1
                bank = ps_next()
                ps = c.PS[bank]
                Y3 = _v3(ps[:, 0:SP_], 16)
                U3 = _v3(Uown[:, t, cs_], 16)
                for tau in range(16):
                    mm(c, Y3[:, :, tau:16], Kt[:, t, tau, :], U3[:, :, 0:16 - tau], tau == 0, False, ["Kt", "Uown"], [f"ps{bank}"])
                for q in range(4):
                    pr = 4 * t + q
                    cols = slice(pr * 32, (pr + 1) * 32)
                    if q < 3:
                        Yq = _v3(ps[32 * q:32 * q + 32, 0:SP_], 16)
                    else:
                        Yq = _v3(ps[64:128, 0:SP_], 16)
                    for j in range(16):
                        for ri in range(2):
                            last = (q == 3 and j == 15 and ri == 1)
                            lt = Cm[ri][:, j + 1, cols] if q < 3 else CmZ[ri][:, j + 1, t, :]
                            mm(c, Yq[:, :, j], lt, Xown[ri][:, pr, NCS * s:NCS * (s + 1)], False, last,
                               [f"Cm{ri}", f"CmZ{ri}", f"Xown{ri}"], [f"ps{bank}"])
                stt(c, yf[b2][:], Uown[:, t, cs_], c.vecs[:, D_SKIP + t:D_SKIP + t + 1], ps[:, 0:SP_],
                    ALU.mult, ALU.add, ["Uown", "vecs", f"ps{bank}"], [f"yf{b2}"])
                act(c, wk[b2][:], yf[b2][:], AF.Square, [f"yf{b2}"], [f"wk{b2}"])
                ts(c, wk[b2][:], wk[b2][:], 0.044715, ALU.mult, [f"wk{b2}"], [f"wk{b2}"], s2=1.0, op1=ALU.add)
                tt(c, wk[b2][:], wk[b2][:], yf[b2][:], ALU.mult, [f"wk{b2}", f"yf{b2}"], [f"wk{b2}"])
                act(c, sg[b2][:], wk[b2][:], AF.Sigmoid, [f"wk{b2}"], [f"sg{b2}"], scale=2.0 * GELU_C)
                tt(c, gf[slot][:, t, :], yf[b2][:], sg[b2][:], ALU.mult, [f"yf{b2}", f"sg{b2}"], [f"gf{slot}"])
                cp(c, "act", gb[slot][:, t, :], gf[slot][:, t, :], [f"gf{slot}"], [f"gb{slot}"])
            for t2 in range(4):
                b2 = it % 2
                it += 1
                bank = ps_next()
                ps = c.PS[bank]
                for t in range(4):
                    mm(c, ps[:, 0:SP_], Wglu[:, t, t2 * 128:(t2 + 1) * 128], gb[slot][:, t, :], t == 0, t == 3,
                       ["Wglu", f"gb{slot}"], [f"ps{bank}"])
                act(c, sz[b2][:], ps[:, 0:SP_], AF.Sigmoid, [f"ps{bank}", "vecs"], [f"sz{b2}"],
                    bias=c.vecs[:, B_GLU + t2:B_GLU + t2 + 1])
                tt(c, yst[slot][:, t2, :], gf[slot][:, t2, :], sz[b2][:], ALU.mult, [f"gf{slot}", f"sz{b2}"], [f"yst{slot}"])
            dma(c, ysT_v[:, :, cs_], yst[slot][:], [f"yst{slot}"], ["ysT_scr"], f"yo{slot}")
        p.barrier()
        A2c.close()
        A2.close()
        A5b.close()
        A5.close()

        A3 = Alloc(nc)
        set_wst(A3, 2)
        Wkv = load_weight(c, A3, "Wkv", w_in, D, [(1024, 2048)], scale_cols=G_MIX)
        def slabs(total, W):
            return [(c0, min(W, total - c0)) for c0 in range(0, total, W)]

        SLk = 512
        nb = alloc_norm_bufs(A3, "c", SLk)
        xs = [A3.sb(f"xr{i}", [128, KT, SLk], F32) for i in range(2)]
        kst = [A3.sb(f"kst{i}", [128, 4, SLk], BF16) for i in range(2)]
        vst = [A3.sb(f"vst{i}", [128, 4, 512], BF16) for i in range(2)]
        xTr_v = xTr.rearrange("(k p) t -> p k t", p=128)
        kT_v = kT_scr.rearrange("(k p) t -> p k t", p=128)
        v_v = v_scr.rearrange("hp p blk c -> p hp blk c")
        sls = slabs(TA, SLk)
        ukk = {}

        def ld_k(s):
            c0, W = sls[s]
            sl_ = s % 2
            dma(c, xs[sl_][:, :, 0:W], xTr_v[:, :, c0:c0 + W], [], [f"xr{sl_}"], f"xs{sl_}")

        def nm_k(s):
            ukk[s] = norm_slab(c, xs[s % 2], f"xr{s % 2}", sls[s][1], nb, s % 2)

        def pj_k(s, part):
            c0, W = sls[s]
            slot = s % 2
            uk = ukk[s]
            uT = nb["uT"][slot]
            if part == 1:
                return pj_kv(s, c0, W, slot, uk, uT)
            for ct in range(4):
                bank = ps_next()
                ps = c.PS[bank]
                for k in range(KT):
                    mm(c, ps[:, 0:W], Wkv[:, k, ct * 128:(ct + 1) * 128], uT[:, k, 0:W], k == 0, k == KT - 1,
                       ["Wkv", uk], [f"ps{bank}"])
                cp(c, "act" if ct % 2 == 0 else "dve", kst[slot][:, ct, 0:W], ps[:, 0:W], [f"ps{bank}"], [f"kst{slot}"])
            dma(c, kT_v[:, :, c0:c0 + W], kst[slot][:, :, 0:W], [f"kst{slot}"], ["kT_scr"], f"ko{slot}")

        def pj_kv(s, c0, W, slot, uk, uT):
            ntb = W // 128
            for tb in range(ntb):
                bank = ps_next()
                ps = c.PS[bank]
                for k in range(KT):
                    mm(c, ps[:, :], uT[:, k, tb * 128:(tb + 1) * 128], Wkv[:, k, 512:1024], k == 0, k == KT - 1,
                       ["Wkv", uk], [f"ps{bank}"])
                cp(c, "act" if tb % 2 == 1 else "dve", vst[slot][:, tb, :], ps[:, :], [f"ps{bank}"], [f"vst{slot}"])
            blk0 = c0 // 128
            for hp in range(4):
                dma(c, v_scr[hp, :, blk0:blk0 + ntb, :], vst[slot][:, 0:ntb, hp * 128:(hp + 1) * 128],
                    [f"vst{slot}"], ["v_scr"], f"vo{slot}")
        slab_pipeline(len(sls), ld_k, nm_k, lambda s: pj_k(s, 0), lambda s: pj_k(s, 1))
        p.barrier()
        A3.close()

        A4 = Alloc(nc)
        ya = A4.sb("ya", [128, 32, 512], BF16)
        Khp = [A4.sb(f"Khp{i}", [128, TA], BF16) for i in range(2)]
        Vhp = [A4.sb(f"Vhp{i}", [128, 66, 128], BF16) for i in range(2)]
        Qhp = [A4.sb(f"Qhp{i}", [128, TO], BF16) for i in range(2)]
        NS = 3
        fb = [A4.sb(f"fb{i}", [128, 513], F32) for i in range(NS)]
        Pb = [A4.sb(f"Pb{i}", [128, 513], F32) for i in range(NS)]
        Ab = [A4.sb(f"Ab{i}", [128, 512], BF16) for i in range(NS)]
        ATb = [A4.sb(f"ATb{i}", [128, 512], BF16) for i in range(NS)]
        zer = A4.sb("zer", [128, 513], F32)
        memset(c, zer[:], 0.0, ["zer"])
        for i in range(NS):
            memset(c, fb[i][:, 0:1], 1.0, [f"fb{i}"])

        def load_hp(hp):
            sl = hp % 2
            for j in range(4):
                a, b = j * (TA // 4), (j + 1) * (TA // 4)
                dma(c, Khp[sl][:, a:b], kT_scr[hp * 128:(hp + 1) * 128, a:b], ["kT_scr"], [f"Khp{sl}"], f"khp{sl}")
            for j in range(2):
                dma(c, Vhp[sl][:, j * 33:(j + 1) * 33, :], v_scr[hp, :, j * 33:(j + 1) * 33, :], ["v_scr"], [f"Vhp{sl}"], f"vhp{sl}")
            dma(c, Qhp[sl][:], qT_scr[hp * 128:(hp + 1) * 128, :], ["qT_scr"], [f"Qhp{sl}"], f"qhp{sl}")

        tasks = []
        for hp in range(4):
            for i in range(32):
                r0 = 8064 - 256 * i
                chunks = []
                r = r0
                while r < 8320:
                    w = min(512, 8320 - r)
                    chunks.append((r, w))
                    r += w
                for hh in range(2):
                    for ci, (r, w) in enumerate(chunks):
                        tasks.append(dict(hp=hp, i=i, hh=hh, ci=ci, r=r, w=w, last=(ci == len(chunks) - 1)))
        NT = len(tasks)
        yo_i = [0]

        def stageA1(n):
            tk = tasks[n]
            sl = tk["hp"] % 2
            s3 = n % NS
            hs = slice(64 * tk["hh"], 64 * tk["hh"] + 64)
            w = tk["w"]
            zb = n % 3
            z = c.PS[zb]
            mm(c, z[:, 0:w], Qhp[sl][hs, tk["i"] * 128:(tk["i"] + 1) * 128], Khp[sl][hs, tk["r"]:tk["r"] + w], True, True,
               [f"Qhp{sl}", f"Khp{sl}"], [f"ps{zb}"])
            act(c, fb[s3][:, 1:1 + w], z[:, 0:w], AF.Sigmoid, [f"ps{zb}"], [f"fb{s3}"], scale=-0.125)

        def stageA2(n):
            tk = tasks[n]
            s3 = n % NS
            w = tk["w"]
            if tk["ci"] == 0:
                tt(c, fb[s3][:, 1:129], fb[s3][:, 1:129], c.mask[:], ALU.max, [f"fb{s3}", "mask"], [f"fb{s3}"])
                init = 1.0
                rd = [f"fb{s3}", "zer"]
            else:
                pw = tasks[n - 1]["w"]
                sp_ = (n - 1) % NS
                init = Pb[sp_][:, pw:pw + 1]
                rd = [f"fb{s3}", "zer", f"Pb{sp_}"]
            c.p.op("dve", lambda e: e.tensor_tensor_scan(out=Pb[s3][:, 0:w + 1], data0=fb[s3][:, 0:w + 1], data1=zer[:, 0:w + 1],
                                                         initial=init, op0=ALU.mult, op1=ALU.add),
                   reads=rd, writes=[f"Pb{s3}"])
            tt(c, Ab[s3][:, 0:w], Pb[s3][:, 0:w], Pb[s3][:, 1:w + 1], ALU.subtract, [f"Pb{s3}"], [f"Ab{s3}"])

        def stageB(n):
            tk = tasks[n]
            s3 = n % NS
            w = tk["w"]
            pb = n % 2
            for c4 in range(w // 128):
                c.p.op("pe", lambda e, c4=c4: e.transpose(out=c.PB[pb][:, c4 * 128:(c4 + 1) * 128],
                                                          in_=Ab[s3][:, c4 * 128:(c4 + 1) * 128], identity=c.ident[:]),
                       reads=[f"Ab{s3}", "ident"], writes=[f"pb{pb}"])
            act(c, ATb[s3][:, 0:w], c.PB[pb][:, 0:w], AF.Copy, [f"pb{pb}"], [f"ATb{s3}"])

        def stageC(n):
            tk = tasks[n]
            sl = tk["hp"] % 2
            s3 = n % NS
            w = tk["w"]
            if tk["ci"] == 0:
                yo_i[0] = 3 + (yo_i[0] + 1) % 3
            yb = yo_i[0]
            yo = c.PS[yb]
            nb4 = w // 128
            for c4 in range(nb4):
                blk = tk["r"] // 128 + c4
                mm(c, yo[:, 0:64], ATb[s3][:, c4 * 128:(c4 + 1) * 128], Vhp[sl][:, blk, 64 * tk["hh"]:64 * tk["hh"] + 64],
                   tk["ci"] == 0 and c4 == 0, tk["last"] and c4 == nb4 - 1, [f"ATb{s3}", f"Vhp{sl}"], [f"ps{yb}"])
            if tk["last"]:
                head = 2 * tk["hp"] + tk["hh"]
                cp(c, "act", ya[:, tk["i"], head * 64:(head + 1) * 64], yo[:, 0:64], [f"ps{yb}"], ["ya"])

        load_hp(0)
        stageA1(0)
        for n in range(NT + 2):
            if n + 1 < NT and tasks[n + 1]["hp"] == tasks[min(n, NT - 1)]["hp"]:
                stageA1(n + 1)
            if n < NT:
                if tasks[n]["hp"] != tasks[n - 1]["hp"] and n > 0:
                    stageA1(n)
                stageA2(n)
            if 0 <= n - 1 < NT:
                stageB(n - 1)
            if 0 <= n - 2 < NT:
                stageC(n - 2)
                tk = tasks[n - 2]
                if tk["i"] == 0 and tk["hh"] == 0 and tk["ci"] == 0 and tk["hp"] + 1 < 4:
                    load_hp(tk["hp"] + 1)
        yat = [A4.sb(f"yat{i}", [128, 4, 128], BF16) for i in range(2)]
        yaT_v = yaT_scr.rearrange("(k p) t -> p k t", p=128)
        for i in range(32):
            pb = i % 2
            for t in range(4):
                c.p.op("pe", lambda e, t=t, i=i, pb=pb: e.transpose(out=c.PB[pb][:, t * 128:(t + 1) * 128],
                                                                     in_=ya[:, i, t * 128:(t + 1) * 128], identity=c.ident[:]),
                       reads=["ya", "ident"], writes=[f"pb{pb}"])
            cp(c, "act" if i % 2 == 0 else "dve", yat[pb][:].rearrange("p a b -> p (a b)"), c.PB[pb][:, 0:512], [f"pb{pb}"], [f"yat{pb}"])
            dma(c, yaT_v[:, :, i * 128:(i + 1) * 128], yat[pb][:], [f"yat{pb}"], ["yaT_scr"], f"yao{pb}")
        p.barrier()
        A4.close()

        A6 = Alloc(nc)
        set_wst(A6, 2)
        Wg = load_weight(c, A6, "Wg", w_gate, D, [(0, 2048)], scale_cols=G_MIX)
        Wus = load_weight(c, A6, "Wus", w_up_ssm, 512, [(0, D)])
        Wua = load_weight(c, A6, "Wua", w_up_attn, 512, [(0, D)])
        Wo = load_weight(c, A6, "Wo", w_out, D, [(0, D)])
        SB = 512
        nb = alloc_norm_bufs(A6, "d", SB, nslot=1)
        xs = [A6.sb(f"xb{i}", [128, KT, SB], F32) for i in range(2)]
        ysl = [A6.sb(f"ysl{i}", [128, 4, SB], BF16) for i in range(2)]
        yal = [A6.sb(f"yal{i}", [128, 4, SB], BF16) for i in range(2)]
        sgs = [A6.sb(f"sgs{i}", [128, 2, SB], F32) for i in range(2)]
        m1 = [A6.sb(f"m1{i}", [128, SB], F32) for i in range(2)]
        m2 = [A6.sb(f"m2{i}", [128, SB], F32) for i in range(2)]
        mg = A6.sb("mg", [128, KT, SB], BF16)
        hst = A6.sb("hst", [128, KT, SB], F32)
        hT_v = hT_scr.rearrange("(k p) t -> p k t", p=128)
        nsl = TO // SB

        def ld3b(s):
            sl = s % 2
            dma(c, xs[sl][:], xTo_v[:, :, s * SB:(s + 1) * SB], [], [f"xb{sl}"], f"xs{sl}")
            dma(c, ysl[sl][:], ysT_v[:, :, s * SB:(s + 1) * SB], ["ysT_scr"], [f"ysl{sl}"], f"ysl{sl}")
            dma(c, yal[sl][:], yaT_v[:, :, s * SB:(s + 1) * SB], ["yaT_scr"], [f"yal{sl}"], f"yal{sl}")
        ld3b(0)
        it = 0
        for s in range(nsl):
            slot = s % 2
            if s + 1 < nsl:
                ld3b(s + 1)
            uk = norm_slab(c, xs[slot], f"xb{slot}", SB, nb, 0)
            uT = nb["uT"][0]
            for j in range(8):
                b2 = it % 2
                it += 1
                banks = [ps_next() for _ in range(4)]
                pgs, pga, pus, pua = (c.PS[b_] for b_ in banks)
                for k in range(KT):
                    mm(c, pgs[:, :], Wg[:, k, j * 128:(j + 1) * 128], uT[:, k, :], k == 0, k == KT - 1, ["Wg", uk], [f"ps{banks[0]}"])
                for k in range(KT):
                    mm(c, pga[:, :], Wg[:, k, 1024 + j * 128:1024 + (j + 1) * 128], uT[:, k, :], k == 0, k == KT - 1,
                       ["Wg", uk], [f"ps{banks[1]}"])
                for t in range(4):
                    mm(c, pus[:, :], Wus[:, t, j * 128:(j + 1) * 128], ysl[slot][:, t, :], t == 0, t == 3,
                       ["Wus", f"ysl{slot}"], [f"ps{banks[2]}"])
                for t in range(4):
                    mm(c, pua[:, :], Wua[:, t, j * 128:(j + 1) * 128], yal[slot][:, t, :], t == 0, t == 3,
                       ["Wua", f"yal{slot}"], [f"ps{banks[3]}"])
                act(c, sgs[b2][:, 0, :], pgs[:, :], AF.Sigmoid, [f"ps{banks[0]}", "vecs"], [f"sgs{b2}a"],
                    bias=c.vecs[:, B_GATE + j:B_GATE + j + 1])
                act(c, sgs[b2][:, 1, :], pga[:, :], AF.Sigmoid, [f"ps{banks[1]}", "vecs"], [f"sgs{b2}b"],
                    bias=c.vecs[:, B_GATE + 8 + j:B_GATE + 8 + j + 1])
                tt(c, m1[b2][:], sgs[b2][:, 0, :], pus[:, :], ALU.mult, [f"sgs{b2}a", f"ps{banks[2]}"], [f"m1{b2}"])
                tt(c, m2[b2][:], sgs[b2][:, 1, :], pua[:, :], ALU.mult, [f"sgs{b2}b", f"ps{banks[3]}"], [f"m2{b2}"])
                tt(c, mg[:, j, :], m1[b2][:], m2[b2][:], ALU.add, [f"m1{b2}", f"m2{b2}"], [f"mg{j}"])
            for j in range(8):
                bo = ps_next()
                po = c.PS[bo]
                for k in range(KT):
                    mm(c, po[:, :], Wo[:, k, j * 128:(j + 1) * 128], mg[:, k, :], k == 0, k == KT - 1,
                       ["Wo", f"mg{k}"], [f"ps{bo}"])
                tt(c, hst[:, j, :], xs[slot][:, j, :], po[:, :], ALU.add, [f"xb{slot}", f"ps{bo}"], ["hst"])
            dma(c, hT_v[:, :, s * SB:(s + 1) * SB], hst[:], ["hst"], ["hT_scr"], "ho0")
        p.barrier()
        A6.close()

        A7 = Alloc(nc)
        W1 = A7.sb("W1", [128, 8, 4096], BF16)
        W2 = A7.sb("W2", [128, 32, 1024], BF16)
        A7s = Alloc(nc)
        set_wst(A7s, 2)
        load_weight(c, None, "W1", w_ff1, D, [(0, 4096)], scale_cols=G_MLP, wsb=W1)
        load_weight(c, None, "W2", w_ff2, 4096, [(0, D)], wsb=W2)
        p.barrier()
        A7s.close()
        FW = 512
        nb = alloc_norm_bufs(A7, "e", FW, nslot=1)
        hsb = [A7.sb(f"hs{i}", [128, KT, FW], F32) for i in range(2)]
        hid = A7.sb("hid", [128, 16, FW], BF16)
        sq4 = [A7.sb(f"sq4{i}", [128, 512], F32) for i in range(2)]
        yT_v = yT.rearrange("(k p) t -> p k t", p=128)
        nsl = TO // FW
        it = 0

        def ld_h(s):
            dma(c, hsb[s % 2][:], hT_v[:, :, s * FW:(s + 1) * FW], ["hT_scr"], [f"hs{s % 2}"], f"xs{s % 2}")
        ld_h(0)
        for s in range(nsl):
            hs_ = hsb[s % 2]
            hk = f"hs{s % 2}"
            if s + 1 < nsl:
                ld_h(s + 1)
            uk = norm_slab(c, hs_, hk, FW, nb, 0)
            hn = nb["uT"][0]
            for half in range(2):
                for b in range(16):
                    jj = half * 16 + b
                    b2 = it % 2
                    it += 1
                    bank = ps_next()
                    ps = c.PS[bank]
                    for k in range(KT):
                        mm(c, ps[:, :], W1[:, k, jj * 128:(jj + 1) * 128], hn[:, k, :], k == 0, k == KT - 1,
                           ["W1", uk], [f"ps{bank}"])
                    act(c, sq4[b2][:], ps[:, :], AF.Square, [f"ps{bank}"], [f"sq4{b2}"])
                    stt(c, hid[:, b, :], ps[:, :], 0.0, sq4[b2][:], ALU.is_gt, ALU.mult, [f"ps{bank}", f"sq4{b2}"], [f"hid{b}"])
                for j in range(8):
                    bank = ps_next()
                    ps = c.PS[bank]
                    for kk in range(16):
                        mm(c, ps[:, :], W2[:, half * 16 + kk, j * 128:(j + 1) * 128], hid[:, kk, :], kk == 0, kk == 15,
                           ["W2", f"hid{kk}"], [f"ps{bank}"])
                    tt(c, hs_[:, j, :], hs_[:, j, :], ps[:, :], ALU.add, [f"{hk}_{j}", hk, f"ps{bank}"], [f"{hk}_{j}"])
            sq = nb["sq"][0]
            rt = nb["rt"][0]
            rs = nb["rs"][0]
            hkeys = [f"{hk}_{j}" for j in range(8)]
            act(c, sq[:], hs_[:], AF.Square, hkeys + [hk, uk], ["esq0"])
            bank = ps_next()
            ps = c.PS[bank]
            for k in range(KT):
                mm(c, ps[:, 0:FW], c.ones[:], sq[:, k, :], k == 0, k == KT - 1, ["esq0", "ones"], [f"ps{bank}"])
            act(c, rt[:], ps[:, 0:FW], AF.Sqrt, [f"ps{bank}"], ["ert0"], scale=1.0 / D, bias=c.epsb[:, 0:1])
            c.p.op("dve", lambda e, rs=rs, rt=rt: e.reciprocal(out=rs[:], in_=rt[:]), reads=["ert0"], writes=["ers0"])
            for j in range(8):
                stt(c, hs_[:, j, :], hs_[:, j, :], c.vecs[:, G_FIN + j:G_FIN + j + 1], rs[:], ALU.mult, ALU.mult,
                    [f"{hk}_{j}", "ers0", "vecs"], [f"{hk}_{j}"])
            dma(c, yT_v[:, :, s * FW:(s + 1) * FW], hs_[:], hkeys + [hk], ["yT"], f"out{s % 2}")
        p.barrier()
        A7.close()
        G.close()
    return nc


def _s5_tables(A_re, A_im, log_dt, B_re, B_im, C_re, C_im):
    f = np.float32
    sp_par = np.zeros((128, 3, 16, 32), f)
    sp_C = np.zeros((128, 2, 16, 32), f)
    sp_B = np.zeros((128, 2, 16, 32), f)
    for gi in range(2):
        rows = slice(64 * gi, 64 * gi + 64)
        for pr in range(16):
            g = 2 * pr + gi
            sp_par[rows, 0, pr, :] = A_re[g][:, None]
            sp_par[rows, 1, pr, :] = A_im[g][:, None]
            sp_par[rows, 2, pr, :] = log_dt[g]
            cs = slice(16 * gi, 16 * gi + 16)
            sp_C[rows, 0, pr, cs] = C_re[g].T
            sp_C[rows, 1, pr, cs] = C_im[g].T
            sp_B[rows, 0, pr, cs] = B_re[g]
            sp_B[rows, 1, pr, cs] = B_im[g]
    ch_par = np.zeros((128, 3, 4, 2, 128), f)
    ch_B = np.zeros((128, 2, 4, 2, 128), f)
    for t in range(4):
        for v in range(2):
            for q in range(4):
                pr = 4 * t + (q if v == 0 else 3)
                rows_q = slice(32 * q, 32 * q + 32)
                for gi2 in range(2):
                    g2 = 2 * pr + gi2
                    cs = slice(64 * gi2, 64 * gi2 + 64)
                    ch_par[rows_q, 0, t, v, cs] = A_re[g2][None, :]
                    ch_par[rows_q, 1, t, v, cs] = A_im[g2][None, :]
                    ch_par[rows_q, 2, t, v, cs] = log_dt[g2]
                    if (v == 0 and q < 3) or (v == 1 and q == 3):
                        rows = slice(32 * q + 16 * gi2, 32 * q + 16 * gi2 + 16)
                        ch_B[rows, 0, t, v, cs] = B_re[g2].T
                        ch_B[rows, 1, t, v, cs] = B_im[g2].T
    return (sp_par.reshape(128, 3, 512), sp_C.reshape(128, 2, 512), sp_B.reshape(128, 2, 512),
            ch_par.reshape(128, 3, 1024), ch_B.reshape(128, 2, 1024))


def make_in_maps(x, norm_mix, w_in, A_re, A_im, log_dt, B_re, B_im, C_re, C_im, D_skip, w_glu, b_glu,
                 w_up_ssm, w_up_attn, w_gate, b_gate, w_out, norm_mlp, w_ff1, w_ff2, norm_final, cores=range(8)):
    f = np.float32
    x = np.asarray(x, f)
    tile = lambda v: np.asarray(v, f).reshape(-1, 128).T
    vecs = np.concatenate([tile(norm_mix[0]), tile(norm_mlp[0]), tile(norm_final), tile(b_gate[0]),
                           tile(D_skip[0]), tile(b_glu[0])], axis=1)
    vecs = np.ascontiguousarray(vecs, f)
    assert vecs.shape == (128, 48)
    sp_par, sp_C, sp_B, ch_par, ch_B = _s5_tables(np.asarray(A_re[0], f), np.asarray(A_im[0], f), np.asarray(log_dt[0], f),
                                                  np.asarray(B_re[0], f), np.asarray(B_im[0], f),
                                                  np.asarray(C_re[0], f), np.asarray(C_im[0], f))
    ql = np.arange(128)
    mask = (ql[None, :] + ql[:, None] <= 127).astype(f)
    shared = dict(w_in=np.ascontiguousarray(w_in[0], f), w_gate=np.ascontiguousarray(w_gate[0], f),
                  w_glu=np.ascontiguousarray(w_glu[0], f), w_up_ssm=np.ascontiguousarray(w_up_ssm[0], f),
                  w_up_attn=np.ascontiguousarray(w_up_attn[0], f), w_out=np.ascontiguousarray(w_out[0], f),
                  w_ff1=np.ascontiguousarray(w_ff1[0], f), w_ff2=np.ascontiguousarray(w_ff2[0], f),
                  vecs=vecs, sp_par=sp_par, sp_C=sp_C, sp_B=sp_B, ch_par=ch_par, ch_B=ch_B, mask=mask)
    maps = []
    for core in cores:
        b, h = core // 2, core % 2
        xb = x[b]
        r = np.arange(TA)
        tok = 8191 + 128 * h - r
        val = (tok >= 0) & (tok < S)
        xr = np.zeros((TA, D), f)
        xr[val] = xb[tok[val]]
        tokn = r - 128 * (1 - h)
        valn = (tokn >= 0) & (tokn < S)
        xn = np.zeros((TA, D), f)
        xn[valn] = xb[tokn[valn]]
        own = xb.reshape(32, 2, 128, D)[:, h].reshape(TO, D)
        m = dict(shared)
        m["xTr"] = np.ascontiguousarray(xr.T)
        m["xTn"] = np.ascontiguousarray(xn.T)
        m["xTo"] = np.ascontiguousarray(own.T)
        maps.append(m)
    return maps


def kernel(**inputs):
    nc = build()
    maps = make_in_maps(**inputs)
    res = run_bass_kernel_spmd(nc, maps, core_ids=list(range(8)))
    out = np.zeros((4, S, D), np.float32)
    ov = out.reshape(4, 32, 2, 128, D)
    for core in range(8):
        b, h = core // 2, core % 2
        yT = np.asarray(res.results[core]["yT"], np.float32)
        ov[b, :, h] = yT.T.reshape(32, 128, D)
    return out
```

```python
import contextlib
import math
import numpy as np
import concourse.bass as bass
import concourse.mybir as mybir
from concourse.bass_utils import run_bass_kernel_spmd

F32 = mybir.dt.float32
BF16 = mybir.dt.bfloat16
I32 = mybir.dt.int32
AF = mybir.ActivationFunctionType
ALU = mybir.AluOpType

D = 1024
KT = 8
S = 8192
TA = 8448
TO = 4096
NCH = TA // 16
SL = 256
EPS = 1e-6
TWO_PI = 2.0 * math.pi
GELU_C = math.sqrt(2.0 / math.pi)


class Prog:
    ENGS = ("pe", "act", "dve", "pool", "sp")
    EPOCH = 30000
    NEP = 4
    NDMA = 72

    def __init__(self, nc, st):
        self.nc = nc
        self.ops = {e: [] for e in self.ENGS}
        self.cnt = {e: 0 for e in self.ENGS}
        self.last_w = {}
        self.readers = {}
        self.waited = {e: {} for e in self.ENGS}
        self.streams = {}
        self.strict_same = False
        self.sems = {}
        for e in self.ENGS[:4]:
            for ep in range(self.NEP):
                self.sems[("eng", e, ep)] = st.enter_context(nc.semaphore(f"s_{e}_{ep}"))
        self.dma_pool = [st.enter_context(nc.semaphore(f"s_dma_{i}")) for i in range(self.NDMA)]
        self.block = st.enter_context(nc.Block())
        self.engmap = {"pe": self.block.tensor, "act": self.block.scalar, "dve": self.block.vector,
                       "pool": self.block.gpsimd, "sp": self.block.sync}

    def _sem(self, key):
        if key not in self.sems:
            self.sems[key] = self.dma_pool.pop()
        return self.sems[key]

    def _need(self, eng, tok, waits):
        if tok is None:
            return
        key, val, peng = tok
        if peng == eng and key[0] != "dma" and not self.strict_same:
            return
        if self.waited[eng].get(key, 0) >= val:
            return
        self.waited[eng][key] = val
        waits[key] = max(waits.get(key, 0), val)

    def op(self, eng, fn, reads=(), writes=(), stream=None):
        waits = {}
        for r in reads:
            self._need(eng, self.last_w.get(r), waits)
        for w in writes:
            self._need(eng, self.last_w.get(w), waits)
            for t in self.readers.get(w, ()):
                self._need(eng, t, waits)
        if eng == "sp":
            assert stream is not None
            self.streams[stream] = self.streams.get(stream, 0) + 1
            tok = (("dma", stream), 16 * self.streams[stream], "sp")
        else:
            self.cnt[eng] += 1
            g = self.cnt[eng]
            ep = (g - 1) // self.EPOCH
            assert ep < self.NEP
            tok = (("eng", eng, ep), g - ep * self.EPOCH, eng)
        for r in reads:
            self.readers.setdefault(r, []).append(tok)
        for w in writes:
            self.last_w[w] = tok
            self.readers[w] = []
        self.ops[eng].append((fn, list(waits.items()), tok))
        return tok

    def barrier(self):
        toks = []
        for e in self.ENGS[:4]:
            g = self.cnt[e]
            if g:
                ep = (g - 1) // self.EPOCH
                toks.append((("eng", e, ep), g - ep * self.EPOCH, e))
        for s, n in self.streams.items():
            toks.append((("dma", s), 16 * n, "sp"))
        for e in self.ENGS:
            waits = {}
            for t in toks:
                self._need(e, t, waits)
            self.ops[e].append((None, list(waits.items()), None))
        self.last_w.clear()
        self.readers.clear()
        self.flush()

    def flush(self):
        for e in self.ENGS:
            ops = self.ops[e]
            if not ops:
                continue

            def body(eng, ops=ops):
                for fn, waits, tok in ops:
                    for k, v in waits:
                        eng.wait_ge(self._sem(k), v)
                    if fn is None:
                        continue
                    ins = fn(eng)
                    ins.then_inc(self._sem(tok[0]), 16 if tok[0][0] == "dma" else 1)
            self.engmap[e](body)
            self.ops[e] = []


class Ctx:
    pass


def _v3(ap, j):
    return ap.rearrange("p (c j) -> p c j", j=j)


def dma(c, out, in_, reads, writes, stream):
    return c.p.op("sp", lambda e: e.dma_start(out=out, in_=in_), reads=reads, writes=writes, stream=stream)


def act(c, out, in_, func, reads, writes, scale=None, bias=None):
    kw = {}
    if scale is not None:
        kw["scale"] = scale
    if bias is not None:
        kw["bias"] = bias
    return c.p.op("act", lambda e: e.activation(out=out, in_=in_, func=func, **kw), reads=reads, writes=writes)


def tt(c, out, a, b, op, reads, writes, eng="dve"):
    return c.p.op(eng, lambda e: e.tensor_tensor(out=out, in0=a, in1=b, op=op), reads=reads, writes=writes)


def ts(c, out, a, s1, op0, reads, writes, s2=None, op1=None, eng="dve"):
    if op1 is None:
        return c.p.op(eng, lambda e: e.tensor_scalar(out=out, in0=a, scalar1=s1, scalar2=None, op0=op0),
                      reads=reads, writes=writes)
    return c.p.op(eng, lambda e: e.tensor_scalar(out=out, in0=a, scalar1=s1, scalar2=s2, op0=op0, op1=op1),
                  reads=reads, writes=writes)


def stt(c, out, a, scalar, b, op0, op1, reads, writes):
    return c.p.op("dve", lambda e: e.scalar_tensor_tensor(out=out, in0=a, scalar=scalar, in1=b, op0=op0, op1=op1),
                  reads=reads, writes=writes)


def cp(c, eng, out, in_, reads, writes):
    if eng == "act":
        return act(c, out, in_, AF.Copy, reads, writes)
    return c.p.op(eng, lambda e: e.tensor_copy(out=out, in_=in_), reads=reads, writes=writes)


def mm(c, out, lhsT, rhs, start, stop, reads, writes):
    return c.p.op("pe", lambda e: e.matmul(out, lhsT=lhsT, rhs=rhs, start=start, stop=stop), reads=reads, writes=writes)


def memset(c, ap, val, writes, eng="pool"):
    return c.p.op(eng, lambda e: e.memset(ap, val), writes=writes)


class Alloc:
    def __init__(self, nc):
        self.nc = nc
        self.st = contextlib.ExitStack()
        self.n = 0

    _uid = [0]

    def sb(self, name, shape, dt):
        Alloc._uid[0] += 1
        return self.st.enter_context(self.nc.sbuf_tensor(f"{name}_u{Alloc._uid[0]}", list(shape), dt))

    def close(self):
        self.st.close()


def load_weight(c, al, name, w_dram, kd, col_ranges, scale_cols=None, wsb=None):
    ncols = sum(b - a for a, b in col_ranges)
    nk = kd // 128
    if wsb is None:
        wsb = al.sb(name, [128, nk, ncols], BF16)
    i = 0
    for kt in range(nk):
        o = 0
        for (a, b) in col_ranges:
            for a2 in range(a, b, 1024):
                b2 = min(b, a2 + 1024)
                w = b2 - a2
                slot = c.wst_i % len(c.wst)
                c.wst_i += 1
                stg = c.wst[slot]
                dma(c, stg[:, 0:w], w_dram[kt * 128:(kt + 1) * 128, a2:b2], [], [f"wst{slot}"], f"wst{slot}")
                dst = wsb[:, kt, o:o + w]
                if scale_cols is not None:
                    sc = c.vecs[:, scale_cols + kt:scale_cols + kt + 1]
                    if i % 2 == 0:
                        act(c, dst, stg[:, 0:w], AF.Copy, [f"wst{slot}", "vecs"], [name], scale=sc)
                    else:
                        ts(c, dst, stg[:, 0:w], sc, ALU.mult, [f"wst{slot}", "vecs"], [name])
                else:
                    cp(c, "act" if i % 2 == 0 else "dve", dst, stg[:, 0:w], [f"wst{slot}"], [name])
                i += 1
                o += w
    return wsb


def norm_slab(c, xs, xkey, W, nb, slot):
    sq = nb["sq"][slot]
    rt = nb["rt"][slot]
    rs = nb["rs"][slot]
    uT = nb["uT"][slot]
    pfx = nb["pfx"]
    act(c, sq[:, :, 0:W], xs[:, :, 0:W], AF.Square, [xkey], [f"{pfx}sq{slot}"])
    bank = c.ps_next()
    ps = c.PS[bank]
    for k in range(KT):
        mm(c, ps[:, 0:W], c.ones[:], sq[:, k, 0:W], k == 0, k == KT - 1, [f"{pfx}sq{slot}", "ones"], [f"ps{bank}"])
    act(c, rt[:, 0:W], ps[:, 0:W], AF.Sqrt, [f"ps{bank}"], [f"{pfx}rt{slot}"], scale=1.0 / D, bias=c.epsb[:, 0:1])
    c.p.op("dve", lambda e: e.reciprocal(out=rs[:, 0:W], in_=rt[:, 0:W]), reads=[f"{pfx}rt{slot}"], writes=[f"{pfx}rs{slot}"])
    tt(c, uT[:, :, 0:W], xs[:, :, 0:W], rs[:, 0:W].unsqueeze(1).to_broadcast([128, KT, W]), ALU.mult,
       [xkey, f"{pfx}rs{slot}"], [f"{pfx}uT{slot}"])
    return f"{pfx}uT{slot}"


def slab_pipeline(n, load_fn, norm_fn, proj_a, proj_b):
    load_fn(0)
    if n > 1:
        load_fn(1)
    norm_fn(0)
    for s in range(n):
        if s + 2 < n:
            load_fn(s + 2)
        proj_a(s)
        if s + 1 < n:
            norm_fn(s + 1)
        proj_b(s)


def alloc_norm_bufs(al, pfx, W, nslot=2):
    return {"pfx": pfx,
            "sq": [al.sb(f"{pfx}sq{i}", [128, KT, W], BF16) for i in range(nslot)],
            "rt": [al.sb(f"{pfx}rt{i}", [128, W], F32) for i in range(nslot)],
            "rs": [al.sb(f"{pfx}rs{i}", [128, W], F32) for i in range(nslot)],
            "uT": [al.sb(f"{pfx}uT{i}", [128, KT, W], BF16) for i in range(nslot)]}


def gen_abeta(c, al, par, pk, pfx, F=512):
    T = lambda n: al.sb(f"{pfx}_{n}", [128, F], F32)
    dt, lr, li, er, t1, t2, sn, cs, ar, ai, br, bi = (T(n) for n in
                                                        ("dt", "lr", "li", "er", "t1", "t2", "sn", "cs", "ar", "ai", "br", "bi"))
    ti = al.sb(f"{pfx}_ti", [128, F], I32)
    K = lambda n: f"{pfx}_{n}"
    Are, Aim, Ldt = par[:, 0, :], par[:, 1, :], par[:, 2, :]
    act(c, dt[:], Ldt, AF.Exp, [pk], [K("dt")])
    tt(c, lr[:], Are, dt[:], ALU.mult, [pk, K("dt")], [K("lr")])
    tt(c, li[:], Aim, dt[:], ALU.mult, [pk, K("dt")], [K("li")])
    act(c, er[:], lr[:], AF.Exp, [K("lr")], [K("er")])

    def sinshift(out, okey, shift):
        ts(c, t1[:], li[:], 1.0 / TWO_PI, ALU.mult, [K("li")], [K("t1")], s2=shift / TWO_PI, op1=ALU.add)
        cp(c, "dve", ti[:], t1[:], [K("t1")], [K("ti")])
        cp(c, "dve", t2[:], ti[:], [K("ti")], [K("t2")])
        tt(c, t1[:], t1[:], t2[:], ALU.subtract, [K("t1"), K("t2")], [K("t1")])
        ts(c, t1[:], t1[:], TWO_PI, ALU.mult, [K("t1")], [K("t1")], s2=math.pi, op1=ALU.min)
        ts(c, t1[:], t1[:], -math.pi, ALU.max, [K("t1")], [K("t1")])
        act(c, out[:], t1[:], AF.Sin, [K("t1")], [okey])

    sinshift(sn, K("sn"), 0.0)
    sinshift(cs, K("cs"), math.pi / 2)
    tt(c, ar[:], er[:], cs[:], ALU.mult, [K("er"), K("cs")], [K("ar")])
    tt(c, ai[:], er[:], sn[:], ALU.mult, [K("er"), K("sn")], [K("ai")])
    den, am1 = dt, lr
    tt(c, t1[:], Are, Are, ALU.mult, [pk], [K("t1")])
    tt(c, t2[:], Aim, Aim, ALU.mult, [pk], [K("t2")])
    tt(c, den[:], t1[:], t2[:], ALU.add, [K("t1"), K("t2")], [K("dt")])
    c.p.op("dve", lambda e: e.reciprocal(out=den[:], in_=den[:]), reads=[K("dt")], writes=[K("dt")])
    ts(c, am1[:], ar[:], -1.0, ALU.add, [K("ar")], [K("lr")])
    tt(c, t1[:], am1[:], Are, ALU.mult, [K("lr"), pk], [K("t1")])
    tt(c, t2[:], ai[:], Aim, ALU.mult, [K("ai"), pk], [K("t2")])
    tt(c, t1[:], t1[:], t2[:], ALU.add, [K("t1"), K("t2")], [K("t1")])
    tt(c, br[:], t1[:], den[:], ALU.mult, [K("t1"), K("dt")], [K("br")])
    tt(c, t1[:], ai[:], Are, ALU.mult, [K("ai"), pk], [K("t1")])
    tt(c, t2[:], am1[:], Aim, ALU.mult, [K("lr"), pk], [K("t2")])
    tt(c, t1[:], t1[:], t2[:], ALU.subtract, [K("t1"), K("t2")], [K("t1")])
    tt(c, bi[:], t1[:], den[:], ALU.mult, [K("t1"), K("dt")], [K("bi")])
    return dict(ar=ar, ai=ai, br=br, bi=bi, kar=K("ar"), kai=K("ai"), kbr=K("br"), kbi=K("bi"),
                tmp=[(t1, K("t1")), (t2, K("t2")), (sn, K("sn")), (cs, K("cs")), (er, K("er")), (li, K("li"))])


def cmul(c, outr, outi, kor, koi, xr, xi, kxr, kxi, yr, yi, kyr, kyi, tmps):
    (t1, k1), (t2, k2), (t3, k3), (t4, k4) = tmps[:4]
    tt(c, t1[:], xr, yr, ALU.mult, [kxr, kyr], [k1])
    tt(c, t2[:], xi, yi, ALU.mult, [kxi, kyi], [k2])
    tt(c, t3[:], xr, yi, ALU.mult, [kxr, kyi], [k3])
    tt(c, t4[:], xi, yr, ALU.mult, [kxi, kyr], [k4])
    tt(c, outr, t1[:], t2[:], ALU.subtract, [k1, k2], [kor])
    tt(c, outi, t3[:], t4[:], ALU.add, [k3, k4], [koi])


def build(debug=False):
    nc = bass.Bass("TRN2", target_bir_lowering=False)
    din = lambda n, s, d=F32: nc.dram_tensor(n, list(s), d, kind="ExternalInput").ap()
    scr_kind = "ExternalOutput"
    dscr = lambda n, s, d: nc.dram_tensor(n, list(s), d, kind=scr_kind).ap()

    xTr = din("xTr", [D, TA])
    xTn = din("xTn", [D, TA])
    xTo = din("xTo", [D, TO])
    w_in = din("w_in", [D, 2048])
    w_gate = din("w_gate", [D, 2048])
    w_glu = din("w_glu", [512, 512])
    w_up_ssm = din("w_up_ssm", [512, D])
    w_up_attn = din("w_up_attn", [512, D])
    w_out = din("w_out", [D, D])
    w_ff1 = din("w_ff1", [D, 4096])
    w_ff2 = din("w_ff2", [4096, D])
    vecs_d = din("vecs", [128, 48])
    sp_par_d = din("sp_par", [128, 3, 512])
    sp_C_d = din("sp_C", [128, 2, 512])
    sp_B_d = din("sp_B", [128, 2, 512])
    ch_par_d = din("ch_par", [128, 3, 1024])
    ch_B_d = din("ch_B", [128, 2, 1024])
    mask_d = din("mask", [128, 128])
    yT = nc.dram_tensor("yT", [D, TO], F32, kind="ExternalOutput").ap()

    kT_scr = dscr("kT_scr", [512, TA], BF16)
    v_scr = dscr("v_scr", [4, 128, 66, 128], BF16)
    qT_scr = dscr("qT_scr", [512, TO], BF16)
    ysT_scr = dscr("ysT_scr", [512, TO], BF16)
    yaT_scr = dscr("yaT_scr", [512, TO], BF16)
    hT_scr = dscr("hT_scr", [D, TO], F32)

    c = Ctx()
    c.nc = nc
    with contextlib.ExitStack() as st:
        c.p = Prog(nc, st)
        p = c.p
        c.PS = [st.enter_context(nc.psum_tensor(f"psf{i}", [128, 512], F32)) for i in range(6)]
        c.PB = [st.enter_context(nc.psum_tensor(f"psb{i}", [128, 1024], BF16)) for i in range(2)]
        c.ps_i = 0

        def ps_next():
            c.ps_i = (c.ps_i + 1) % 6
            return c.ps_i
        c.ps_next = ps_next

        G = Alloc(nc)
        c.vecs = G.sb("vecs", [128, 48], F32)
        c.ones = G.sb("ones", [128, 128], BF16)
        c.ident = G.sb("ident", [128, 128], BF16)
        c.epsb = G.sb("epsb", [128, 1], F32)
        c.mask = G.sb("maskt", [128, 128], F32)
        c.wst_i = 0

        def set_wst(al, n):
            c.wst = [al.sb(f"wst{i}", [128, 1024], F32) for i in range(n)]
        dma(c, c.vecs[:], vecs_d[:, :], [], ["vecs"], "c_vecs")
        dma(c, c.mask[:], mask_d[:, :], [], ["mask"], "c_mask")
        memset(c, c.ones[:], 1.0, ["ones"])
        memset(c, c.epsb[:], EPS, ["epsb"])
        memset(c, c.ident[:], 1.0, ["ident"])
        p.op("pool", lambda e: e.affine_select(out=c.ident[:], in_=c.ident[:], pattern=[[-1, 128]],
                                               compare_op=ALU.is_equal, fill=0.0, base=0, channel_multiplier=1),
             reads=["ident"], writes=["ident"])
        G_MIX, G_MLP, G_FIN, B_GATE, D_SKIP, B_GLU = 0, 8, 16, 24, 40, 44

        A5 = Alloc(nc)
        Xown = [A5.sb(f"Xown{ri}", [128, 16, 256], BF16) for ri in range(2)]
        ksr = A5.sb("ksr", [128, 10, 16], F32)
        ksi = A5.sb("ksi", [128, 10, 16], F32)
        ksni = A5.sb("ksni", [128, 10, 16], F32)
        pwr = A5.sb("pwr", [128, 8, 16], F32)
        pwi = A5.sb("pwi", [128, 8, 16], F32)
        pwn = A5.sb("pwn", [128, 8, 16], F32)
        memset(c, pwr[:].rearrange("p k n -> p (k n)"), 0.0, ["pwr"])
        memset(c, pwi[:].rearrange("p k n -> p (k n)"), 0.0, ["pwi"])

        def sp_gen(want_ks, Cm=None, Kt=None, CmZ=None):
            A0 = Alloc(nc)
            sp_par = A0.sb("sp_par", [128, 3, 512], F32)
            dma(c, sp_par[:], sp_par_d[:, :, :], [], ["sp_par"], "c_sp_par")
            g = gen_abeta(c, A0, sp_par, "sp_par", "sp")
            tmps = g["tmp"]
            if Cm is not None:
                sp_C = A0.sb("sp_C", [128, 2, 512], F32)
                sp_B = A0.sb("sp_B", [128, 2, 512], F32)
                dma(c, sp_C[:], sp_C_d[:, :, :], [], ["sp_C"], "c_sp_C")
                dma(c, sp_B[:], sp_B_d[:, :, :], [], ["sp_B"], "c_sp_B")
                BTb = [A0.sb(f"BTb{ri}", [128, 512], BF16) for ri in range(2)]
                cmul(c, BTb[0][:], BTb[1][:], "BTb0", "BTb1", g["br"][:], g["bi"][:], g["kbr"], g["kbi"],
                     sp_B[:, 0, :], sp_B[:, 1, :], "sp_B", "sp_B", tmps)
                curr = A0.sb("curr", [128, 512], F32)
                curi = A0.sb("curi", [128, 512], F32)
                cp(c, "act", curr[:], sp_C[:, 0, :], ["sp_C"], ["curr"])
                cp(c, "act", curi[:], sp_C[:, 1, :], ["sp_C"], ["curi"])
                for tau in range(17):
                    cp(c, "act", Cm[0][:, tau, :], curr[:], ["curr"], ["Cm0"])
                    act(c, Cm[1][:, tau, :], curi[:], AF.Copy, ["curi"], ["Cm1"], scale=-1.0)
                    if tau < 16:
                        cmul(c, curr[:], curi[:], "curr", "curi", curr[:], curi[:], "curr", "curi",
                             g["ar"][:], g["ai"][:], g["kar"], g["kai"], tmps)
                BTbZ = [A0.sb(f"BTbZ{ri}", [128, 4, 64], BF16) for ri in range(2)]
                for ri in range(2):
                    act(c, BTbZ[ri][:].rearrange("p a b -> p (a b)"), c.ones[:, 0:1].to_broadcast([128, 256]), AF.Copy,
                        ["ones"], [f"BTbZ{ri}"], scale=0.0)
                    act(c, CmZ[ri][:].rearrange("p a b c -> p (a b c)"), c.ones[:, 0:1].to_broadcast([128, 17 * 4 * 64]), AF.Copy,
                        ["ones"], [f"CmZ{ri}"], scale=0.0)
                    for t in range(4):
                        cols = slice((4 * t + 3) * 32, (4 * t + 4) * 32)
                        cp(c, "act", BTbZ[ri][:, t, 32:64], BTb[ri][:, cols], [f"BTb{ri}", f"BTbZ{ri}"], [f"BTbZ{ri}"])
                        cp(c, "act", CmZ[ri][:, :, t, 32:64], Cm[ri][:, :, cols], [f"Cm{ri}", f"CmZ{ri}"], [f"CmZ{ri}"])
                act(c, Kt[:].rearrange("p a b c -> p (a b c)"), c.ones[:, 0:1].to_broadcast([128, 4 * 16 * 128]), AF.Copy,
                    ["ones"], ["Kt"], scale=0.0)
                for t in range(4):
                    bank = ps_next()
                    ps = c.PS[bank]
                    for q in (3, 2, 1, 0):
                        pr = 4 * t + q
                        cols = slice(pr * 32, (pr + 1) * 32)
                        if q == 3:
                            o3 = _v3(ps[64:128, :], 32)
                            l0, l1 = BTbZ[0][:, t, :], BTbZ[1][:, t, :]
                        else:
                            o3 = _v3(ps[32 * q:32 * q + 32, :], 32)
                            l0, l1 = BTb[0][:, cols], BTb[1][:, cols]
                        mm(c, o3, l0, Cm[0][:, 0:16, cols], q != 2, False, ["BTb0", "BTbZ0", "Cm0"], [f"ps{bank}"])
                        mm(c, o3, l1, Cm[1][:, 0:16, cols], False, q == 0, ["BTb1", "BTbZ1", "Cm1"], [f"ps{bank}"])
                    for q in range(4):
                        act(c, Kt[32 * q:32 * q + 32, t, :, 32 * q:32 * q + 32], _v3(ps[32 * q:32 * q + 32, :], 32), AF.Copy,
                            [f"ps{bank}"], ["Kt"])
            if want_ks:
                acr = A0.sb("acr", [128, 16], F32)
                aci = A0.sb("aci", [128, 16], F32)
                q1 = A0.sb("q1", [128, 16], F32)
                q2 = A0.sb("q2", [128, 16], F32)
                q3 = A0.sb("q3", [128, 16], F32)
                cp(c, "dve", acr[:], _v3(g["ar"][:], 32)[:, :, 0], [g["kar"]], ["acr"])
                cp(c, "dve", aci[:], _v3(g["ai"][:], 32)[:, :, 0], [g["kai"]], ["aci"])

                pad_t = A0.sb("pad_t", [128, 8], F32)

                def pad(n=2):
                    for _ in range(n):
                        memset(c, pad_t[:, 0:1], 0.0, ["pad_t"], eng="dve")

                def csq(outr, outi, kor, koi, inr, ini, kir, kii):
                    tt(c, q1[:], inr, inr, ALU.mult, [kir], ["q1"])
                    tt(c, q2[:], ini, ini, ALU.mult, [kii], ["q2"])
                    tt(c, q3[:], inr, ini, ALU.mult, [kir, kii], ["q3"])
                    pad(2)
                    tt(c, outr, q1[:], q2[:], ALU.subtract, ["q1", "q2"], [kor])
                    ts(c, outi, q3[:], 2.0, ALU.mult, ["q3"], [koi])
                    pad(2)

                for _ in range(4):
                    csq(acr[:], aci[:], "acr", "aci", acr[:], aci[:], "acr", "aci")
                cp(c, "dve", ksr[:, 0, :], acr[:], ["acr"], ["ksr"])
                cp(c, "dve", ksi[:, 0, :], aci[:], ["aci"], ["ksi"])
                for k in range(1, 10):
                    csq(ksr[:, k, :], ksi[:, k, :], "ksr", "ksi", ksr[:, k - 1, :], ksi[:, k - 1, :], "ksr", "ksi")
                ts(c, ksni[:].rearrange("p k n -> p (k n)"), ksi[:].rearrange("p k n -> p (k n)"), -1.0, ALU.mult, ["ksi"], ["ksni"])
                T_ = A0.sb("pwT", [128, 16, 16], F32)
                pad(3)

                def prod_mults(ti, xr, xi, yr, yi, rd):
                    tt(c, T_[:, ti + 0, :], xr, yr, ALU.mult, rd, ["pwT"])
                    tt(c, T_[:, ti + 1, :], xi, yi, ALU.mult, rd, ["pwT"])
                    tt(c, T_[:, ti + 2, :], xr, yi, ALU.mult, rd, ["pwT"])
                    tt(c, T_[:, ti + 3, :], xi, yr, ALU.mult, rd, ["pwT"])

                def prod_fin(ti, m):
                    tt(c, pwr[:, m, :], T_[:, ti + 0, :], T_[:, ti + 1, :], ALU.subtract, ["pwT"], ["pwr"])
                    tt(c, pwi[:, m, :], T_[:, ti + 2, :], T_[:, ti + 3, :], ALU.add, ["pwT"], ["pwi"])

                K_ = lambda k: (ksr[:, k, :], ksi[:, k, :])
                prod_mults(0, *K_(0), *K_(1), ["ksr", "ksi"])
                prod_mults(4, *K_(0), *K_(2), ["ksr", "ksi"])
                prod_mults(8, *K_(1), *K_(2), ["ksr", "ksi"])
                prod_fin(0, 3)
                prod_fin(4, 5)
                prod_fin(8, 6)
                for m, k in ((1, 0), (2, 1), (4, 2)):
                    cp(c, "dve", pwr[:, m, :], ksr[:, k, :], ["ksr"], ["pwr"])
                    cp(c, "dve", pwi[:, m, :], ksi[:, k, :], ["ksi"], ["pwi"])
                prod_mults(12, pwr[:, 3, :], pwi[:, 3, :], *K_(2), ["pwr", "pwi", "ksr", "ksi"])
                pad(3)
                prod_fin(12, 7)
                pad(3)
                ts(c, pwn[:].rearrange("p k n -> p (k n)"), pwi[:].rearrange("p k n -> p (k n)"), -1.0, ALU.mult, ["pwi"], ["pwn"])
                pad(3)
            p.barrier()
            A0.close()

        sp_gen(True)
        if debug:
            dbg = nc.dram_tensor("dbg", [128, 4, 8, 16], F32, kind="ExternalOutput").ap()
            dma(c, dbg[:, 0, :, :], pwr[:], ["pwr"], ["dbg"], "dbg0")
            dma(c, dbg[:, 1, :, :], pwi[:], ["pwi"], ["dbg"], "dbg1")
            dma(c, dbg[:, 2, :, :], ksr[:, 0:8, :], ["ksr"], ["dbg"], "dbg2")
            dma(c, dbg[:, 3, :, :], ksi[:, 0:8, :], ["ksi"], ["dbg"], "dbg3")
        A1 = Alloc(nc)
        Uall = A1.sb("Uall", [128, 4, 16, NCH], BF16)
        A1b = Alloc(nc)
        set_wst(A1b, 2)
        Wssm = load_weight(c, A1b, "Wssm", w_in, D, [(0, 512)], scale_cols=G_MIX)
        SLs = 256
        nb = alloc_norm_bufs(A1b, "a", SLs)
        xs = [A1b.sb(f"xs{i}", [128, KT, SLs], F32) for i in range(2)]
        xTn_v = xTn.rearrange("(k p) t -> p k t", p=128)
        nsl = TA // SLs
        uks = {}

        def ld_s(s):
            sl_ = s % 2
            dma(c, xs[sl_][:], xTn_v[:, :, s * SLs:(s + 1) * SLs], [], [f"xs{sl_}"], f"xs{sl_}")

        def nm_s(s):
            uks[s] = norm_slab(c, xs[s % 2], f"xs{s % 2}", SLs, nb, s % 2)

        def pj_s(s, cts=range(4)):
            uT = nb["uT"][s % 2]
            for ct in cts:
                bank = ps_next()
                ps = c.PS[bank]
                for k in range(KT):
                    mm(c, ps[:, 0:SLs], Wssm[:, k, ct * 128:(ct + 1) * 128], uT[:, k, :], k == 0, k == KT - 1,
                       ["Wssm", uks[s]], [f"ps{bank}"])
                n0, nn = (s * SLs) // 16, SLs // 16
                cp(c, "act" if ct % 2 == 0 else "dve", Uall[:, ct, :, n0:n0 + nn],
                   ps[:, 0:SLs].rearrange("p (n r) -> p r n", r=16), [f"ps{bank}"], ["Uall"])
        slab_pipeline(nsl, ld_s, nm_s, lambda s: pj_s(s, range(0, 2)), lambda s: pj_s(s, range(2, 4)))
        p.barrier()
        A1b.close()

        ABm = Alloc(nc)
        Bm = [ABm.sb(f"Bm{ri}", [128, 16, 1024], BF16) for ri in range(2)]
        for hf in range(2):
            A0 = Alloc(nc)
            cs_ = slice(hf * 512, (hf + 1) * 512)
            ch_par = A0.sb("ch_par", [128, 3, 512], F32)
            ch_B = A0.sb("ch_B", [128, 2, 512], F32)
            dma(c, ch_par[:], ch_par_d[:, :, cs_], [], ["ch_par"], "c_ch_par")
            dma(c, ch_B[:], ch_B_d[:, :, cs_], [], ["ch_B"], "c_ch_B")
            g = gen_abeta(c, A0, ch_par, "ch_par", "ch")
            tmps = g["tmp"]
            curr = A0.sb("curr2", [128, 512], F32)
            curi = A0.sb("curi2", [128, 512], F32)
            cmul(c, curr[:], curi[:], "curr2", "curi2", g["br"][:], g["bi"][:], g["kbr"], g["kbi"],
                 ch_B[:, 0, :], ch_B[:, 1, :], "ch_B", "ch_B", tmps)
            for rho in range(15, -1, -1):
                cp(c, "act", Bm[0][:, rho, cs_], curr[:], ["curr2"], ["Bm0"])
                cp(c, "act", Bm[1][:, rho, cs_], curi[:], ["curi2"], ["Bm1"])
                if rho > 0:
                    cmul(c, curr[:], curi[:], "curr2", "curi2", curr[:], curi[:], "curr2", "curi2",
                         g["ar"][:], g["ai"][:], g["kar"], g["kai"], tmps)
            p.barrier()
            A0.close()

        A1c = Alloc(nc)
        Sst2 = [[A1c.sb(f"S{ri}_{b_}", [128, 4, NCH], F32) for ri in range(2)] for b_ in range(2)]
        kb = [[A1c.sb(f"kb{j}_{ri}", [128, NCH], F32) for ri in range(2)] for j in range(2)]
        HN = NCH // 2
        i = 0
        for t in range(4):
            Sst = Sst2[t % 2]
            sp_ = f"b{t % 2}"
            for q in range(4):
                pr = 4 * t + q
                if q < 3:
                    rows = slice(32 * q, 32 * q + 32)
                    bcol = (t * 2 + 0) * 128
                else:
                    rows = slice(64, 128)
                    bcol = (t * 2 + 1) * 128
                for ri in range(2):
                    for half in range(2):
                        bank = ps_next()
                        ps = c.PS[bank]
                        for rho in range(16):
                            mm(c, ps[:, 0:HN], Bm[ri][rows, rho, bcol:bcol + 128], Uall[rows, t, rho, half * HN:(half + 1) * HN],
                               rho == 0, rho == 15, [f"Bm{ri}", "Uall"], [f"ps{bank}"])
                        cp(c, "act", Sst[ri][:, q, half * HN:(half + 1) * HN], ps[:, 0:HN],
                           [f"ps{bank}"], [f"S{sp_}{ri}_{q}"])
                        i += 1
            for q in range(4):
                pr = 4 * t + q
                jb = pr % 2
                Sk = [f"S{sp_}0_{q}", f"S{sp_}1_{q}"]
                L3 = [Sst[ri][:, q, :].rearrange("p (b w) -> p b w", w=8) for ri in range(2)]
                a16r, a16i, a16n = ksr[:, 0, pr:pr + 1], ksi[:, 0, pr:pr + 1], ksni[:, 0, pr:pr + 1]
                for cc in range(1, 8):
                    stt(c, L3[0][:, :, cc], L3[0][:, :, cc - 1], a16r, L3[0][:, :, cc], ALU.mult, ALU.add, [Sk[0], "ksr"], [Sk[0]])
                    stt(c, L3[1][:, :, cc], L3[1][:, :, cc - 1], a16r, L3[1][:, :, cc], ALU.mult, ALU.add, [Sk[1], "ksr"], [Sk[1]])
                    stt(c, L3[0][:, :, cc], L3[1][:, :, cc - 1], a16n, L3[0][:, :, cc], ALU.mult, ALU.add, [Sk[0], Sk[1], "ksni"], [Sk[0]])
                    stt(c, L3[1][:, :, cc], L3[0][:, :, cc - 1], a16i, L3[1][:, :, cc], ALU.mult, ALU.add, [Sk[0], Sk[1], "ksi"], [Sk[1]])
                NBK = NCH // 8
                ea = [kb[jb][0][:, 0:NBK], kb[jb][1][:, 0:NBK]]
                eb = [kb[jb][0][:, 128:128 + NBK], kb[jb][1][:, 128:128 + NBK]]
                ka = [f"kb{jb}_0a", f"kb{jb}_1a"]
                kbk = [f"kb{jb}_0b", f"kb{jb}_1b"]
                for ri in range(2):
                    cp(c, "dve", ea[ri], L3[ri][:, :, 7], [Sk[ri]], [ka[ri]])
                cur, oth, ck, ok = ea, eb, ka, kbk
                for k in range(7):
                    sft = 1 << k
                    n = NBK - sft
                    Ar = ksr[:, k + 3, pr:pr + 1]
                    Ai = ksi[:, k + 3, pr:pr + 1]
                    nAi = ksni[:, k + 3, pr:pr + 1]
                    stt(c, oth[0][:, sft:], cur[0][:, 0:n], Ar, cur[0][:, sft:], ALU.mult, ALU.add, [ck[0], "ksr"], [ok[0]])
                    stt(c, oth[1][:, sft:], cur[1][:, 0:n], Ar, cur[1][:, sft:], ALU.mult, ALU.add, [ck[1], "ksr"], [ok[1]])
                    stt(c, oth[0][:, sft:], cur[1][:, 0:n], nAi, oth[0][:, sft:], ALU.mult, ALU.add, [ck[1], "ksni", ok[0]], [ok[0]])
                    stt(c, oth[1][:, sft:], cur[0][:, 0:n], Ai, oth[1][:, sft:], ALU.mult, ALU.add, [ck[0], "ksi", ok[1]], [ok[1]])
                    cp(c, "act", oth[0][:, 0:sft], cur[0][:, 0:sft], [ck[0]], [ok[0]])
                    cp(c, "act", oth[1][:, 0:sft], cur[1][:, 0:sft], [ck[1]], [ok[1]])
                    cur, oth = oth, cur
                    ck, ok = ok, ck
                Xp = [cur[ri].rearrange("p (i w) -> p i w", w=2)[:, 0:32, 0] for ri in range(2)]
                Lo = [Sst[ri][:, q, :].rearrange("p (i w c) -> p i w c", w=2, c=8)[:, 0:32, 1, :] for ri in range(2)]
                Xo = [Xown[ri][:, pr, :].rearrange("p (i c) -> p i c", c=8) for ri in range(2)]
                tmp = [kb[jb][0][:, 256:288], kb[jb][1][:, 256:288]]
                tk_ = [f"kb{jb}_0t", f"kb{jb}_1t"]
                for ri in range(2):
                    cp(c, "act", Xo[ri][:, :, 0], Xp[ri], [ck[ri]], [f"Xown{ri}"])
                for cc in range(1, 8):
                    pr_, pi_, pn_ = pwr[:, cc, pr:pr + 1], pwi[:, cc, pr:pr + 1], pwn[:, cc, pr:pr + 1]
                    tb_ = (cc % 2) * 32
                    t0_, t1_ = kb[jb][0][:, 256 + tb_:288 + tb_], kb[jb][1][:, 256 + tb_:288 + tb_]
                    stt(c, t0_, Xp[0], pr_, Lo[0][:, :, cc - 1], ALU.mult, ALU.add, [ck[0], Sk[0], "pwr"], [tk_[0]])
                    stt(c, t1_, Xp[1], pr_, Lo[1][:, :, cc - 1], ALU.mult, ALU.add, [ck[1], Sk[1], "pwr"], [tk_[1]])
                    stt(c, Xo[0][:, :, cc], Xp[1], pn_, t0_, ALU.mult, ALU.add, [ck[1], tk_[0], "pwn"], ["Xown0"])
                    stt(c, Xo[1][:, :, cc], Xp[0], pi_, t1_, ALU.mult, ALU.add, [ck[0], tk_[1], "pwi"], ["Xown1"])
        p.barrier()
        A1c.close()
        ABm.close()
        A1.close()

        A5b = Alloc(nc)
        Cm = [A5b.sb(f"Cm{ri}", [128, 17, 512], BF16) for ri in range(2)]
        Kt = A5b.sb("Kt", [128, 4, 16, 128], BF16)
        CmZ = [A5b.sb(f"CmZ{ri}", [128, 17, 4, 64], BF16) for ri in range(2)]
        sp_gen(False, Cm, Kt, CmZ)

        A2 = Alloc(nc)
        Uown = A2.sb("Uown", [128, 4, TO], BF16)
        A2b = Alloc(nc)
        set_wst(A2b, 2)
        Wsq = load_weight(c, A2b, "Wsq", w_in, D, [(0, 1024)], scale_cols=G_MIX)
        nb = alloc_norm_bufs(A2b, "b", SL)
        xs = [A2b.sb(f"xo{i}", [128, KT, SL], F32) for i in range(2)]
        qst = [A2b.sb(f"qst{i}", [128, 4, SL], BF16) for i in range(2)]
        xTo_v = xTo.rearrange("(k p) t -> p k t", p=128)
        qT_v = qT_scr.rearrange("(k p) t -> p k t", p=128)
        nsl = TO // SL
        uko = {}

        def ld_o(s):
            sl_ = s % 2
            dma(c, xs[sl_][:], xTo_v[:, :, s * SL:(s + 1) * SL], [], [f"xo{sl_}"], f"xs{sl_}")

        def nm_o(s):
            uko[s] = norm_slab(c, xs[s % 2], f"xo{s % 2}", SL, nb, s % 2)

        def pj_o(s, cts=range(8), store=True):
            slot = s % 2
            uT = nb["uT"][slot]
            for ct in cts:
                bank = ps_next()
                ps = c.PS[bank]
                for k in range(KT):
                    mm(c, ps[:, 0:SL], Wsq[:, k, ct * 128:(ct + 1) * 128], uT[:, k, :], k == 0, k == KT - 1,
                       ["Wsq", uko[s]], [f"ps{bank}"])
                eng = "act" if ct % 2 == 0 else "dve"
                if ct < 4:
                    cp(c, eng, Uown[:, ct, s * SL:(s + 1) * SL], ps[:, 0:SL], [f"ps{bank}"], ["Uown"])
                else:
                    cp(c, eng, qst[slot][:, ct - 4, :], ps[:, 0:SL], [f"ps{bank}"], [f"qst{slot}"])
            if store:
                dma(c, qT_v[:, :, s * SL:(s + 1) * SL], qst[slot][:], [f"qst{slot}"], ["qT_scr"], f"qo{slot}")
        slab_pipeline(nsl, ld_o, nm_o, lambda s: pj_o(s, range(0, 4), False), lambda s: pj_o(s, range(4, 8), True))
        p.barrier()
        A2b.close()

        A2c = Alloc(nc)
        set_wst(A2c, 2)
        Wglu = load_weight(c, A2c, "Wglu", w_glu, 512, [(0, 512)])
        SP_ = 512
        NCS = SP_ // 16
        yf = [A2c.sb(f"yf{i}", [128, SP_], F32) for i in range(2)]
        wk = [A2c.sb(f"wk{i}", [128, SP_], F32) for i in range(2)]
        sg = [A2c.sb(f"sg{i}", [128, SP_], F32) for i in range(2)]
        gf = [A2c.sb(f"gf{i}", [128, 4, SP_], F32) for i in range(2)]
        gb = [A2c.sb(f"gb{i}", [128, 4, SP_], BF16) for i in range(2)]
        sz = [A2c.sb(f"sz{i}", [128, SP_], F32) for i in range(2)]
        yst = [A2c.sb(f"yst{i}", [128, 4, SP_], BF16) for i in range(2)]
        ysT_v = ysT_scr.rearrange("(k p) t -> p k t", p=128)
        it = 0
        for s in range(TO // SP_):
            slot = s % 2
            cs_ = slice(s * SP_, (s + 1) * SP_)
            for t in range(4):
                b2 = it % 2
                it += 1
                bank = ps_next()
                ps = c.PS[bank]
                Y3 = _v3(ps[:, 0:SP_], 16)
                U3 = _v3(Uown[:, t, cs_], 16)
                for tau in range(16):
                    mm(c, Y3[:, :, tau:16], Kt[:, t, tau, :], U3[:, :, 0:16 - tau], tau == 0, False, ["Kt", "Uown"], [f"ps{bank}"])
                for q in range(4):
                    pr = 4 * t + q
                    cols = slice(pr * 32, (pr + 1) * 32)
                    if q < 3:
                        Yq = _v3(ps[32 * q:32 * q + 32, 0:SP_], 16)
                    else:
                        Yq = _v3(ps[64:128, 0:SP_], 16)
                    for j in range(16):
                        for ri in range(2):
                            last = (q == 3 and j == 15 and ri == 1)
                            lt = Cm[ri][:, j + 1, cols] if q < 3 else CmZ[ri][:, j + 1, t, :]
                            mm(c, Yq[:, :, j], lt, Xown[ri][:, pr, NCS * s:NCS * (s + 1)], False, last,
                               [f"Cm{ri}", f"CmZ{ri}", f"Xown{ri}"], [f"ps{bank}"])
                stt(c, yf[b2][:], Uown[:, t, cs_], c.vecs[:, D_SKIP + t:D_SKIP + t + 1], ps[:, 0:SP_],
                    ALU.mult, ALU.add, ["Uown", "vecs", f"ps{bank}"], [f"yf{b2}"])
                act(c, wk[b2][:], yf[b2][:], AF.Square, [f"yf{b2}"], [f"wk{b2}"])
                ts(c, wk[b2][:], wk[b2][:], 0.044715, ALU.mult, [f"wk{b2}"], [f"wk{b2}"], s2=1.0, op1=ALU.add)
                tt(c, wk[b2][:], wk[b2][:], yf[b2][:], ALU.mult, [f"wk{b2}", f"yf{b2}"], [f"wk{b2}"])
                act(c, sg[b2][:], wk[b2][:], AF.Sigmoid, [f"wk{b2}"], [f"sg{b2}"], scale=2.0 * GELU_C)
                tt(c, gf[slot][:, t, :], yf[b2][:], sg[b2][:], ALU.mult, [f"yf{b2}", f"sg{b2}"], [f"gf{slot}"])
                cp(c, "act", gb[slot][:, t, :], gf[slot][:, t, :], [f"gf{slot}"], [f"gb{slot}"])
            for t2 in range(4):
                b2 = it % 2
                it += 1
                bank = ps_next()
                ps = c.PS[bank]
                for t in range(4):
                    mm(c, ps[:, 0:SP_], Wglu[:, t, t2 * 128:(t2 + 1) * 128], gb[slot][:, t, :], t == 0, t == 3,
                       ["Wglu", f"gb{slot}"], [f"ps{bank}"])
                act(c, sz[b2][:], ps[:, 0:SP_], AF.Sigmoid, [f"ps{bank}", "vecs"], [f"sz{b2}"],
                    bias=c.vecs[:, B_GLU + t2:B_GLU + t2 + 1])
                tt(c, yst[slot][:, t2, :], gf[slot][:, t2, :], sz[b2][:], ALU.mult, [f"gf{slot}", f"sz{b2}"], [f"yst{slot}"])
            dma(c, ysT_v[:, :, cs_], yst[slot][:], [f"yst{slot}"], ["ysT_scr"], f"yo{slot}")
        p.barrier()
        A2c.close()
        A2.close()
        A5b.close()
        A5.close()

        A3 = Alloc(nc)
        set_wst(A3, 2)
        Wkv = load_weight(c, A3, "Wkv", w_in, D, [(1024, 2048)], scale_cols=G_MIX)
        def slabs(total, W):
            return [(c0, min(W, total - c0)) for c0 in range(0, total, W)]

        SLk = 512
        nb = alloc_norm_bufs(A3, "c", SLk)
        xs = [A3.sb(f"xr{i}", [128, KT, SLk], F32) for i in range(2)]
        kst = [A3.sb(f"kst{i}", [128, 4, SLk], BF16) for i in range(2)]
        vst = [A3.sb(f"vst{i}", [128, 4, 512], BF16) for i in range(2)]
        xTr_v = xTr.rearrange("(k p) t -> p k t", p=128)
        kT_v = kT_scr.rearrange("(k p) t -> p k t", p=128)
        v_v = v_scr.rearrange("hp p blk c -> p hp blk c")
        sls = slabs(TA, SLk)
        ukk = {}

        def ld_k(s):
            c0, W = sls[s]
            sl_ = s % 2
            dma(c, xs[sl_][:, :, 0:W], xTr_v[:, :, c0:c0 + W], [], [f"xr{sl_}"], f"xs{sl_}")

        def nm_k(s):
            ukk[s] = norm_slab(c, xs[s % 2], f"xr{s % 2}", sls[s][1], nb, s % 2)

        def pj_k(s, part):
            c0, W = sls[s]
            slot = s % 2
            uk = ukk[s]
            uT = nb["uT"][slot]
            if part == 1:
                return pj_kv(s, c0, W, slot, uk, uT)
            for ct in range(4):
                bank = ps_next()
                ps = c.PS[bank]
                for k in range(KT):
                    mm(c, ps[:, 0:W], Wkv[:, k, ct * 128:(ct + 1) * 128], uT[:, k, 0:W], k == 0, k == KT - 1,
                       ["Wkv", uk], [f"ps{bank}"])
                cp(c, "act" if ct % 2 == 0 else "dve", kst[slot][:, ct, 0:W], ps[:, 0:W], [f"ps{bank}"], [f"kst{slot}"])
            dma(c, kT_v[:, :, c0:c0 + W], kst[slot][:, :, 0:W], [f"kst{slot}"], ["kT_scr"], f"ko{slot}")

        def pj_kv(s, c0, W, slot, uk, uT):
            ntb = W // 128
            for tb in range(ntb):
                bank = ps_next()
                ps = c.PS[bank]
                for k in range(KT):
                    mm(c, ps[:, :], uT[:, k, tb * 128:(tb + 1) * 128], Wkv[:, k, 512:1024], k == 0, k == KT - 1,
                       ["Wkv", uk], [f"ps{bank}"])
                cp(c, "act" if tb % 2 == 1 else "dve", vst[slot][:, tb, :], ps[:, :], [f"ps{bank}"], [f"vst{slot}"])
            blk0 = c0 // 128
            for hp in range(4):
                dma(c, v_scr[hp, :, blk0:blk0 + ntb, :], vst[slot][:, 0:ntb, hp * 128:(hp + 1) * 128],
                    [f"vst{slot}"], ["v_scr"], f"vo{slot}")
        slab_pipeline(len(sls), ld_k, nm_k, lambda s: pj_k(s, 0), lambda s: pj_k(s, 1))
        p.barrier()
        A3.close()

        A4 = Alloc(nc)
        ya = A4.sb("ya", [128, 32, 512], BF16)
        Khp = [A4.sb(f"Khp{i}", [128, TA], BF16) for i in range(2)]
        Vhp = [A4.sb(f"Vhp{i}", [128, 66, 128], BF16) for i in range(2)]
        Qhp = [A4.sb(f"Qhp{i}", [128, TO], BF16) for i in range(2)]
        NS = 3
        fb = [A4.sb(f"fb{i}", [128, 513], F32) for i in range(NS)]
        Pb = [A4.sb(f"Pb{i}", [128, 513], F32) for i in range(NS)]
        Ab = [A4.sb(f"Ab{i}", [128, 512], BF16) for i in range(NS)]
        ATb = [A4.sb(f"ATb{i}", [128, 512], BF16) for i in range(NS)]
        zer = A4.sb("zer", [128, 513], F32)
        maskneg = A4.sb("maskneg", [128, 128], BF16)
        act(c, maskneg[:], c.mask[:], AF.Copy, ["mask"], ["maskneg"], scale=-30000.0)
        memset(c, zer[:], 0.0, ["zer"])
        for i in range(NS):
            memset(c, fb[i][:, 0:1], 1.0, [f"fb{i}"])

        def load_hp(hp):
            sl = hp % 2
            for j in range(4):
                a, b = j * (TA // 4), (j + 1) * (TA // 4)
                dma(c, Khp[sl][:, a:b], kT_scr[hp * 128:(hp + 1) * 128, a:b], ["kT_scr"], [f"Khp{sl}"], f"khp{sl}")
            for j in range(2):
                dma(c, Vhp[sl][:, j * 33:(j + 1) * 33, :], v_scr[hp, :, j * 33:(j + 1) * 33, :], ["v_scr"], [f"Vhp{sl}"], f"vhp{sl}")
            dma(c, Qhp[sl][:], qT_scr[hp * 128:(hp + 1) * 128, :], ["qT_scr"], [f"Qhp{sl}"], f"qhp{sl}")

        tasks = []
        for hp in range(4):
            for i in range(32):
                r0 = 8064 - 256 * i
                chunks = []
                r = r0
                while r < 8320:
                    w = min(512, 8320 - r)
                    chunks.append((r, w))
                    r += w
                for hh in range(2):
                    for ci, (r, w) in enumerate(chunks):
                        tasks.append(dict(hp=hp, i=i, hh=hh, ci=ci, r=r, w=w, last=(ci == len(chunks) - 1)))
        NT = len(tasks)
        yo_i = [0]

        def stageA1(n):
            tk = tasks[n]
            sl = tk["hp"] % 2
            s3 = n % NS
            hs = slice(64 * tk["hh"], 64 * tk["hh"] + 64)
            w = tk["w"]
            zb = n % 3
            z = c.PS[zb]
            mm(c, z[:, 0:w], Qhp[sl][hs, tk["i"] * 128:(tk["i"] + 1) * 128], Khp[sl][hs, tk["r"]:tk["r"] + w], True, True,
               [f"Qhp{sl}", f"Khp{sl}"], [f"ps{zb}"])
            if tk["ci"] == 0:
                mm(c, z[:, 0:128], c.ident[:], maskneg[:], False, True, ["ident", "maskneg", f"ps{zb}"], [f"ps{zb}"])
            act(c, fb[s3][:, 1:1 + w], z[:, 0:w], AF.Sigmoid, [f"ps{zb}"], [f"fb{s3}"], scale=-0.125)

        def stageA2(n):
            tk = tasks[n]
            s3 = n % NS
            w = tk["w"]
            if tk["ci"] == 0:
                init = 1.0
                rd = [f"fb{s3}", "zer"]
            else:
                pw = tasks[n - 1]["w"]
                sp_ = (n - 1) % NS
                init = Pb[sp_][:, pw:pw + 1]
                rd = [f"fb{s3}", "zer", f"Pb{sp_}"]
            c.p.op("dve", lambda e: e.tensor_tensor_scan(out=Pb[s3][:, 0:w + 1], data0=fb[s3][:, 0:w + 1], data1=zer[:, 0:w + 1],
                                                         initial=init, op0=ALU.mult, op1=ALU.add),
                   reads=rd, writes=[f"Pb{s3}"])
            tt(c, Ab[s3][:, 0:w], Pb[s3][:, 0:w], Pb[s3][:, 1:w + 1], ALU.subtract, [f"Pb{s3}"], [f"Ab{s3}"])

        def stageB(n):
            tk = tasks[n]
            s3 = n % NS
            w = tk["w"]
            pb = n % 2
            for c4 in range(w // 128):
                c.p.op("pe", lambda e, c4=c4: e.transpose(out=c.PB[pb][:, c4 * 128:(c4 + 1) * 128],
                                                          in_=Ab[s3][:, c4 * 128:(c4 + 1) * 128], identity=c.ident[:]),
                       reads=[f"Ab{s3}", "ident"], writes=[f"pb{pb}"])
            act(c, ATb[s3][:, 0:w], c.PB[pb][:, 0:w], AF.Copy, [f"pb{pb}"], [f"ATb{s3}"])

        def stageC(n):
            tk = tasks[n]
            sl = tk["hp"] % 2
            s3 = n % NS
            w = tk["w"]
            if tk["ci"] == 0:
                yo_i[0] = 3 + (yo_i[0] + 1) % 3
            yb = yo_i[0]
            yo = c.PS[yb]
            nb4 = w // 128
            for c4 in range(nb4):
                blk = tk["r"] // 128 + c4
                mm(c, yo[:, 0:64], ATb[s3][:, c4 * 128:(c4 + 1) * 128], Vhp[sl][:, blk, 64 * tk["hh"]:64 * tk["hh"] + 64],
                   tk["ci"] == 0 and c4 == 0, tk["last"] and c4 == nb4 - 1, [f"ATb{s3}", f"Vhp{sl}"], [f"ps{yb}"])
            if tk["last"]:
                head = 2 * tk["hp"] + tk["hh"]
                cp(c, "act", ya[:, tk["i"], head * 64:(head + 1) * 64], yo[:, 0:64], [f"ps{yb}"], ["ya"])

        load_hp(0)
        stageA1(0)
        for n in range(NT + 2):
            if n + 1 < NT and tasks[n + 1]["hp"] == tasks[min(n, NT - 1)]["hp"]:
                stageA1(n + 1)
            if n < NT:
                if tasks[n]["hp"] != tasks[n - 1]["hp"] and n > 0:
                    stageA1(n)
                stageA2(n)
            if 0 <= n - 1 < NT:
                stageB(n - 1)
            if 0 <= n - 2 < NT:
                stageC(n - 2)
                tk = tasks[n - 2]
                if tk["i"] == 0 and tk["hh"] == 0 and tk["ci"] == 0 and tk["hp"] + 1 < 4:
                    load_hp(tk["hp"] + 1)
        yat = [A4.sb(f"yat{i}", [128, 4, 128], BF16) for i in range(2)]
        yaT_v = yaT_scr.rearrange("(k p) t -> p k t", p=128)
        for i in range(32):
            pb = i % 2
            for t in range(4):
                c.p.op("pe", lambda e, t=t, i=i, pb=pb: e.transpose(out=c.PB[pb][:, t * 128:(t + 1) * 128],
                                                                     in_=ya[:, i, t * 128:(t + 1) * 128], identity=c.ident[:]),
                       reads=["ya", "ident"], writes=[f"pb{pb}"])
            cp(c, "act" if i % 2 == 0 else "dve", yat[pb][:].rearrange("p a b -> p (a b)"), c.PB[pb][:, 0:512], [f"pb{pb}"], [f"yat{pb}"])
            dma(c, yaT_v[:, :, i * 128:(i + 1) * 128], yat[pb][:], [f"yat{pb}"], ["yaT_scr"], f"yao{pb}")
        p.barrier()
        A4.close()

        A6 = Alloc(nc)
        set_wst(A6, 2)
        Wg = load_weight(c, A6, "Wg", w_gate, D, [(0, 2048)], scale_cols=G_MIX)
        Wus = load_weight(c, A6, "Wus", w_up_ssm, 512, [(0, D)])
        Wua = load_weight(c, A6, "Wua", w_up_attn, 512, [(0, D)])
        Wo = load_weight(c, A6, "Wo", w_out, D, [(0, D)])
        SB = 512
        nb = alloc_norm_bufs(A6, "d", SB, nslot=1)
        xs = [A6.sb(f"xb{i}", [128, KT, SB], F32) for i in range(2)]
        ysl = [A6.sb(f"ysl{i}", [128, 4, SB], BF16) for i in range(2)]
        yal = [A6.sb(f"yal{i}", [128, 4, SB], BF16) for i in range(2)]
        sgs = [A6.sb(f"sgs{i}", [128, 2, SB], F32) for i in range(2)]
        m1 = [A6.sb(f"m1{i}", [128, SB], F32) for i in range(2)]
        m2 = [A6.sb(f"m2{i}", [128, SB], F32) for i in range(2)]
        mg = A6.sb("mg", [128, KT, SB], BF16)
        hst = A6.sb("hst", [128, KT, SB], F32)
        hT_v = hT_scr.rearrange("(k p) t -> p k t", p=128)
        nsl = TO // SB

        def ld3b(s):
            sl = s % 2
            dma(c, xs[sl][:], xTo_v[:, :, s * SB:(s + 1) * SB], [], [f"xb{sl}"], f"xs{sl}")
            dma(c, ysl[sl][:], ysT_v[:, :, s * SB:(s + 1) * SB], ["ysT_scr"], [f"ysl{sl}"], f"ysl{sl}")
            dma(c, yal[sl][:], yaT_v[:, :, s * SB:(s + 1) * SB], ["yaT_scr"], [f"yal{sl}"], f"yal{sl}")
        ld3b(0)
        it = 0
        for s in range(nsl):
            slot = s % 2
            if s + 1 < nsl:
                ld3b(s + 1)
            uk = norm_slab(c, xs[slot], f"xb{slot}", SB, nb, 0)
            uT = nb["uT"][0]
            for j in range(8):
                b2 = it % 2
                it += 1
                banks = [ps_next() for _ in range(4)]
                pgs, pga, pus, pua = (c.PS[b_] for b_ in banks)
                for k in range(KT):
                    mm(c, pgs[:, :], Wg[:, k, j * 128:(j + 1) * 128], uT[:, k, :], k == 0, k == KT - 1, ["Wg", uk], [f"ps{banks[0]}"])
                for k in range(KT):
                    mm(c, pga[:, :], Wg[:, k, 1024 + j * 128:1024 + (j + 1) * 128], uT[:, k, :], k == 0, k == KT - 1,
                       ["Wg", uk], [f"ps{banks[1]}"])
                for t in range(4):
                    mm(c, pus[:, :], Wus[:, t, j * 128:(j + 1) * 128], ysl[slot][:, t, :], t == 0, t == 3,
                       ["Wus", f"ysl{slot}"], [f"ps{banks[2]}"])
                for t in range(4):
                    mm(c, pua[:, :], Wua[:, t, j * 128:(j + 1) * 128], yal[slot][:, t, :], t == 0, t == 3,
                       ["Wua", f"yal{slot}"], [f"ps{banks[3]}"])
                act(c, sgs[b2][:, 0, :], pgs[:, :], AF.Sigmoid, [f"ps{banks[0]}", "vecs"], [f"sgs{b2}a"],
                    bias=c.vecs[:, B_GATE + j:B_GATE + j + 1])
                act(c, sgs[b2][:, 1, :], pga[:, :], AF.Sigmoid, [f"ps{banks[1]}", "vecs"], [f"sgs{b2}b"],
                    bias=c.vecs[:, B_GATE + 8 + j:B_GATE + 8 + j + 1])
                tt(c, m1[b2][:], sgs[b2][:, 0, :], pus[:, :], ALU.mult, [f"sgs{b2}a", f"ps{banks[2]}"], [f"m1{b2}"])
                tt(c, m2[b2][:], sgs[b2][:, 1, :], pua[:, :], ALU.mult, [f"sgs{b2}b", f"ps{banks[3]}"], [f"m2{b2}"])
                tt(c, mg[:, j, :], m1[b2][:], m2[b2][:], ALU.add, [f"m1{b2}", f"m2{b2}"], [f"mg{j}"])
            for j in range(8):
                bo = ps_next()
                po = c.PS[bo]
                for k in range(KT):
                    mm(c, po[:, :], Wo[:, k, j * 128:(j + 1) * 128], mg[:, k, :], k == 0, k == KT - 1,
                       ["Wo", f"mg{k}"], [f"ps{bo}"])
                tt(c, hst[:, j, :], xs[slot][:, j, :], po[:, :], ALU.add, [f"xb{slot}", f"ps{bo}"], ["hst"])
            dma(c, hT_v[:, :, s * SB:(s + 1) * SB], hst[:], ["hst"], ["hT_scr"], "ho0")
        p.barrier()
        A6.close()

        A7 = Alloc(nc)
        W1 = A7.sb("W1", [128, 8, 4096], BF16)
        W2 = A7.sb("W2", [128, 32, 1024], BF16)
        A7s = Alloc(nc)
        set_wst(A7s, 2)
        load_weight(c, None, "W1", w_ff1, D, [(0, 4096)], scale_cols=G_MLP, wsb=W1)
        load_weight(c, None, "W2", w_ff2, 4096, [(0, D)], wsb=W2)
        p.barrier()
        A7s.close()
        FW = 512
        nb = alloc_norm_bufs(A7, "e", FW, nslot=1)
        hsb = [A7.sb(f"hs{i}", [128, KT, FW], F32) for i in range(2)]
        hid = A7.sb("hid", [128, 16, FW], BF16)
        sq4 = [A7.sb(f"sq4{i}", [128, 512], F32) for i in range(2)]
        yT_v = yT.rearrange("(k p) t -> p k t", p=128)
        nsl = TO // FW
        it = 0

        def ld_h(s):
            dma(c, hsb[s % 2][:], hT_v[:, :, s * FW:(s + 1) * FW], ["hT_scr"], [f"hs{s % 2}"], f"xs{s % 2}")
        ld_h(0)
        for s in range(nsl):
            hs_ = hsb[s % 2]
            hk = f"hs{s % 2}"
            if s + 1 < nsl:
                ld_h(s + 1)
            uk = norm_slab(c, hs_, hk, FW, nb, 0)
            hn = nb["uT"][0]
            for half in range(2):
                for b in range(16):
                    jj = half * 16 + b
                    b2 = it % 2
                    it += 1
                    bank = ps_next()
                    ps = c.PS[bank]
                    for k in range(KT):
                        mm(c, ps[:, :], W1[:, k, jj * 128:(jj + 1) * 128], hn[:, k, :], k == 0, k == KT - 1,
                           ["W1", uk], [f"ps{bank}"])
                    act(c, sq4[b2][:], ps[:, :], AF.Square, [f"ps{bank}"], [f"sq4{b2}"])
                    stt(c, hid[:, b, :], ps[:, :], 0.0, sq4[b2][:], ALU.is_gt, ALU.mult, [f"ps{bank}", f"sq4{b2}"], [f"hid{b}"])
                for j in range(8):
                    bank = ps_next()
                    ps = c.PS[bank]
                    for kk in range(16):
                        mm(c, ps[:, :], W2[:, half * 16 + kk, j * 128:(j + 1) * 128], hid[:, kk, :], kk == 0, kk == 15,
                           ["W2", f"hid{kk}"], [f"ps{bank}"])
                    tt(c, hs_[:, j, :], hs_[:, j, :], ps[:, :], ALU.add, [f"{hk}_{j}", hk, f"ps{bank}"], [f"{hk}_{j}"])
            sq = nb["sq"][0]
            rt = nb["rt"][0]
            rs = nb["rs"][0]
            hkeys = [f"{hk}_{j}" for j in range(8)]
            act(c, sq[:], hs_[:], AF.Square, hkeys + [hk, uk], ["esq0"])
            bank = ps_next()
            ps = c.PS[bank]
            for k in range(KT):
                mm(c, ps[:, 0:FW], c.ones[:], sq[:, k, :], k == 0, k == KT - 1, ["esq0", "ones"], [f"ps{bank}"])
            act(c, rt[:], ps[:, 0:FW], AF.Sqrt, [f"ps{bank}"], ["ert0"], scale=1.0 / D, bias=c.epsb[:, 0:1])
            c.p.op("dve", lambda e, rs=rs, rt=rt: e.reciprocal(out=rs[:], in_=rt[:]), reads=["ert0"], writes=["ers0"])
            for j in range(8):
                stt(c, hs_[:, j, :], hs_[:, j, :], c.vecs[:, G_FIN + j:G_FIN + j + 1], rs[:], ALU.mult, ALU.mult,
                    [f"{hk}_{j}", "ers0", "vecs"], [f"{hk}_{j}"])
            dma(c, yT_v[:, :, s * FW:(s + 1) * FW], hs_[:], hkeys + [hk], ["yT"], f"out{s % 2}")
        p.barrier()
        A7.close()
        G.close()
    return nc


def _s5_tables(A_re, A_im, log_dt, B_re, B_im, C_re, C_im):
    f = np.float32
    sp_par = np.zeros((128, 3, 16, 32), f)
    sp_C = np.zeros((128, 2, 16, 32), f)
    sp_B = np.zeros((128, 2, 16, 32), f)
    for gi in range(2):
        rows = slice(64 * gi, 64 * gi + 64)
        for pr in range(16):
            g = 2 * pr + gi
            sp_par[rows, 0, pr, :] = A_re[g][:, None]
            sp_par[rows, 1, pr, :] = A_im[g][:, None]
            sp_par[rows, 2, pr, :] = log_dt[g]
            cs = slice(16 * gi, 16 * gi + 16)
            sp_C[rows, 0, pr, cs] = C_re[g].T
            sp_C[rows, 1, pr, cs] = C_im[g].T
            sp_B[rows, 0, pr, cs] = B_re[g]
            sp_B[rows, 1, pr, cs] = B_im[g]
    ch_par = np.zeros((128, 3, 4, 2, 128), f)
    ch_B = np.zeros((128, 2, 4, 2, 128), f)
    for t in range(4):
        for v in range(2):
            for q in range(4):
                pr = 4 * t + (q if v == 0 else 3)
                rows_q = slice(32 * q, 32 * q + 32)
                for gi2 in range(2):
                    g2 = 2 * pr + gi2
                    cs = slice(64 * gi2, 64 * gi2 + 64)
                    ch_par[rows_q, 0, t, v, cs] = A_re[g2][None, :]
                    ch_par[rows_q, 1, t, v, cs] = A_im[g2][None, :]
                    ch_par[rows_q, 2, t, v, cs] = log_dt[g2]
                    if (v == 0 and q < 3) or (v == 1 and q == 3):
                        rows = slice(32 * q + 16 * gi2, 32 * q + 16 * gi2 + 16)
                        ch_B[rows, 0, t, v, cs] = B_re[g2].T
                        ch_B[rows, 1, t, v, cs] = B_im[g2].T
    return (sp_par.reshape(128, 3, 512), sp_C.reshape(128, 2, 512), sp_B.reshape(128, 2, 512),
            ch_par.reshape(128, 3, 1024), ch_B.reshape(128, 2, 1024))


def make_in_maps(x, norm_mix, w_in, A_re, A_im, log_dt, B_re, B_im, C_re, C_im, D_skip, w_glu, b_glu,
                 w_up_ssm, w_up_attn, w_gate, b_gate, w_out, norm_mlp, w_ff1, w_ff2, norm_final, cores=range(8)):
    f = np.float32
    x = np.asarray(x, f)
    tile = lambda v: np.asarray(v, f).reshape(-1, 128).T
    vecs = np.concatenate([tile(norm_mix[0]), tile(norm_mlp[0]), tile(norm_final), tile(b_gate[0]),
                           tile(D_skip[0]), tile(b_glu[0])], axis=1)
    vecs = np.ascontiguousarray(vecs, f)
    assert vecs.shape == (128, 48)
    sp_par, sp_C, sp_B, ch_par, ch_B = _s5_tables(np.asarray(A_re[0], f), np.asarray(A_im[0], f), np.asarray(log_dt[0], f),
                                                  np.asarray(B_re[0], f), np.asarray(B_im[0], f),
                                                  np.asarray(C_re[0], f), np.asarray(C_im[0], f))
    ql = np.arange(128)
    mask = (ql[None, :] + ql[:, None] <= 127).astype(f)
    shared = dict(w_in=np.ascontiguousarray(w_in[0], f), w_gate=np.ascontiguousarray(w_gate[0], f),
                  w_glu=np.ascontiguousarray(w_glu[0], f), w_up_ssm=np.ascontiguousarray(w_up_ssm[0], f),
                  w_up_attn=np.ascontiguousarray(w_up_attn[0], f), w_out=np.ascontiguousarray(w_out[0], f),
                  w_ff1=np.ascontiguousarray(w_ff1[0], f), w_ff2=np.ascontiguousarray(w_ff2[0], f),
                  vecs=vecs, sp_par=sp_par, sp_C=sp_C, sp_B=sp_B, ch_par=ch_par, ch_B=ch_B, mask=mask)
    maps = []
    for core in cores:
        b, h = core // 2, core % 2
        xb = x[b]
        r = np.arange(TA)
        tok = 8191 + 128 * h - r
        val = (tok >= 0) & (tok < S)
        xr = np.zeros((TA, D), f)
        xr[val] = xb[tok[val]]
        tokn = r - 128 * (1 - h)
        valn = (tokn >= 0) & (tokn < S)
        xn = np.zeros((TA, D), f)
        xn[valn] = xb[tokn[valn]]
        own = xb.reshape(32, 2, 128, D)[:, h].reshape(TO, D)
        m = dict(shared)
        m["xTr"] = np.ascontiguousarray(xr.T)
        m["xTn"] = np.ascontiguousarray(xn.T)
        m["xTo"] = np.ascontiguousarray(own.T)
        maps.append(m)
    return maps


def kernel(**inputs):
    nc = build()
    maps = make_in_maps(**inputs)
    res = run_bass_kernel_spmd(nc, maps, core_ids=list(range(8)))
    out = np.zeros((4, S, D), np.float32)
    ov = out.reshape(4, 32, 2, 128, D)
    for core in range(8):
        b, h = core // 2, core % 2
        yT = np.asarray(res.results[core]["yT"], np.float32)
        ov[b, :, h] = yT.T.reshape(32, 128, D)
    return out
```

```python
import contextlib
import math
import numpy as np
import concourse.bass as bass
import concourse.mybir as mybir
from concourse.bass_utils import run_bass_kernel_spmd

F32 = mybir.dt.float32
BF16 = mybir.dt.bfloat16
I32 = mybir.dt.int32
AF = mybir.ActivationFunctionType
ALU = mybir.AluOpType

D = 1024
KT = 8
S = 8192
TA = 8448
TO = 4096
NCH = TA // 16
SL = 256
EPS = 1e-6
TWO_PI = 2.0 * math.pi
GELU_C = math.sqrt(2.0 / math.pi)


class Prog:
    ENGS = ("pe", "act", "dve", "pool", "sp")
    EPOCH = 30000
    NEP = 4
    NDMA = 72

    def __init__(self, nc, st):
        self.nc = nc
        self.ops = {e: [] for e in self.ENGS}
        self.cnt = {e: 0 for e in self.ENGS}
        self.last_w = {}
        self.readers = {}
        self.waited = {e: {} for e in self.ENGS}
        self.streams = {}
        self.strict_same = False
        self.sems = {}
        for e in self.ENGS[:4]:
            for ep in range(self.NEP):
                self.sems[("eng", e, ep)] = st.enter_context(nc.semaphore(f"s_{e}_{ep}"))
        self.dma_pool = [st.enter_context(nc.semaphore(f"s_dma_{i}")) for i in range(self.NDMA)]
        self.block = st.enter_context(nc.Block())
        self.engmap = {"pe": self.block.tensor, "act": self.block.scalar, "dve": self.block.vector,
                       "pool": self.block.gpsimd, "sp": self.block.sync}

    def _sem(self, key):
        if key not in self.sems:
            self.sems[key] = self.dma_pool.pop()
        return self.sems[key]

    def _need(self, eng, tok, waits):
        if tok is None:
            return
        key, val, peng = tok
        if peng == eng and key[0] != "dma" and not self.strict_same:
            return
        if self.waited[eng].get(key, 0) >= val:
            return
        self.waited[eng][key] = val
        waits[key] = max(waits.get(key, 0), val)

    def op(self, eng, fn, reads=(), writes=(), stream=None):
        waits = {}
        for r in reads:
            self._need(eng, self.last_w.get(r), waits)
        for w in writes:
            self._need(eng, self.last_w.get(w), waits)
            for t in self.readers.get(w, ()):
                self._need(eng, t, waits)
        if eng == "sp":
            assert stream is not None
            self.streams[stream] = self.streams.get(stream, 0) + 1
            tok = (("dma", stream), 16 * self.streams[stream], "sp")
        else:
            self.cnt[eng] += 1
            g = self.cnt[eng]
            ep = (g - 1) // self.EPOCH
            assert ep < self.NEP
            tok = (("eng", eng, ep), g - ep * self.EPOCH, eng)
        for r in reads:
            self.readers.setdefault(r, []).append(tok)
        for w in writes:
            self.last_w[w] = tok
            self.readers[w] = []
        self.ops[eng].append((fn, list(waits.items()), tok))
        return tok

    def barrier(self):
        toks = []
        for e in self.ENGS[:4]:
            g = self.cnt[e]
            if g:
                ep = (g - 1) // self.EPOCH
                toks.append((("eng", e, ep), g - ep * self.EPOCH, e))
        for s, n in self.streams.items():
            toks.append((("dma", s), 16 * n, "sp"))
        for e in self.ENGS:
            waits = {}
            for t in toks:
                self._need(e, t, waits)
            self.ops[e].append((None, list(waits.items()), None))
        self.last_w.clear()
        self.readers.clear()
        self.flush()

    def flush(self):
        for e in self.ENGS:
            ops = self.ops[e]
            if not ops:
                continue

            def body(eng, ops=ops):
                for fn, waits, tok in ops:
                    for k, v in waits:
                        eng.wait_ge(self._sem(k), v)
                    if fn is None:
                        continue
                    ins = fn(eng)
                    ins.then_inc(self._sem(tok[0]), 16 if tok[0][0] == "dma" else 1)
            self.engmap[e](body)
            self.ops[e] = []


class Ctx:
    pass


def _v3(ap, j):
    return ap.rearrange("p (c j) -> p c j", j=j)


def dma(c, out, in_, reads, writes, stream):
    return c.p.op("sp", lambda e: e.dma_start(out=out, in_=in_), reads=reads, writes=writes, stream=stream)


def act(c, out, in_, func, reads, writes, scale=None, bias=None):
    kw = {}
    if scale is not None:
        kw["scale"] = scale
    if bias is not None:
        kw["bias"] = bias
    return c.p.op("act", lambda e: e.activation(out=out, in_=in_, func=func, **kw), reads=reads, writes=writes)


def tt(c, out, a, b, op, reads, writes, eng="dve"):
    return c.p.op(eng, lambda e: e.tensor_tensor(out=out, in0=a, in1=b, op=op), reads=reads, writes=writes)


def ts(c, out, a, s1, op0, reads, writes, s2=None, op1=None, eng="dve"):
    if op1 is None:
        return c.p.op(eng, lambda e: e.tensor_scalar(out=out, in0=a, scalar1=s1, scalar2=None, op0=op0),
                      reads=reads, writes=writes)
    return c.p.op(eng, lambda e: e.tensor_scalar(out=out, in0=a, scalar1=s1, scalar2=s2, op0=op0, op1=op1),
                  reads=reads, writes=writes)


def stt(c, out, a, scalar, b, op0, op1, reads, writes):
    return c.p.op("dve", lambda e: e.scalar_tensor_tensor(out=out, in0=a, scalar=scalar, in1=b, op0=op0, op1=op1),
                  reads=reads, writes=writes)


def cp(c, eng, out, in_, reads, writes):
    if eng == "act":
        return act(c, out, in_, AF.Copy, reads, writes)
    return c.p.op(eng, lambda e: e.tensor_copy(out=out, in_=in_), reads=reads, writes=writes)


def mm(c, out, lhsT, rhs, start, stop, reads, writes):
    return c.p.op("pe", lambda e: e.matmul(out, lhsT=lhsT, rhs=rhs, start=start, stop=stop), reads=reads, writes=writes)


def memset(c, ap, val, writes, eng="pool"):
    return c.p.op(eng, lambda e: e.memset(ap, val), writes=writes)


class Alloc:
    def __init__(self, nc):
        self.nc = nc
        self.st = contextlib.ExitStack()
        self.n = 0

    _uid = [0]

    def sb(self, name, shape, dt):
        Alloc._uid[0] += 1
        return self.st.enter_context(self.nc.sbuf_tensor(f"{name}_u{Alloc._uid[0]}", list(shape), dt))

    def close(self):
        self.st.close()


def load_weight(c, al, name, w_dram, kd, col_ranges, scale_cols=None, wsb=None):
    ncols = sum(b - a for a, b in col_ranges)
    nk = kd // 128
    if wsb is None:
        wsb = al.sb(name, [128, nk, ncols], BF16)
    i = 0
    for kt in range(nk):
        o = 0
        for (a, b) in col_ranges:
            for a2 in range(a, b, 1024):
                b2 = min(b, a2 + 1024)
                w = b2 - a2
                slot = c.wst_i % len(c.wst)
                c.wst_i += 1
                stg = c.wst[slot]
                dma(c, stg[:, 0:w], w_dram[kt * 128:(kt + 1) * 128, a2:b2], [], [f"wst{slot}"], f"wst{slot}")
                dst = wsb[:, kt, o:o + w]
                if scale_cols is not None:
                    sc = c.vecs[:, scale_cols + kt:scale_cols + kt + 1]
                    if i % 2 == 0:
                        act(c, dst, stg[:, 0:w], AF.Copy, [f"wst{slot}", "vecs"], [name], scale=sc)
                    else:
                        ts(c, dst, stg[:, 0:w], sc, ALU.mult, [f"wst{slot}", "vecs"], [name])
                else:
                    cp(c, "act" if i % 2 == 0 else "dve", dst, stg[:, 0:w], [f"wst{slot}"], [name])
                i += 1
                o += w
    return wsb


def norm_slab(c, xs, xkey, W, nb, slot):
    sq = nb["sq"][slot]
    rt = nb["rt"][slot]
    rs = nb["rs"][slot]
    uT = nb["uT"][slot]
    pfx = nb["pfx"]
    act(c, sq[:, :, 0:W], xs[:, :, 0:W], AF.Square, [xkey], [f"{pfx}sq{slot}"])
    bank = c.ps_next()
    ps = c.PS[bank]
    for k in range(KT):
        mm(c, ps[:, 0:W], c.ones[:], sq[:, k, 0:W], k == 0, k == KT - 1, [f"{pfx}sq{slot}", "ones"], [f"ps{bank}"])
    act(c, rt[:, 0:W], ps[:, 0:W], AF.Sqrt, [f"ps{bank}"], [f"{pfx}rt{slot}"], scale=1.0 / D, bias=c.epsb[:, 0:1])
    c.p.op("dve", lambda e: e.reciprocal(out=rs[:, 0:W], in_=rt[:, 0:W]), reads=[f"{pfx}rt{slot}"], writes=[f"{pfx}rs{slot}"])
    tt(c, uT[:, :, 0:W], xs[:, :, 0:W], rs[:, 0:W].unsqueeze(1).to_broadcast([128, KT, W]), ALU.mult,
       [xkey, f"{pfx}rs{slot}"], [f"{pfx}uT{slot}"])
    return f"{pfx}uT{slot}"


def slab_pipeline(n, load_fn, norm_fn, proj_a, proj_b):
    load_fn(0)
    if n > 1:
        load_fn(1)
    norm_fn(0)
    for s in range(n):
        if s + 2 < n:
            load_fn(s + 2)
        proj_a(s)
        if s + 1 < n:
            norm_fn(s + 1)
        proj_b(s)


def alloc_norm_bufs(al, pfx, W, nslot=2):
    return {"pfx": pfx,
            "sq": [al.sb(f"{pfx}sq{i}", [128, KT, W], BF16) for i in range(nslot)],
            "rt": [al.sb(f"{pfx}rt{i}", [128, W], F32) for i in range(nslot)],
            "rs": [al.sb(f"{pfx}rs{i}", [128, W], F32) for i in range(nslot)],
            "uT": [al.sb(f"{pfx}uT{i}", [128, KT, W], BF16) for i in range(nslot)]}


def gen_abeta(c, al, par, pk, pfx, F=512):
    T = lambda n: al.sb(f"{pfx}_{n}", [128, F], F32)
    dt, lr, li, er, t1, t2, sn, cs, ar, ai, br, bi = (T(n) for n in
                                                        ("dt", "lr", "li", "er", "t1", "t2", "sn", "cs", "ar", "ai", "br", "bi"))
    ti = al.sb(f"{pfx}_ti", [128, F], I32)
    K = lambda n: f"{pfx}_{n}"
    Are, Aim, Ldt = par[:, 0, :], par[:, 1, :], par[:, 2, :]
    act(c, dt[:], Ldt, AF.Exp, [pk], [K("dt")])
    tt(c, lr[:], Are, dt[:], ALU.mult, [pk, K("dt")], [K("lr")])
    tt(c, li[:], Aim, dt[:], ALU.mult, [pk, K("dt")], [K("li")])
    act(c, er[:], lr[:], AF.Exp, [K("lr")], [K("er")])

    def sinshift(out, okey, shift):
        ts(c, t1[:], li[:], 1.0 / TWO_PI, ALU.mult, [K("li")], [K("t1")], s2=shift / TWO_PI, op1=ALU.add)
        cp(c, "dve", ti[:], t1[:], [K("t1")], [K("ti")])
        cp(c, "dve", t2[:], ti[:], [K("ti")], [K("t2")])
        tt(c, t1[:], t1[:], t2[:], ALU.subtract, [K("t1"), K("t2")], [K("t1")])
        ts(c, t1[:], t1[:], TWO_PI, ALU.mult, [K("t1")], [K("t1")], s2=math.pi, op1=ALU.min)
        ts(c, t1[:], t1[:], -math.pi, ALU.max, [K("t1")], [K("t1")])
        act(c, out[:], t1[:], AF.Sin, [K("t1")], [okey])

    sinshift(sn, K("sn"), 0.0)
    sinshift(cs, K("cs"), math.pi / 2)
    tt(c, ar[:], er[:], cs[:], ALU.mult, [K("er"), K("cs")], [K("ar")])
    tt(c, ai[:], er[:], sn[:], ALU.mult, [K("er"), K("sn")], [K("ai")])
    den, am1 = dt, lr
    tt(c, t1[:], Are, Are, ALU.mult, [pk], [K("t1")])
    tt(c, t2[:], Aim, Aim, ALU.mult, [pk], [K("t2")])
    tt(c, den[:], t1[:], t2[:], ALU.add, [K("t1"), K("t2")], [K("dt")])
    c.p.op("dve", lambda e: e.reciprocal(out=den[:], in_=den[:]), reads=[K("dt")], writes=[K("dt")])
    ts(c, am1[:], ar[:], -1.0, ALU.add, [K("ar")], [K("lr")])
    tt(c, t1[:], am1[:], Are, ALU.mult, [K("lr"), pk], [K("t1")])
    tt(c, t2[:], ai[:], Aim, ALU.mult, [K("ai"), pk], [K("t2")])
    tt(c, t1[:], t1[:], t2[:], ALU.add, [K("t1"), K("t2")], [K("t1")])
    tt(c, br[:], t1[:], den[:], ALU.mult, [K("t1"), K("dt")], [K("br")])
    tt(c, t1[:], ai[:], Are, ALU.mult, [K("ai"), pk], [K("t1")])
    tt(c, t2[:], am1[:], Aim, ALU.mult, [K("lr"), pk], [K("t2")])
    tt(c, t1[:], t1[:], t2[:], ALU.subtract, [K("t1"), K("t2")], [K("t1")])
    tt(c, bi[:], t1[:], den[:], ALU.mult, [K("t1"), K("dt")], [K("bi")])
    return dict(ar=ar, ai=ai, br=br, bi=bi, kar=K("ar"), kai=K("ai"), kbr=K("br"), kbi=K("bi"),
                tmp=[(t1, K("t1")), (t2, K("t2")), (sn, K("sn")), (cs, K("cs")), (er, K("er")), (li, K("li"))])


def cmul(c, outr, outi, kor, koi, xr, xi, kxr, kxi, yr, yi, kyr, kyi, tmps):
    (t1, k1), (t2, k2), (t3, k3), (t4, k4) = tmps[:4]
    tt(c, t1[:], xr, yr, ALU.mult, [kxr, kyr], [k1])
    tt(c, t2[:], xi, yi, ALU.mult, [kxi, kyi], [k2])
    tt(c, t3[:], xr, yi, ALU.mult, [kxr, kyi], [k3])
    tt(c, t4[:], xi, yr, ALU.mult, [kxi, kyr], [k4])
    tt(c, outr, t1[:], t2[:], ALU.subtract, [k1, k2], [kor])
    tt(c, outi, t3[:], t4[:], ALU.add, [k3, k4], [koi])


def build(debug=False):
    nc = bass.Bass("TRN2", target_bir_lowering=False)
    din = lambda n, s, d=F32: nc.dram_tensor(n, list(s), d, kind="ExternalInput").ap()
    scr_kind = "ExternalOutput"
    dscr = lambda n, s, d: nc.dram_tensor(n, list(s), d, kind=scr_kind).ap()

    xTr = din("xTr", [D, TA])
    xTn = din("xTn", [D, TA])
    xTo = din("xTo", [D, TO])
    w_in = din("w_in", [D, 2048])
    w_gate = din("w_gate", [D, 2048])
    w_glu = din("w_glu", [512, 512])
    w_up_ssm = din("w_up_ssm", [512, D])
    w_up_attn = din("w_up_attn", [512, D])
    w_out = din("w_out", [D, D])
    w_ff1 = din("w_ff1", [D, 4096])
    w_ff2 = din("w_ff2", [4096, D])
    vecs_d = din("vecs", [128, 48])
    sp_par_d = din("sp_par", [128, 3, 512])
    sp_C_d = din("sp_C", [128, 2, 512])
    sp_B_d = din("sp_B", [128, 2, 512])
    ch_par_d = din("ch_par", [128, 3, 1024])
    ch_B_d = din("ch_B", [128, 2, 1024])
    mask_d = din("mask", [128, 128])
    yT = nc.dram_tensor("yT", [D, TO], F32, kind="ExternalOutput").ap()

    kT_scr = dscr("kT_scr", [512, TA], BF16)
    v_scr = dscr("v_scr", [4, 128, 66, 128], BF16)
    qT_scr = dscr("qT_scr", [512, TO], BF16)
    ysT_scr = dscr("ysT_scr", [512, TO], BF16)
    yaT_scr = dscr("yaT_scr", [512, TO], BF16)
    hT_scr = dscr("hT_scr", [D, TO], F32)

    c = Ctx()
    c.nc = nc
    with contextlib.ExitStack() as st:
        c.p = Prog(nc, st)
        p = c.p
        c.PS = [st.enter_context(nc.psum_tensor(f"psf{i}", [128, 512], F32)) for i in range(6)]
        c.PB = [st.enter_context(nc.psum_tensor(f"psb{i}", [128, 1024], BF16)) for i in range(2)]
        c.ps_i = 0

        def ps_next():
            c.ps_i = (c.ps_i + 1) % 6
            return c.ps_i
        c.ps_next = ps_next

        G = Alloc(nc)
        c.vecs = G.sb("vecs", [128, 48], F32)
        c.ones = G.sb("ones", [128, 128], BF16)
        c.ident = G.sb("ident", [128, 128], BF16)
        c.epsb = G.sb("epsb", [128, 1], F32)
        c.mask = G.sb("maskt", [128, 128], F32)
        c.wst_i = 0

        def set_wst(al, n):
            c.wst = [al.sb(f"wst{i}", [128, 1024], F32) for i in range(n)]
        dma(c, c.vecs[:], vecs_d[:, :], [], ["vecs"], "c_vecs")
        dma(c, c.mask[:], mask_d[:, :], [], ["mask"], "c_mask")
        memset(c, c.ones[:], 1.0, ["ones"])
        memset(c, c.epsb[:], EPS, ["epsb"])
        memset(c, c.ident[:], 1.0, ["ident"])
        p.op("pool", lambda e: e.affine_select(out=c.ident[:], in_=c.ident[:], pattern=[[-1, 128]],
                                               compare_op=ALU.is_equal, fill=0.0, base=0, channel_multiplier=1),
             reads=["ident"], writes=["ident"])
        G_MIX, G_MLP, G_FIN, B_GATE, D_SKIP, B_GLU = 0, 8, 16, 24, 40, 44

        A5 = Alloc(nc)
        Xown = [A5.sb(f"Xown{ri}", [128, 16, 256], BF16) for ri in range(2)]
        ksr = A5.sb("ksr", [128, 10, 16], F32)
        ksi = A5.sb("ksi", [128, 10, 16], F32)
        ksni = A5.sb("ksni", [128, 10, 16], F32)
        pwr = A5.sb("pwr", [128, 8, 16], F32)
        pwi = A5.sb("pwi", [128, 8, 16], F32)
        pwn = A5.sb("pwn", [128, 8, 16], F32)
        memset(c, pwr[:].rearrange("p k n -> p (k n)"), 0.0, ["pwr"])
        memset(c, pwi[:].rearrange("p k n -> p (k n)"), 0.0, ["pwi"])

        def sp_gen(want_ks, Cm=None, Kt=None, CmZ=None):
            A0 = Alloc(nc)
            sp_par = A0.sb("sp_par", [128, 3, 512], F32)
            dma(c, sp_par[:], sp_par_d[:, :, :], [], ["sp_par"], "c_sp_par")
            g = gen_abeta(c, A0, sp_par, "sp_par", "sp")
            tmps = g["tmp"]
            if Cm is not None:
                sp_C = A0.sb("sp_C", [128, 2, 512], F32)
                sp_B = A0.sb("sp_B", [128, 2, 512], F32)
                dma(c, sp_C[:], sp_C_d[:, :, :], [], ["sp_C"], "c_sp_C")
                dma(c, sp_B[:], sp_B_d[:, :, :], [], ["sp_B"], "c_sp_B")
                BTb = [A0.sb(f"BTb{ri}", [128, 512], BF16) for ri in range(2)]
                cmul(c, BTb[0][:], BTb[1][:], "BTb0", "BTb1", g["br"][:], g["bi"][:], g["kbr"], g["kbi"],
                     sp_B[:, 0, :], sp_B[:, 1, :], "sp_B", "sp_B", tmps)
                curr = A0.sb("curr", [128, 512], F32)
                curi = A0.sb("curi", [128, 512], F32)
                cp(c, "act", curr[:], sp_C[:, 0, :], ["sp_C"], ["curr"])
                cp(c, "act", curi[:], sp_C[:, 1, :], ["sp_C"], ["curi"])
                for tau in range(17):
                    cp(c, "act", Cm[0][:, tau, :], curr[:], ["curr"], ["Cm0"])
                    act(c, Cm[1][:, tau, :], curi[:], AF.Copy, ["curi"], ["Cm1"], scale=-1.0)
                    if tau < 16:
                        cmul(c, curr[:], curi[:], "curr", "curi", curr[:], curi[:], "curr", "curi",
                             g["ar"][:], g["ai"][:], g["kar"], g["kai"], tmps)
                BTbZ = [A0.sb(f"BTbZ{ri}", [128, 4, 64], BF16) for ri in range(2)]
                for ri in range(2):
                    act(c, BTbZ[ri][:].rearrange("p a b -> p (a b)"), c.ones[:, 0:1].to_broadcast([128, 256]), AF.Copy,
                        ["ones"], [f"BTbZ{ri}"], scale=0.0)
                    act(c, CmZ[ri][:].rearrange("p a b c -> p (a b c)"), c.ones[:, 0:1].to_broadcast([128, 17 * 4 * 64]), AF.Copy,
                        ["ones"], [f"CmZ{ri}"], scale=0.0)
                    for t in range(4):
                        cols = slice((4 * t + 3) * 32, (4 * t + 4) * 32)
                        cp(c, "act", BTbZ[ri][:, t, 32:64], BTb[ri][:, cols], [f"BTb{ri}", f"BTbZ{ri}"], [f"BTbZ{ri}"])
                        cp(c, "act", CmZ[ri][:, :, t, 32:64], Cm[ri][:, :, cols], [f"Cm{ri}", f"CmZ{ri}"], [f"CmZ{ri}"])
                act(c, Kt[:].rearrange("p a b c -> p (a b c)"), c.ones[:, 0:1].to_broadcast([128, 4 * 16 * 128]), AF.Copy,
                    ["ones"], ["Kt"], scale=0.0)
                for t in range(4):
                    bank = ps_next()
                    ps = c.PS[bank]
                    for q in (3, 2, 1, 0):
                        pr = 4 * t + q
                        cols = slice(pr * 32, (pr + 1) * 32)
                        if q == 3:
                            o3 = _v3(ps[64:128, :], 32)
                            l0, l1 = BTbZ[0][:, t, :], BTbZ[1][:, t, :]
                        else:
                            o3 = _v3(ps[32 * q:32 * q + 32, :], 32)
                            l0, l1 = BTb[0][:, cols], BTb[1][:, cols]
                        mm(c, o3, l0, Cm[0][:, 0:16, cols], q != 2, False, ["BTb0", "BTbZ0", "Cm0"], [f"ps{bank}"])
                        mm(c, o3, l1, Cm[1][:, 0:16, cols], False, q == 0, ["BTb1", "BTbZ1", "Cm1"], [f"ps{bank}"])
                    for q in range(4):
                        act(c, Kt[32 * q:32 * q + 32, t, :, 32 * q:32 * q + 32], _v3(ps[32 * q:32 * q + 32, :], 32), AF.Copy,
                            [f"ps{bank}"], ["Kt"])
            if want_ks:
                acr = A0.sb("acr", [128, 16], F32)
                aci = A0.sb("aci", [128, 16], F32)
                q1 = A0.sb("q1", [128, 16], F32)
                q2 = A0.sb("q2", [128, 16], F32)
                q3 = A0.sb("q3", [128, 16], F32)
                cp(c, "dve", acr[:], _v3(g["ar"][:], 32)[:, :, 0], [g["kar"]], ["acr"])
                cp(c, "dve", aci[:], _v3(g["ai"][:], 32)[:, :, 0], [g["kai"]], ["aci"])

                pad_t = A0.sb("pad_t", [128, 8], F32)

                def pad(n=2):
                    for _ in range(n):
                        memset(c, pad_t[:, 0:1], 0.0, ["pad_t"], eng="dve")

                def csq(outr, outi, kor, koi, inr, ini, kir, kii):
                    tt(c, q1[:], inr, inr, ALU.mult, [kir], ["q1"])
                    tt(c, q2[:], ini, ini, ALU.mult, [kii], ["q2"])
                    tt(c, q3[:], inr, ini, ALU.mult, [kir, kii], ["q3"])
                    pad(2)
                    tt(c, outr, q1[:], q2[:], ALU.subtract, ["q1", "q2"], [kor])
                    ts(c, outi, q3[:], 2.0, ALU.mult, ["q3"], [koi])
                    pad(2)

                for _ in range(4):
                    csq(acr[:], aci[:], "acr", "aci", acr[:], aci[:], "acr", "aci")
                cp(c, "dve", ksr[:, 0, :], acr[:], ["acr"], ["ksr"])
                cp(c, "dve", ksi[:, 0, :], aci[:], ["aci"], ["ksi"])
                for k in range(1, 10):
                    csq(ksr[:, k, :], ksi[:, k, :], "ksr", "ksi", ksr[:, k - 1, :], ksi[:, k - 1, :], "ksr", "ksi")
                ts(c, ksni[:].rearrange("p k n -> p (k n)"), ksi[:].rearrange("p k n -> p (k n)"), -1.0, ALU.mult, ["ksi"], ["ksni"])
                T_ = A0.sb("pwT", [128, 16, 16], F32)
                pad(3)

                def prod_mults(ti, xr, xi, yr, yi, rd):
                    tt(c, T_[:, ti + 0, :], xr, yr, ALU.mult, rd, ["pwT"])
                    tt(c, T_[:, ti + 1, :], xi, yi, ALU.mult, rd, ["pwT"])
                    tt(c, T_[:, ti + 2, :], xr, yi, ALU.mult, rd, ["pwT"])
                    tt(c, T_[:, ti + 3, :], xi, yr, ALU.mult, rd, ["pwT"])

                def prod_fin(ti, m):
                    tt(c, pwr[:, m, :], T_[:, ti + 0, :], T_[:, ti + 1, :], ALU.subtract, ["pwT"], ["pwr"])
                    tt(c, pwi[:, m, :], T_[:, ti + 2, :], T_[:, ti + 3, :], ALU.add, ["pwT"], ["pwi"])

                K_ = lambda k: (ksr[:, k, :], ksi[:, k, :])
                prod_mults(0, *K_(0), *K_(1), ["ksr", "ksi"])
                prod_mults(4, *K_(0), *K_(2), ["ksr", "ksi"])
                prod_mults(8, *K_(1), *K_(2), ["ksr", "ksi"])
                prod_fin(0, 3)
                prod_fin(4, 5)
                prod_fin(8, 6)
                for m, k in ((1, 0), (2, 1), (4, 2)):
                    cp(c, "dve", pwr[:, m, :], ksr[:, k, :], ["ksr"], ["pwr"])
                    cp(c, "dve", pwi[:, m, :], ksi[:, k, :], ["ksi"], ["pwi"])
                prod_mults(12, pwr[:, 3, :], pwi[:, 3, :], *K_(2), ["pwr", "pwi", "ksr", "ksi"])
                pad(3)
                prod_fin(12, 7)
                pad(3)
                ts(c, pwn[:].rearrange("p k n -> p (k n)"), pwi[:].rearrange("p k n -> p (k n)"), -1.0, ALU.mult, ["pwi"], ["pwn"])
                pad(3)
            p.barrier()
            A0.close()

        sp_gen(True)
        if debug:
            dbg = nc.dram_tensor("dbg", [128, 4, 8, 16], F32, kind="ExternalOutput").ap()
            dma(c, dbg[:, 0, :, :], pwr[:], ["pwr"], ["dbg"], "dbg0")
            dma(c, dbg[:, 1, :, :], pwi[:], ["pwi"], ["dbg"], "dbg1")
            dma(c, dbg[:, 2, :, :], ksr[:, 0:8, :], ["ksr"], ["dbg"], "dbg2")
            dma(c, dbg[:, 3, :, :], ksi[:, 0:8, :], ["ksi"], ["dbg"], "dbg3")
        A1 = Alloc(nc)
        Uall = A1.sb("Uall", [128, 4, 16, NCH], BF16)
        A1b = Alloc(nc)
        set_wst(A1b, 2)
        Wssm = load_weight(c, A1b, "Wssm", w_in, D, [(0, 512)], scale_cols=G_MIX)
        SLs = 256
        nb = alloc_norm_bufs(A1b, "a", SLs)
        xs = [A1b.sb(f"xs{i}", [128, KT, SLs], F32) for i in range(2)]
        xTn_v = xTn.rearrange("(k p) t -> p k t", p=128)
        nsl = TA // SLs
        uks = {}

        def ld_s(s):
            sl_ = s % 2
            dma(c, xs[sl_][:], xTn_v[:, :, s * SLs:(s + 1) * SLs], [], [f"xs{sl_}"], f"xs{sl_}")

        def nm_s(s):
            uks[s] = norm_slab(c, xs[s % 2], f"xs{s % 2}", SLs, nb, s % 2)

        def pj_s(s, cts=range(4)):
            uT = nb["uT"][s % 2]
            for ct in cts:
                bank = ps_next()
                ps = c.PS[bank]
                for k in range(KT):
                    mm(c, ps[:, 0:SLs], Wssm[:, k, ct * 128:(ct + 1) * 128], uT[:, k, :], k == 0, k == KT - 1,
                       ["Wssm", uks[s]], [f"ps{bank}"])
                n0, nn = (s * SLs) // 16, SLs // 16
                cp(c, "act" if ct % 2 == 0 else "dve", Uall[:, ct, :, n0:n0 + nn],
                   ps[:, 0:SLs].rearrange("p (n r) -> p r n", r=16), [f"ps{bank}"], ["Uall"])
        slab_pipeline(nsl, ld_s, nm_s, lambda s: pj_s(s, range(0, 2)), lambda s: pj_s(s, range(2, 4)))
        p.barrier()
        A1b.close()

        ABm = Alloc(nc)
        Bm = [ABm.sb(f"Bm{ri}", [128, 16, 1024], BF16) for ri in range(2)]
        for hf in range(2):
            A0 = Alloc(nc)
            cs_ = slice(hf * 512, (hf + 1) * 512)
            ch_par = A0.sb("ch_par", [128, 3, 512], F32)
            ch_B = A0.sb("ch_B", [128, 2, 512], F32)
            dma(c, ch_par[:], ch_par_d[:, :, cs_], [], ["ch_par"], "c_ch_par")
            dma(c, ch_B[:], ch_B_d[:, :, cs_], [], ["ch_B"], "c_ch_B")
            g = gen_abeta(c, A0, ch_par, "ch_par", "ch")
            tmps = g["tmp"]
            curr = A0.sb("curr2", [128, 512], F32)
            curi = A0.sb("curi2", [128, 512], F32)
            cmul(c, curr[:], curi[:], "curr2", "curi2", g["br"][:], g["bi"][:], g["kbr"], g["kbi"],
                 ch_B[:, 0, :], ch_B[:, 1, :], "ch_B", "ch_B", tmps)
            for rho in range(15, -1, -1):
                cp(c, "act", Bm[0][:, rho, cs_], curr[:], ["curr2"], ["Bm0"])
                cp(c, "act", Bm[1][:, rho, cs_], curi[:], ["curi2"], ["Bm1"])
                if rho > 0:
                    cmul(c, curr[:], curi[:], "curr2", "curi2", curr[:], curi[:], "curr2", "curi2",
                         g["ar"][:], g["ai"][:], g["kar"], g["kai"], tmps)
            p.barrier()
            A0.close()

        A1c = Alloc(nc)
        Sst2 = [[A1c.sb(f"S{ri}_{b_}", [128, 4, NCH], F32) for ri in range(2)] for b_ in range(2)]
        kb = [[A1c.sb(f"kb{j}_{ri}", [128, NCH], F32) for ri in range(2)] for j in range(2)]
        HN = NCH // 2
        i = 0
        for t in range(4):
            Sst = Sst2[t % 2]
            sp_ = f"b{t % 2}"
            for q in range(4):
                pr = 4 * t + q
                if q < 3:
                    rows = slice(32 * q, 32 * q + 32)
                    bcol = (t * 2 + 0) * 128
                else:
                    rows = slice(64, 128)
                    bcol = (t * 2 + 1) * 128
                for ri in range(2):
                    for half in range(2):
                        bank = ps_next()
                        ps = c.PS[bank]
                        for rho in range(16):
                            mm(c, ps[:, 0:HN], Bm[ri][rows, rho, bcol:bcol + 128], Uall[rows, t, rho, half * HN:(half + 1) * HN],
                               rho == 0, rho == 15, [f"Bm{ri}", "Uall"], [f"ps{bank}"])
                        cp(c, "act", Sst[ri][:, q, half * HN:(half + 1) * HN], ps[:, 0:HN],
                           [f"ps{bank}"], [f"S{sp_}{ri}_{q}"])
                        i += 1
            for q in range(4):
                pr = 4 * t + q
                jb = pr % 2
                Sk = [f"S{sp_}0_{q}", f"S{sp_}1_{q}"]
                L3 = [Sst[ri][:, q, :].rearrange("p (b w) -> p b w", w=8) for ri in range(2)]
                a16r, a16i, a16n = ksr[:, 0, pr:pr + 1], ksi[:, 0, pr:pr + 1], ksni[:, 0, pr:pr + 1]
                for cc in range(1, 8):
                    stt(c, L3[0][:, :, cc], L3[0][:, :, cc - 1], a16r, L3[0][:, :, cc], ALU.mult, ALU.add, [Sk[0], "ksr"], [Sk[0]])
                    stt(c, L3[1][:, :, cc], L3[1][:, :, cc - 1], a16r, L3[1][:, :, cc], ALU.mult, ALU.add, [Sk[1], "ksr"], [Sk[1]])
                    stt(c, L3[0][:, :, cc], L3[1][:, :, cc - 1], a16n, L3[0][:, :, cc], ALU.mult, ALU.add, [Sk[0], Sk[1], "ksni"], [Sk[0]])
                    stt(c, L3[1][:, :, cc], L3[0][:, :, cc - 1], a16i, L3[1][:, :, cc], ALU.mult, ALU.add, [Sk[0], Sk[1], "ksi"], [Sk[1]])
                NBK = NCH // 8
                ea = [kb[jb][0][:, 0:NBK], kb[jb][1][:, 0:NBK]]
                eb = [kb[jb][0][:, 128:128 + NBK], kb[jb][1][:, 128:128 + NBK]]
                ka = [f"kb{jb}_0a", f"kb{jb}_1a"]
                kbk = [f"kb{jb}_0b", f"kb{jb}_1b"]
                for ri in range(2):
                    cp(c, "dve", ea[ri], L3[ri][:, :, 7], [Sk[ri]], [ka[ri]])
                cur, oth, ck, ok = ea, eb, ka, kbk
                for k in range(7):
                    sft = 1 << k
                    n = NBK - sft
                    Ar = ksr[:, k + 3, pr:pr + 1]
                    Ai = ksi[:, k + 3, pr:pr + 1]
                    nAi = ksni[:, k + 3, pr:pr + 1]
                    stt(c, oth[0][:, sft:], cur[0][:, 0:n], Ar, cur[0][:, sft:], ALU.mult, ALU.add, [ck[0], "ksr"], [ok[0]])
                    stt(c, oth[1][:, sft:], cur[1][:, 0:n], Ar, cur[1][:, sft:], ALU.mult, ALU.add, [ck[1], "ksr"], [ok[1]])
                    stt(c, oth[0][:, sft:], cur[1][:, 0:n], nAi, oth[0][:, sft:], ALU.mult, ALU.add, [ck[1], "ksni", ok[0]], [ok[0]])
                    stt(c, oth[1][:, sft:], cur[0][:, 0:n], Ai, oth[1][:, sft:], ALU.mult, ALU.add, [ck[0], "ksi", ok[1]], [ok[1]])
                    cp(c, "act", oth[0][:, 0:sft], cur[0][:, 0:sft], [ck[0]], [ok[0]])
                    cp(c, "act", oth[1][:, 0:sft], cur[1][:, 0:sft], [ck[1]], [ok[1]])
                    cur, oth = oth, cur
                    ck, ok = ok, ck
                Xp = [cur[ri].rearrange("p (i w) -> p i w", w=2)[:, 0:32, 0] for ri in range(2)]
                Lo = [Sst[ri][:, q, :].rearrange("p (i w c) -> p i w c", w=2, c=8)[:, 0:32, 1, :] for ri in range(2)]
                Xo = [Xown[ri][:, pr, :].rearrange("p (i c) -> p i c", c=8) for ri in range(2)]
                tmp = [kb[jb][0][:, 256:288], kb[jb][1][:, 256:288]]
                tk_ = [f"kb{jb}_0t", f"kb{jb}_1t"]
                for ri in range(2):
                    cp(c, "act", Xo[ri][:, :, 0], Xp[ri], [ck[ri]], [f"Xown{ri}"])
                for cc in range(1, 8):
                    pr_, pi_, pn_ = pwr[:, cc, pr:pr + 1], pwi[:, cc, pr:pr + 1], pwn[:, cc, pr:pr + 1]
                    tb_ = (cc % 2) * 32
                    t0_, t1_ = kb[jb][0][:, 256 + tb_:288 + tb_], kb[jb][1][:, 256 + tb_:288 + tb_]
                    stt(c, t0_, Xp[0], pr_, Lo[0][:, :, cc - 1], ALU.mult, ALU.add, [ck[0], Sk[0], "pwr"], [tk_[0]])
                    stt(c, t1_, Xp[1], pr_, Lo[1][:, :, cc - 1], ALU.mult, ALU.add, [ck[1], Sk[1], "pwr"], [tk_[1]])
                    stt(c, Xo[0][:, :, cc], Xp[1], pn_, t0_, ALU.mult, ALU.add, [ck[1], tk_[0], "pwn"], ["Xown0"])
                    stt(c, Xo[1][:, :, cc], Xp[0], pi_, t1_, ALU.mult, ALU.add, [ck[0], tk_[1], "pwi"], ["Xown1"])
        p.barrier()
        A1c.close()
        ABm.close()
        A1.close()

        A5b = Alloc(nc)
        Cm = [A5b.sb(f"Cm{ri}", [128, 17, 512], BF16) for ri in range(2)]
        Kt = A5b.sb("Kt", [128, 4, 16, 128], BF16)
        CmZ = [A5b.sb(f"CmZ{ri}", [128, 17, 4, 64], BF16) for ri in range(2)]
        sp_gen(False, Cm, Kt, CmZ)

        A2 = Alloc(nc)
        Uown = A2.sb("Uown", [128, 4, TO], BF16)
        A2b = Alloc(nc)
        set_wst(A2b, 2)
        Wsq = load_weight(c, A2b, "Wsq", w_in, D, [(0, 1024)], scale_cols=G_MIX)
        nb = alloc_norm_bufs(A2b, "b", SL)
        xs = [A2b.sb(f"xo{i}", [128, KT, SL], F32) for i in range(2)]
        qst = [A2b.sb(f"qst{i}", [128, 4, SL], BF16) for i in range(2)]
        xTo_v = xTo.rearrange("(k p) t -> p k t", p=128)
        qT_v = qT_scr.rearrange("(k p) t -> p k t", p=128)
        nsl = TO // SL
        uko = {}

        def ld_o(s):
            sl_ = s % 2
            dma(c, xs[sl_][:], xTo_v[:, :, s * SL:(s + 1) * SL], [], [f"xo{sl_}"], f"xs{sl_}")

        def nm_o(s):
            uko[s] = norm_slab(c, xs[s % 2], f"xo{s % 2}", SL, nb, s % 2)

        def pj_o(s, cts=range(8), store=True):
            slot = s % 2
            uT = nb["uT"][slot]
            for ct in cts:
                bank = ps_next()
                ps = c.PS[bank]
                for k in range(KT):
                    mm(c, ps[:, 0:SL], Wsq[:, k, ct * 128:(ct + 1) * 128], uT[:, k, :], k == 0, k == KT - 1,
                       ["Wsq", uko[s]], [f"ps{bank}"])
                eng = "act" if ct % 2 == 0 else "dve"
                if ct < 4:
                    cp(c, eng, Uown[:, ct, s * SL:(s + 1) * SL], ps[:, 0:SL], [f"ps{bank}"], ["Uown"])
                else:
                    cp(c, eng, qst[slot][:, ct - 4, :], ps[:, 0:SL], [f"ps{bank}"], [f"qst{slot}"])
            if store:
                dma(c, qT_v[:, :, s * SL:(s + 1) * SL], qst[slot][:], [f"qst{slot}"], ["qT_scr"], f"qo{slot}")
        slab_pipeline(nsl, ld_o, nm_o, lambda s: pj_o(s, range(0, 4), False), lambda s: pj_o(s, range(4, 8), True))
        p.barrier()
        A2b.close()

        A2c = Alloc(nc)
        set_wst(A2c, 2)
        Wglu = load_weight(c, A2c, "Wglu", w_glu, 512, [(0, 512)])
        SP_ = 512
        NCS = SP_ // 16
        yf = [A2c.sb(f"yf{i}", [128, SP_], F32) for i in range(2)]
        wk = [A2c.sb(f"wk{i}", [128, SP_], F32) for i in range(2)]
        sg = [A2c.sb(f"sg{i}", [128, SP_], F32) for i in range(2)]
        gf = [A2c.sb(f"gf{i}", [128, 4, SP_], F32) for i in range(2)]
        gb = [A2c.sb(f"gb{i}", [128, 4, SP_], BF16) for i in range(2)]
        sz = [A2c.sb(f"sz{i}", [128, SP_], F32) for i in range(2)]
        yst = [A2c.sb(f"yst{i}", [128, 4, SP_], BF16) for i in range(2)]
        ysT_v = ysT_scr.rearrange("(k p) t -> p k t", p=128)
        it = 0
        for s in range(TO // SP_):
            slot = s % 2
            cs_ = slice(s * SP_, (s + 1) * SP_)
            for t in range(4):
                b2 = it % 2
                it += 1
                bank = ps_next()
                ps = c.PS[bank]
                Y3 = _v3(ps[:, 0:SP_], 16)
                U3 = _v3(Uown[:, t, cs_], 16)
                for tau in range(16):
                    mm(c, Y3[:, :, tau:16], Kt[:, t, tau, :], U3[:, :, 0:16 - tau], tau == 0, False, ["Kt", "Uown"], [f"ps{bank}"])
                for q in range(4):
                    pr = 4 * t + q
                    cols = slice(pr * 32, (pr + 1) * 32)
                    if q < 3:
                        Yq = _v3(ps[32 * q:32 * q + 32, 0:SP_], 16)
                    else:
                        Yq = _v3(ps[64:128, 0:SP_], 16)
                    for j in range(16):
                        for ri in range(2):
                            last = (q == 3 and j == 15 and ri == 1)
                            lt = Cm[ri][:, j + 1, cols] if q < 3 else CmZ[ri][:, j + 1, t, :]
                            mm(c, Yq[:, :, j], lt, Xown[ri][:, pr, NCS * s:NCS * (s + 1)], False, last,
                               [f"Cm{ri}", f"CmZ{ri}", f"Xown{ri}"], [f"ps{bank}"])
                stt(c, yf[b2][:], Uown[:, t, cs_], c.vecs[:, D_SKIP + t:D_SKIP + t + 1], ps[:, 0:SP_],
                    ALU.mult, ALU.add, ["Uown", "vecs", f"ps{bank}"], [f"yf{b2}"])
                act(c, wk[b2][:], yf[b2][:], AF.Square, [f"yf{b2}"], [f"wk{b2}"])
                ts(c, wk[b2][:], wk[b2][:], 0.044715, ALU.mult, [f"wk{b2}"], [f"wk{b2}"], s2=1.0, op1=ALU.add)
                tt(c, wk[b2][:], wk[b2][:], yf[b2][:], ALU.mult, [f"wk{b2}", f"yf{b2}"], [f"wk{b2}"])
                act(c, sg[b2][:], wk[b2][:], AF.Sigmoid, [f"wk{b2}"], [f"sg{b2}"], scale=2.0 * GELU_C)
                tt(c, gf[slot][:, t, :], yf[b2][:], sg[b2][:], ALU.mult, [f"yf{b2}", f"sg{b2}"], [f"gf{slot}"])
                cp(c, "act", gb[slot][:, t, :], gf[slot][:, t, :], [f"gf{slot}"], [f"gb{slot}"])
            for t2 in range(4):
                b2 = it % 2
                it += 1
                bank = ps_next()
                ps = c.PS[bank]
                for t in range(4):
                    mm(c, ps[:, 0:SP_], Wglu[:, t, t2 * 128:(t2 + 1) * 128], gb[slot][:, t, :], t == 0, t == 3,
                       ["Wglu", f"gb{slot}"], [f"ps{bank}"])
                act(c, sz[b2][:], ps[:, 0:SP_], AF.Sigmoid, [f"ps{bank}", "vecs"], [f"sz{b2}"],
                    bias=c.vecs[:, B_GLU + t2:B_GLU + t2 + 1])
                tt(c, yst[slot][:, t2, :], gf[slot][:, t2, :], sz[b2][:], ALU.mult, [f"gf{slot}", f"sz{b2}"], [f"yst{slot}"])
            dma(c, ysT_v[:, :, cs_], yst[slot][:], [f"yst{slot}"], ["ysT_scr"], f"yo{slot}")
        p.barrier()
        A2c.close()
        A2.close()
        A5b.close()
        A5.close()

        A3 = Alloc(nc)
        set_wst(A3, 2)
        Wkv = load_weight(c, A3, "Wkv", w_in, D, [(1024, 2048)], scale_cols=G_MIX)
        def slabs(total, W):
            return [(c0, min(W, total - c0)) for c0 in range(0, total, W)]

        SLk = 512
        nb = alloc_norm_bufs(A3, "c", SLk)
        xs = [A3.sb(f"xr{i}", [128, KT, SLk], F32) for i in range(2)]
        kst = [A3.sb(f"kst{i}", [128, 4, SLk], BF16) for i in range(2)]
        vst = [A3.sb(f"vst{i}", [128, 4, 512], BF16) for i in range(2)]
        xTr_v = xTr.rearrange("(k p) t -> p k t", p=128)
        kT_v = kT_scr.rearrange("(k p) t -> p k t", p=128)
        v_v = v_scr.rearrange("hp p blk c -> p hp blk c")
        sls = slabs(TA, SLk)
        ukk = {}

        def ld_k(s):
            c0, W = sls[s]
            sl_ = s % 2
            dma(c, xs[sl_][:, :, 0:W], xTr_v[:, :, c0:c0 + W], [], [f"xr{sl_}"], f"xs{sl_}")

        def nm_k(s):
            ukk[s] = norm_slab(c, xs[s % 2], f"xr{s % 2}", sls[s][1], nb, s % 2)

        def pj_k(s, part):
            c0, W = sls[s]
            slot = s % 2
            uk = ukk[s]
            uT = nb["uT"][slot]
            if part == 1:
                return pj_kv(s, c0, W, slot, uk, uT)
            for ct in range(4):
                bank = ps_next()
                ps = c.PS[bank]
                for k in range(KT):
                    mm(c, ps[:, 0:W], Wkv[:, k, ct * 128:(ct + 1) * 128], uT[:, k, 0:W], k == 0, k == KT - 1,
                       ["Wkv", uk], [f"ps{bank}"])
                cp(c, "act" if ct % 2 == 0 else "dve", kst[slot][:, ct, 0:W], ps[:, 0:W], [f"ps{bank}"], [f"kst{slot}"])
            dma(c, kT_v[:, :, c0:c0 + W], kst[slot][:, :, 0:W], [f"kst{slot}"], ["kT_scr"], f"ko{slot}")

        def pj_kv(s, c0, W, slot, uk, uT):
            ntb = W // 128
            for tb in range(ntb):
                bank = ps_next()
                ps = c.PS[bank]
                for k in range(KT):
                    mm(c, ps[:, :], uT[:, k, tb * 128:(tb + 1) * 128], Wkv[:, k, 512:1024], k == 0, k == KT - 1,
                       ["Wkv", uk], [f"ps{bank}"])
                cp(c, "act" if tb % 2 == 1 else "dve", vst[slot][:, tb, :], ps[:, :], [f"ps{bank}"], [f"vst{slot}"])
            blk0 = c0 // 128
            for hp in range(4):
                dma(c, v_scr[hp, :, blk0:blk0 + ntb, :], vst[slot][:, 0:ntb, hp * 128:(hp + 1) * 128],
                    [f"vst{slot}"], ["v_scr"], f"vo{slot}")
        slab_pipeline(len(sls), ld_k, nm_k, lambda s: pj_k(s, 0), lambda s: pj_k(s, 1))
        p.barrier()
        A3.close()

        A4 = Alloc(nc)
        ya = A4.sb("ya", [128, 32, 512], BF16)
        Khp = [A4.sb(f"Khp{i}", [128, TA], BF16) for i in range(2)]
        Vhp = [A4.sb(f"Vhp{i}", [128, 66, 128], BF16) for i in range(2)]
        Qhp = [A4.sb(f"Qhp{i}", [128, TO], BF16) for i in range(2)]
        NS = 3
        fb = [A4.sb(f"fb{i}", [128, 513], F32) for i in range(NS)]
        Pb = [A4.sb(f"Pb{i}", [128, 513], F32) for i in range(NS)]
        Ab = [A4.sb(f"Ab{i}", [128, 512], BF16) for i in range(NS)]
        ATb = [A4.sb(f"ATb{i}", [128, 512], BF16) for i in range(NS)]
        zer = A4.sb("zer", [128, 513], F32)
        maskneg = A4.sb("maskneg", [128, 128], BF16)
        act(c, maskneg[:], c.mask[:], AF.Copy, ["mask"], ["maskneg"], scale=-30000.0)
        memset(c, zer[:], 0.0, ["zer"])
        for i in range(NS):
            memset(c, fb[i][:, 0:1], 1.0, [f"fb{i}"])

        def load_hp(hp):
            sl = hp % 2
            for j in range(4):
                a, b = j * (TA // 4), (j + 1) * (TA // 4)
                dma(c, Khp[sl][:, a:b], kT_scr[hp * 128:(hp + 1) * 128, a:b], ["kT_scr"], [f"Khp{sl}"], f"khp{sl}")
            for j in range(2):
                dma(c, Vhp[sl][:, j * 33:(j + 1) * 33, :], v_scr[hp, :, j * 33:(j + 1) * 33, :], ["v_scr"], [f"Vhp{sl}"], f"vhp{sl}")
            dma(c, Qhp[sl][:], qT_scr[hp * 128:(hp + 1) * 128, :], ["qT_scr"], [f"Qhp{sl}"], f"qhp{sl}")

        tasks = []
        for hp in range(4):
            for i in range(32):
                r0 = 8064 - 256 * i
                chunks = []
                r = r0
                while r < 8320:
                    w = min(512, 8320 - r)
                    chunks.append((r, w))
                    r += w
                for hh in range(2):
                    for ci, (r, w) in enumerate(chunks):
                        tasks.append(dict(hp=hp, i=i, hh=hh, ci=ci, r=r, w=w, last=(ci == len(chunks) - 1)))
        NT = len(tasks)
        yo_i = [0]

        def stageA1(n):
            tk = tasks[n]
            sl = tk["hp"] % 2
            s3 = n % NS
            hs = slice(64 * tk["hh"], 64 * tk["hh"] + 64)
            w = tk["w"]
            zb = n % 3
            z = c.PS[zb]
            mm(c, z[:, 0:w], Qhp[sl][hs, tk["i"] * 128:(tk["i"] + 1) * 128], Khp[sl][hs, tk["r"]:tk["r"] + w], True, True,
               [f"Qhp{sl}", f"Khp{sl}"], [f"ps{zb}"])
            if tk["ci"] == 0:
                mm(c, z[:, 0:128], c.ident[:], maskneg[:], False, True, ["ident", "maskneg", f"ps{zb}"], [f"ps{zb}"])
            act(c, fb[s3][:, 1:1 + w], z[:, 0:w], AF.Sigmoid, [f"ps{zb}"], [f"fb{s3}"], scale=-0.125)

        def stageA2(n):
            tk = tasks[n]
            s3 = n % NS
            w = tk["w"]
            if tk["ci"] == 0:
                init = 1.0
                rd = [f"fb{s3}", "zer"]
            else:
                pw = tasks[n - 1]["w"]
                sp_ = (n - 1) % NS
                init = Pb[sp_][:, pw:pw + 1]
                rd = [f"fb{s3}", "zer", f"Pb{sp_}"]
            c.p.op("dve", lambda e: e.tensor_tensor_scan(out=Pb[s3][:, 0:w + 1], data0=fb[s3][:, 0:w + 1], data1=zer[:, 0:w + 1],
                                                         initial=init, op0=ALU.mult, op1=ALU.add),
                   reads=rd, writes=[f"Pb{s3}"])
            tt(c, Ab[s3][:, 0:w], Pb[s3][:, 0:w], Pb[s3][:, 1:w + 1], ALU.subtract, [f"Pb{s3}"], [f"Ab{s3}"])

        def stageB(n):
            tk = tasks[n]
            s3 = n % NS
            w = tk["w"]
            pb = n % 2
            for c4 in range(w // 128):
                c.p.op("pe", lambda e, c4=c4: e.transpose(out=c.PB[pb][:, c4 * 128:(c4 + 1) * 128],
                                                          in_=Ab[s3][:, c4 * 128:(c4 + 1) * 128], identity=c.ident[:]),
                       reads=[f"Ab{s3}", "ident"], writes=[f"pb{pb}"])
            act(c, ATb[s3][:, 0:w], c.PB[pb][:, 0:w], AF.Copy, [f"pb{pb}"], [f"ATb{s3}"])

        def stageC(n):
            tk = tasks[n]
            sl = tk["hp"] % 2
            s3 = n % NS
            w = tk["w"]
            if tk["ci"] == 0:
                yo_i[0] = 3 + (yo_i[0] + 1) % 3
            yb = yo_i[0]
            yo = c.PS[yb]
            nb4 = w // 128
            for c4 in range(nb4):
                blk = tk["r"] // 128 + c4
                mm(c, yo[:, 0:64], ATb[s3][:, c4 * 128:(c4 + 1) * 128], Vhp[sl][:, blk, 64 * tk["hh"]:64 * tk["hh"] + 64],
                   tk["ci"] == 0 and c4 == 0, tk["last"] and c4 == nb4 - 1, [f"ATb{s3}", f"Vhp{sl}"], [f"ps{yb}"])
            if tk["last"]:
                head = 2 * tk["hp"] + tk["hh"]
                cp(c, "act", ya[:, tk["i"], head * 64:(head + 1) * 64], yo[:, 0:64], [f"ps{yb}"], [f"ya{tk['i']}"])
                if head == 7:
                    emit_ya(tk["i"])

        yat = [A4.sb(f"yat{i}", [128, 4, 128], BF16) for i in range(2)]
        yaT_v = yaT_scr.rearrange("(k p) t -> p k t", p=128)

        def emit_ya(i):
            pb = i % 2
            for t in range(4):
                c.p.op("pe", lambda e, t=t, i=i, pb=pb: e.transpose(out=c.PB[pb][:, t * 128:(t + 1) * 128],
                                                                     in_=ya[:, i, t * 128:(t + 1) * 128], identity=c.ident[:]),
                       reads=[f"ya{i}", "ident"], writes=[f"pb{pb}"])
            cp(c, "act", yat[pb][:].rearrange("p a b -> p (a b)"), c.PB[pb][:, 0:512], [f"pb{pb}"], [f"yat{pb}"])
            dma(c, yaT_v[:, :, i * 128:(i + 1) * 128], yat[pb][:], [f"yat{pb}"], ["yaT_scr"], f"yao{pb}")

        load_hp(0)
        stageA1(0)
        for n in range(NT + 2):
            if n + 1 < NT and tasks[n + 1]["hp"] == tasks[min(n, NT - 1)]["hp"]:
                stageA1(n + 1)
            if n < NT:
                if tasks[n]["hp"] != tasks[n - 1]["hp"] and n > 0:
                    stageA1(n)
                stageA2(n)
            if 0 <= n - 1 < NT:
                stageB(n - 1)
            if 0 <= n - 2 < NT:
                stageC(n - 2)
                tk = tasks[n - 2]
                if tk["i"] == 0 and tk["hh"] == 0 and tk["ci"] == 0 and tk["hp"] + 1 < 4:
                    load_hp(tk["hp"] + 1)
        p.barrier()
        A4.close()

        A6 = Alloc(nc)
        set_wst(A6, 2)
        Wg = load_weight(c, A6, "Wg", w_gate, D, [(0, 2048)], scale_cols=G_MIX)
        Wus = load_weight(c, A6, "Wus", w_up_ssm, 512, [(0, D)])
        Wua = load_weight(c, A6, "Wua", w_up_attn, 512, [(0, D)])
        Wo = load_weight(c, A6, "Wo", w_out, D, [(0, D)])
        SB = 512
        nb = alloc_norm_bufs(A6, "d", SB, nslot=1)
        xs = [A6.sb(f"xb{i}", [128, KT, SB], F32) for i in range(2)]
        ysl = [A6.sb(f"ysl{i}", [128, 4, SB], BF16) for i in range(2)]
        yal = [A6.sb(f"yal{i}", [128, 4, SB], BF16) for i in range(2)]
        sgs = [A6.sb(f"sgs{i}", [128, 2, SB], F32) for i in range(2)]
        m1 = [A6.sb(f"m1{i}", [128, SB], F32) for i in range(2)]
        m2 = [A6.sb(f"m2{i}", [128, SB], F32) for i in range(2)]
        mg = A6.sb("mg", [128, KT, SB], BF16)
        hst = A6.sb("hst", [128, KT, SB], F32)
        hT_v = hT_scr.rearrange("(k p) t -> p k t", p=128)
        nsl = TO // SB

        def ld3b(s):
            sl = s % 2
            dma(c, xs[sl][:], xTo_v[:, :, s * SB:(s + 1) * SB], [], [f"xb{sl}"], f"xs{sl}")
            dma(c, ysl[sl][:], ysT_v[:, :, s * SB:(s + 1) * SB], ["ysT_scr"], [f"ysl{sl}"], f"ysl{sl}")
            dma(c, yal[sl][:], yaT_v[:, :, s * SB:(s + 1) * SB], ["yaT_scr"], [f"yal{sl}"], f"yal{sl}")
        ld3b(0)
        it = 0
        for s in range(nsl):
            slot = s % 2
            if s + 1 < nsl:
                ld3b(s + 1)
            uk = norm_slab(c, xs[slot], f"xb{slot}", SB, nb, 0)
            uT = nb["uT"][0]
            for j in range(8):
                b2 = it % 2
                it += 1
                banks = [ps_next() for _ in range(4)]
                pgs, pga, pus, pua = (c.PS[b_] for b_ in banks)
                for k in range(KT):
                    mm(c, pgs[:, :], Wg[:, k, j * 128:(j + 1) * 128], uT[:, k, :], k == 0, k == KT - 1, ["Wg", uk], [f"ps{banks[0]}"])
                for k in range(KT):
                    mm(c, pga[:, :], Wg[:, k, 1024 + j * 128:1024 + (j + 1) * 128], uT[:, k, :], k == 0, k == KT - 1,
                       ["Wg", uk], [f"ps{banks[1]}"])
                for t in range(4):
                    mm(c, pus[:, :], Wus[:, t, j * 128:(j + 1) * 128], ysl[slot][:, t, :], t == 0, t == 3,
                       ["Wus", f"ysl{slot}"], [f"ps{banks[2]}"])
                for t in range(4):
                    mm(c, pua[:, :], Wua[:, t, j * 128:(j + 1) * 128], yal[slot][:, t, :], t == 0, t == 3,
                       ["Wua", f"yal{slot}"], [f"ps{banks[3]}"])
                act(c, sgs[b2][:, 0, :], pgs[:, :], AF.Sigmoid, [f"ps{banks[0]}", "vecs"], [f"sgs{b2}a"],
                    bias=c.vecs[:, B_GATE + j:B_GATE + j + 1])
                act(c, sgs[b2][:, 1, :], pga[:, :], AF.Sigmoid, [f"ps{banks[1]}", "vecs"], [f"sgs{b2}b"],
                    bias=c.vecs[:, B_GATE + 8 + j:B_GATE + 8 + j + 1])
                tt(c, m1[b2][:], sgs[b2][:, 0, :], pus[:, :], ALU.mult, [f"sgs{b2}a", f"ps{banks[2]}"], [f"m1{b2}"])
                tt(c, m2[b2][:], sgs[b2][:, 1, :], pua[:, :], ALU.mult, [f"sgs{b2}b", f"ps{banks[3]}"], [f"m2{b2}"])
                tt(c, mg[:, j, :], m1[b2][:], m2[b2][:], ALU.add, [f"m1{b2}", f"m2{b2}"], [f"mg{j}"])
            for j in range(8):
                bo = ps_next()
                po = c.PS[bo]
                for k in range(KT):
                    mm(c, po[:, :], Wo[:, k, j * 128:(j + 1) * 128], mg[:, k, :], k == 0, k == KT - 1,
                       ["Wo", f"mg{k}"], [f"ps{bo}"])
                tt(c, hst[:, j, :], xs[slot][:, j, :], po[:, :], ALU.add, [f"xb{slot}", f"ps{bo}"], ["hst"])
            dma(c, hT_v[:, :, s * SB:(s + 1) * SB], hst[:], ["hst"], ["hT_scr"], "ho0")
        p.barrier()
        A6.close()

        A7 = Alloc(nc)
        W1 = A7.sb("W1", [128, 8, 4096], BF16)
        W2 = A7.sb("W2", [128, 32, 1024], BF16)
        A7s = Alloc(nc)
        set_wst(A7s, 2)
        load_weight(c, None, "W1", w_ff1, D, [(0, 4096)], scale_cols=G_MLP, wsb=W1)
        load_weight(c, None, "W2", w_ff2, 4096, [(0, D)], wsb=W2)
        p.barrier()
        A7s.close()
        FW = 512
        nb = alloc_norm_bufs(A7, "e", FW, nslot=1)
        hsb = [A7.sb(f"hs{i}", [128, KT, FW], F32) for i in range(2)]
        hid = A7.sb("hid", [128, 16, FW], BF16)
        sq4 = [A7.sb(f"sq4{i}", [128, 512], F32) for i in range(2)]
        yT_v = yT.rearrange("(k p) t -> p k t", p=128)
        nsl = TO // FW
        it = 0

        def ld_h(s):
            dma(c, hsb[s % 2][:], hT_v[:, :, s * FW:(s + 1) * FW], ["hT_scr"], [f"hs{s % 2}"], f"xs{s % 2}")
        ld_h(0)
        for s in range(nsl):
            hs_ = hsb[s % 2]
            hk = f"hs{s % 2}"
            if s + 1 < nsl:
                ld_h(s + 1)
            uk = norm_slab(c, hs_, hk, FW, nb, 0)
            hn = nb["uT"][0]
            for half in range(2):
                for b in range(16):
                    jj = half * 16 + b
                    b2 = it % 2
                    it += 1
                    bank = ps_next()
                    ps = c.PS[bank]
                    for k in range(KT):
                        mm(c, ps[:, :], W1[:, k, jj * 128:(jj + 1) * 128], hn[:, k, :], k == 0, k == KT - 1,
                           ["W1", uk], [f"ps{bank}"])
                    act(c, sq4[b2][:], ps[:, :], AF.Square, [f"ps{bank}"], [f"sq4{b2}"])
                    stt(c, hid[:, b, :], ps[:, :], 0.0, sq4[b2][:], ALU.is_gt, ALU.mult, [f"ps{bank}", f"sq4{b2}"], [f"hid{b}"])
                for j in range(8):
                    bank = ps_next()
                    ps = c.PS[bank]
                    for kk in range(16):
                        mm(c, ps[:, :], W2[:, half * 16 + kk, j * 128:(j + 1) * 128], hid[:, kk, :], kk == 0, kk == 15,
                           ["W2", f"hid{kk}"], [f"ps{bank}"])
                    tt(c, hs_[:, j, :], hs_[:, j, :], ps[:, :], ALU.add, [f"{hk}_{j}", hk, f"ps{bank}"], [f"{hk}_{j}"])
            sq = nb["sq"][0]
            rt = nb["rt"][0]
            rs = nb["rs"][0]
            hkeys = [f"{hk}_{j}" for j in range(8)]
            act(c, sq[:], hs_[:], AF.Square, hkeys + [hk, uk], ["esq0"])
            bank = ps_next()
            ps = c.PS[bank]
            for k in range(KT):
                mm(c, ps[:, 0:FW], c.ones[:], sq[:, k, :], k == 0, k == KT - 1, ["esq0", "ones"], [f"ps{bank}"])
            act(c, rt[:], ps[:, 0:FW], AF.Sqrt, [f"ps{bank}"], ["ert0"], scale=1.0 / D, bias=c.epsb[:, 0:1])
            c.p.op("dve", lambda e, rs=rs, rt=rt: e.reciprocal(out=rs[:], in_=rt[:]), reads=["ert0"], writes=["ers0"])
            for j in range(8):
                stt(c, hs_[:, j, :], hs_[:, j, :], c.vecs[:, G_FIN + j:G_FIN + j + 1], rs[:], ALU.mult, ALU.mult,
                    [f"{hk}_{j}", "ers0", "vecs"], [f"{hk}_{j}"])
            dma(c, yT_v[:, :, s * FW:(s + 1) * FW], hs_[:], hkeys + [hk], ["yT"], f"out{s % 2}")
        p.barrier()
        A7.close()
        G.close()
    return nc


def _s5_tables(A_re, A_im, log_dt, B_re, B_im, C_re, C_im):
    f = np.float32
    sp_par = np.zeros((128, 3, 16, 32), f)
    sp_C = np.zeros((128, 2, 16, 32), f)
    sp_B = np.zeros((128, 2, 16, 32), f)
    for gi in range(2):
        rows = slice(64 * gi, 64 * gi + 64)
        for pr in range(16):
            g = 2 * pr + gi
            sp_par[rows, 0, pr, :] = A_re[g][:, None]
            sp_par[rows, 1, pr, :] = A_im[g][:, None]
            sp_par[rows, 2, pr, :] = log_dt[g]
            cs = slice(16 * gi, 16 * gi + 16)
            sp_C[rows, 0, pr, cs] = C_re[g].T
            sp_C[rows, 1, pr, cs] = C_im[g].T
            sp_B[rows, 0, pr, cs] = B_re[g]
            sp_B[rows, 1, pr, cs] = B_im[g]
    ch_par = np.zeros((128, 3, 4, 2, 128), f)
    ch_B = np.zeros((128, 2, 4, 2, 128), f)
    for t in range(4):
        for v in range(2):
            for q in range(4):
                pr = 4 * t + (q if v == 0 else 3)
                rows_q = slice(32 * q, 32 * q + 32)
                for gi2 in range(2):
                    g2 = 2 * pr + gi2
                    cs = slice(64 * gi2, 64 * gi2 + 64)
                    ch_par[rows_q, 0, t, v, cs] = A_re[g2][None, :]
                    ch_par[rows_q, 1, t, v, cs] = A_im[g2][None, :]
                    ch_par[rows_q, 2, t, v, cs] = log_dt[g2]
                    if (v == 0 and q < 3) or (v == 1 and q == 3):
                        rows = slice(32 * q + 16 * gi2, 32 * q + 16 * gi2 + 16)
                        ch_B[rows, 0, t, v, cs] = B_re[g2].T
                        ch_B[rows, 1, t, v, cs] = B_im[g2].T
    return (sp_par.reshape(128, 3, 512), sp_C.reshape(128, 2, 512), sp_B.reshape(128, 2, 512),
            ch_par.reshape(128, 3, 1024), ch_B.reshape(128, 2, 1024))


def make_in_maps(x, norm_mix, w_in, A_re, A_im, log_dt, B_re, B_im, C_re, C_im, D_skip, w_glu, b_glu,
                 w_up_ssm, w_up_attn, w_gate, b_gate, w_out, norm_mlp, w_ff1, w_ff2, norm_final, cores=range(8)):
    f = np.float32
    x = np.asarray(x, f)
    tile = lambda v: np.asarray(v, f).reshape(-1, 128).T
    vecs = np.concatenate([tile(norm_mix[0]), tile(norm_mlp[0]), tile(norm_final), tile(b_gate[0]),
                           tile(D_skip[0]), tile(b_glu[0])], axis=1)
    vecs = np.ascontiguousarray(vecs, f)
    assert vecs.shape == (128, 48)
    sp_par, sp_C, sp_B, ch_par, ch_B = _s5_tables(np.asarray(A_re[0], f), np.asarray(A_im[0], f), np.asarray(log_dt[0], f),
                                                  np.asarray(B_re[0], f), np.asarray(B_im[0], f),
                                                  np.asarray(C_re[0], f), np.asarray(C_im[0], f))
    ql = np.arange(128)
    mask = (ql[None, :] + ql[:, None] <= 127).astype(f)
    shared = dict(w_in=np.ascontiguousarray(w_in[0], f), w_gate=np.ascontiguousarray(w_gate[0], f),
                  w_glu=np.ascontiguousarray(w_glu[0], f), w_up_ssm=np.ascontiguousarray(w_up_ssm[0], f),
                  w_up_attn=np.ascontiguousarray(w_up_attn[0], f), w_out=np.ascontiguousarray(w_out[0], f),
                  w_ff1=np.ascontiguousarray(w_ff1[0], f), w_ff2=np.ascontiguousarray(w_ff2[0], f),
                  vecs=vecs, sp_par=sp_par, sp_C=sp_C, sp_B=sp_B, ch_par=ch_par, ch_B=ch_B, mask=mask)
    maps = []
    for core in cores:
        b, h = core // 2, core % 2
        xb = x[b]
        r = np.arange(TA)
        tok = 8191 + 128 * h - r
        val = (tok >= 0) & (tok < S)
        xr = np.zeros((TA, D), f)
        xr[val] = xb[tok[val]]
        tokn = r - 128 * (1 - h)
        valn = (tokn >= 0) & (tokn < S)
        xn = np.zeros((TA, D), f)
        xn[valn] = xb[tokn[valn]]
        own = xb.reshape(32, 2, 128, D)[:, h].reshape(TO, D)
        m = dict(shared)
        m["xTr"] = np.ascontiguousarray(xr.T)
        m["xTn"] = np.ascontiguousarray(xn.T)
        m["xTo"] = np.ascontiguousarray(own.T)
        maps.append(m)
    return maps


def kernel(**inputs):
    nc = build()
    maps = make_in_maps(**inputs)
    res = run_bass_kernel_spmd(nc, maps, core_ids=list(range(8)))
    out = np.zeros((4, S, D), np.float32)
    ov = out.reshape(4, 32, 2, 128, D)
    for core in range(8):
        b, h = core // 2, core % 2
        yT = np.asarray(res.results[core]["yT"], np.float32)
        ov[b, :, h] = yT.T.reshape(32, 128, D)
    return out
```
